# Optimizing a Trainium2 kernel written in Bass

```python
import math, functools
import jax, jax.numpy as jnp
from jax import lax
import numpy as np

D_MODEL = 1024
BATCH = 8
SEQ = 2048
DEPTH = 2
DEC_BATCH = 128
DEC_SEQ = 1
PAST_LEN = 16384
PAGE_SIZE = 128

N_META = 16
M_INNER = D_MODEL
M_HEADDIM = 64
M_HEADS = M_INNER // M_HEADDIM
M_GROUPS = 4
M_HPG = M_HEADS // M_GROUPS
M_STATE = 128
M_CONV = 4
M_CONV_DIM = M_INNER + 2 * M_GROUPS * M_STATE
M_CHUNK = 128
H_KDIM = 128
H_HEADS = D_MODEL // H_KDIM
H_VDIM = 128
H_WIDTH = H_HEADS * H_KDIM
H_VWIDTH = H_HEADS * H_VDIM
H_CHUNK = 32
R_HEADS = 4
R_KDIM = D_MODEL // R_HEADS
R_VDIM = 2 * R_KDIM
R_QK = R_HEADS * R_KDIM
R_V = R_HEADS * R_VDIM
R_CHUNK = 128
ROPE_BASE = 10000.0
D_FF = ((8 * D_MODEL // 3 + 255) // 256) * 256
DN_ALPHA = (2 * DEPTH) ** 0.25
DN_BETA = (8 * DEPTH) ** -0.25
IN_SPLITS = (M_INNER, M_CONV_DIM, M_HEADS, H_WIDTH, H_WIDTH, H_VWIDTH, H_VWIDTH,
             R_QK, R_QK, R_V, R_V, 3 * D_MODEL)
IN_DIM = sum(IN_SPLITS)

kernel_name = "hybrid_ssd_hgrn2_retention_step"


def layer_norm(x, g, b, eps=1e-5):
    xf = x.astype(jnp.float32)
    mu = jnp.mean(xf, axis=-1, keepdims=True)
    var = jnp.mean(jnp.square(xf - mu), axis=-1, keepdims=True)
    return ((xf - mu) * lax.rsqrt(var + eps)).astype(x.dtype) * g + b


def rms_norm(x, eps=1e-6):
    xf = x.astype(jnp.float32)
    return (xf * lax.rsqrt(jnp.mean(jnp.square(xf), axis=-1, keepdims=True) + eps)).astype(x.dtype)


def causal_mask(t):
    return jnp.tril(jnp.ones((t, t), dtype=bool))


def run_chunks(step, state, xs, head_len, chunk):
    bsz, total = xs[0].shape[0], xs[0].shape[1]
    outs = []
    if head_len > 0:
        state, o = step(state, tuple(a[:, :head_len] for a in xs))
        outs.append(o)
    rest = tuple(a[:, head_len:] for a in xs)
    rest_len = total - head_len
    if rest_len <= chunk:
        state, o = step(state, rest)
        outs.append(o)
    else:
        n = rest_len // chunk
        blocks = tuple(jnp.moveaxis(a.reshape((bsz, n, chunk) + a.shape[2:]), 1, 0) for a in rest)
        state, o = lax.scan(step, state, blocks)
        outs.append(jnp.moveaxis(o, 0, 1).reshape((bsz, rest_len) + o.shape[3:]))
    y = outs[0] if len(outs) == 1 else jnp.concatenate(outs, axis=1)
    return state, y


def rotary(x, positions):
    half = x.shape[-1] // 2
    inv_freq = 1.0 / (ROPE_BASE ** jnp.linspace(0.0, 1.0, half, dtype=jnp.float32))
    ang = positions[:, None] * inv_freq[None, :]
    cos = jnp.cos(ang)[None, :, None, :]
    sin = jnp.sin(ang)[None, :, None, :]
    x1 = x[..., :half].astype(jnp.float32)
    x2 = x[..., half:].astype(jnp.float32)
    return jnp.concatenate([x1 * cos - x2 * sin, x2 * cos + x1 * sin], axis=-1).astype(x.dtype)


def ssd_step(A, state, inp):
    x, dt, Bm, Cm = inp
    T = x.shape[1]
    cum = jnp.cumsum(dt * A, axis=1)
    mask = causal_mask(T)[None, :, :, None, None]
    decay = jnp.exp(jnp.where(mask, cum[:, :, None] - cum[:, None, :], -jnp.inf))
    cb = jnp.einsum('btgn,bsgn->btsg', Cm, Bm)
    w = cb[..., None] * decay * dt[:, None]
    y = jnp.einsum('btsgh,bsghp->btghp', w, x)
    y = y + jnp.einsum('btgn,bghpn->btghp', Cm, state) * jnp.exp(cum)[..., None]
    tail = jnp.exp(cum[:, -1:] - cum) * dt
    new_state = state * jnp.exp(cum[:, -1])[..., None, None] + jnp.einsum('bsgh,bsghp,bsgn->bghpn', tail, x, Bm)
    return new_state.astype(state.dtype), y.astype(x.dtype)


def mamba_branch(z, xbc, dt_raw, conv_buf, ssm_state, p, head_len):
    b, L = xbc.shape[0], xbc.shape[1]
    xpad = jnp.concatenate([conv_buf, xbc], axis=1)
    acc = p['conv_b']
    for k in range(M_CONV):
        acc = acc + xpad[:, k:k + L] * p['conv_w'][k]
    new_buf = xpad[:, -(M_CONV - 1):]
    xbc = jax.nn.silu(acc)
    xs = xbc[..., :M_INNER].reshape(b, L, M_GROUPS, M_HPG, M_HEADDIM)
    Bm = xbc[..., M_INNER:M_INNER + M_GROUPS * M_STATE].reshape(b, L, M_GROUPS, M_STATE)
    Cm = xbc[..., M_INNER + M_GROUPS * M_STATE:].reshape(b, L, M_GROUPS, M_STATE)
    dt = jax.nn.softplus((dt_raw + p['dt_bias']).astype(jnp.float32)).reshape(b, L, M_GROUPS, M_HPG)
    A = -jnp.exp(p['a_log'].astype(jnp.float32)).reshape(M_GROUPS, M_HPG)
    state = ssm_state.reshape(b, M_GROUPS, M_HPG, M_HEADDIM, M_STATE)
    state, y = run_chunks(functools.partial(ssd_step, A), state, (xs, dt, Bm, Cm), head_len, M_CHUNK)
    y = y + p['d_skip'].reshape(M_GROUPS, M_HPG)[..., None] * xs
    y = y.reshape(b, L, M_INNER) * jax.nn.silu(z)
    y = rms_norm(y.reshape(b, L, M_GROUPS, M_INNER // M_GROUPS)).reshape(b, L, M_INNER) * p['m_norm_w']
    return y, new_buf, state.reshape(b, M_HEADS, M_HEADDIM, M_STATE)


def hgrn_step(state, inp):
    q, logf, k, v = inp
    T = q.shape[1]
    cum = jnp.cumsum(logf, axis=1)
    mask = causal_mask(T)[None, :, :, None, None]
    decay = jnp.exp(jnp.where(mask, cum[:, :, None] - cum[:, None, :], -jnp.inf))
    scores = jnp.einsum('bthk,bshk,btshk->btsh', q, k, decay)
    o = jnp.einsum('btsh,bshv->bthv', scores, v)
    o = o + jnp.einsum('bthk,bhkv->bthv', q * jnp.exp(cum), state)
    new_state = state * jnp.exp(cum[:, -1])[..., None] + jnp.einsum('bshk,bshv->bhkv', k * jnp.exp(cum[:, -1:] - cum), v)
    return new_state.astype(state.dtype), o.astype(q.dtype)


def hgrn_branch(q, f_raw, i_in, g, state, lb, p, head_len):
    b, L = q.shape[0], q.shape[1]
    q = q.reshape(b, L, H_HEADS, H_KDIM) * (H_KDIM ** -0.5)
    fz = f_raw.astype(jnp.float32).reshape(b, L, H_HEADS, H_KDIM)
    lbh = lb.astype(jnp.float32).reshape(H_HEADS, H_KDIM)
    logf = jnp.logaddexp(jnp.log(lbh), jnp.log1p(-lbh) + jax.nn.log_sigmoid(fz))
    k = ((1.0 - lbh) * jax.nn.sigmoid(-fz)).astype(q.dtype)
    v = i_in.reshape(b, L, H_HEADS, H_VDIM)
    state, o = run_chunks(hgrn_step, state, (q, logf, k, v), head_len, H_CHUNK)
    o = rms_norm(o).reshape(b, L, H_VWIDTH) * p['h_norm_w'] * jax.nn.sigmoid(g)
    return o, state


def ret_step(log_gamma, state, inp):
    q, k, v = inp
    T = q.shape[1]
    t = jnp.arange(T, dtype=jnp.float32)
    diff = (t[:, None] - t[None, :])[..., None] * log_gamma
    decay = jnp.exp(jnp.where(causal_mask(T)[..., None], diff, -jnp.inf))
    scores = jnp.einsum('bthk,bshk->bhts', q, k) * jnp.transpose(decay, (2, 0, 1))[None]
    o = jnp.einsum('bhts,bshv->bthv', scores, v)
    o = o + jnp.einsum('bthk,bhkv->bthv', q, state) * jnp.exp((t + 1.0)[:, None] * log_gamma)[None, :, :, None]
    k_dec = k * jnp.exp((T - 1.0 - t)[:, None] * log_gamma)[None, :, :, None]
    new_state = state * jnp.exp(T * log_gamma)[:, None, None] + jnp.einsum('bshk,bshv->bhkv', k_dec, v)
    return new_state.astype(state.dtype), o.astype(q.dtype)


def retention_branch(q, k, v, g, state, positions, head_len):
    b, L = q.shape[0], q.shape[1]
    q = rotary(q.reshape(b, L, R_HEADS, R_KDIM), positions)
    k = rotary(k.reshape(b, L, R_HEADS, R_KDIM), positions) * (R_KDIM ** -0.5)
    v = v.reshape(b, L, R_HEADS, R_VDIM)
    log_gamma = jnp.log(1.0 - jnp.exp2(-5.0 - jnp.arange(R_HEADS, dtype=jnp.float32)))
    state, o = run_chunks(functools.partial(ret_step, log_gamma), state, (q, k, v), head_len, R_CHUNK)
    o = rms_norm(o).reshape(b, L, R_V) * jax.nn.silu(g)
    return o, state


def trunk_layer(x, conv_buf, ssm_state, hgrn_state, ret_state, positions, head_len, lb, p):
    proj = x @ p['w_in']
    offsets = np.cumsum(IN_SPLITS)[:-1].tolist()
    (m_z, m_xbc, m_dt, h_q, h_f, h_i, h_g, r_q, r_k, r_v, r_g, gates) = jnp.split(proj, offsets, axis=-1)
    y_m, new_conv, new_ssm = mamba_branch(m_z, m_xbc, m_dt, conv_buf, ssm_state, p, head_len)
    y_h, new_hgrn = hgrn_branch(h_q, h_f, h_i, h_g, hgrn_state, lb, p, head_len)
    y_r, new_ret = retention_branch(r_q, r_k, r_v, r_g, ret_state, positions, head_len)
    g_m, g_h, g_r = jnp.split(jax.nn.sigmoid(gates), 3, axis=-1)
    mixed = g_m * (y_m @ p['w_br_m']) + g_h * (y_h @ p['w_br_h']) + g_r * (y_r @ p['w_br_r'])
    x = layer_norm(DN_ALPHA * x + mixed @ p['w_out'], p['ln1_g'], p['ln1_b'])
    hg, hu = jnp.split(x @ p['w_ffn_in'], 2, axis=-1)
    x = layer_norm(DN_ALPHA * x + (jax.nn.silu(hg) * hu) @ p['w_ffn_out'], p['ln2_g'], p['ln2_b'])
    return x, new_conv, new_ssm, new_hgrn, new_ret


def setup_inputs(seed: int = 0) -> dict:
    key = jax.random.key(seed)
    ks = jax.random.split(key, 32)

    def nrm(k, shape, scale=1.0):
        return jax.random.normal(k, shape, jnp.float32) * scale

    dt0 = jnp.exp(jax.random.uniform(ks[10], (DEPTH, M_HEADS), jnp.float32, math.log(1e-3), math.log(1e-1)))
    return {
        "x_prompt": nrm(ks[0], (BATCH, SEQ, D_MODEL)),
        "x_sample": nrm(ks[1], (DEC_BATCH, DEC_SEQ, D_MODEL)),
        "state_ssm": nrm(ks[2], (DEPTH, DEC_BATCH, M_HEADS, M_HEADDIM, M_STATE), 0.3),
        "state_conv": nrm(ks[3], (DEPTH, DEC_BATCH, M_CONV - 1, M_CONV_DIM)),
        "state_hgrn": nrm(ks[4], (DEPTH, DEC_BATCH, H_HEADS, H_KDIM, H_VDIM), 0.3),
        "state_ret": nrm(ks[5], (DEPTH, DEC_BATCH, R_HEADS, R_KDIM, R_VDIM), 0.3),
        "meta_tokens": nrm(ks[6], (N_META, D_MODEL)),
        "ln_in_g": 1.0 + nrm(ks[7], (D_MODEL,), 0.02),
        "ln_in_b": nrm(ks[8], (D_MODEL,), 0.02),
        "w_in": nrm(ks[9], (DEPTH, D_MODEL, IN_DIM), D_MODEL ** -0.5),
        "conv_w": nrm(ks[11], (DEPTH, M_CONV, M_CONV_DIM), M_CONV ** -0.5),
        "conv_b": nrm(ks[12], (DEPTH, M_CONV_DIM), 0.02),
        "dt_bias": dt0 + jnp.log(-jnp.expm1(-dt0)),
        "a_log": jnp.log(jax.random.uniform(ks[13], (DEPTH, M_HEADS), jnp.float32, 1.0, 16.0)),
        "d_skip": 1.0 + nrm(ks[14], (DEPTH, M_HEADS), 0.1),
        "m_norm_w": 1.0 + nrm(ks[15], (DEPTH, M_INNER), 0.02),
        "hgrn_lb_logits": nrm(ks[16], (DEPTH, H_WIDTH)),
        "h_norm_w": 1.0 + nrm(ks[17], (DEPTH, H_VWIDTH), 0.02),
        "w_br_m": nrm(ks[18], (DEPTH, M_INNER, D_MODEL), DN_BETA * M_INNER ** -0.5),
        "w_br_h": nrm(ks[19], (DEPTH, H_VWIDTH, D_MODEL), DN_BETA * H_VWIDTH ** -0.5),
        "w_br_r": nrm(ks[20], (DEPTH, R_V, D_MODEL), DN_BETA * R_V ** -0.5),
        "w_out": nrm(ks[21], (DEPTH, D_MODEL, D_MODEL), DN_BETA * D_MODEL ** -0.5),
        "ln1_g": 1.0 + nrm(ks[22], (DEPTH, D_MODEL), 0.02),
        "ln1_b": nrm(ks[23], (DEPTH, D_MODEL), 0.02),
        "w_ffn_in": nrm(ks[24], (DEPTH, D_MODEL, 2 * D_FF), DN_BETA * D_MODEL ** -0.5),
        "w_ffn_out": nrm(ks[25], (DEPTH, D_FF, D_MODEL), DN_BETA * D_FF ** -0.5),
        "ln2_g": 1.0 + nrm(ks[26], (DEPTH, D_MODEL), 0.02),
        "ln2_b": nrm(ks[27], (DEPTH, D_MODEL), 0.02),
    }


def reference(x_prompt, x_sample, state_ssm, state_conv, state_hgrn, state_ret, meta_tokens,
              ln_in_g, ln_in_b, w_in, conv_w, conv_b, dt_bias, a_log, d_skip, m_norm_w,
              hgrn_lb_logits, h_norm_w, w_br_m, w_br_h, w_br_r, w_out, ln1_g, ln1_b,
              w_ffn_in, w_ffn_out, ln2_g, ln2_b):
    bp, sp = x_prompt.shape[0], x_prompt.shape[1]
    dt_ = x_prompt.dtype
    lb_cum = jnp.cumsum(jax.nn.softmax(hgrn_lb_logits.astype(jnp.float32), axis=0), axis=0)
    lbs = lb_cum - lb_cum[0]

    xp = jnp.concatenate([jnp.broadcast_to(meta_tokens[None].astype(dt_), (bp, N_META, D_MODEL)), x_prompt], axis=1)
    xp = layer_norm(xp, ln_in_g, ln_in_b)
    pos_p = jnp.arange(N_META + sp, dtype=jnp.float32)
    xs = layer_norm(x_sample, ln_in_g, ln_in_b)
    pos_s = PAST_LEN + jnp.arange(x_sample.shape[1], dtype=jnp.float32)

    conv_p, ssm_p, hgrn_p, ret_p = [], [], [], []
    conv_s, ssm_s, hgrn_s, ret_s = [], [], [], []
    for l in range(DEPTH):
        p = dict(w_in=w_in[l], conv_w=conv_w[l], conv_b=conv_b[l], dt_bias=dt_bias[l], a_log=a_log[l],
                 d_skip=d_skip[l], m_norm_w=m_norm_w[l], h_norm_w=h_norm_w[l], w_br_m=w_br_m[l],
                 w_br_h=w_br_h[l], w_br_r=w_br_r[l], w_out=w_out[l], ln1_g=ln1_g[l], ln1_b=ln1_b[l],
                 w_ffn_in=w_ffn_in[l], w_ffn_out=w_ffn_out[l], ln2_g=ln2_g[l], ln2_b=ln2_b[l])
        xp, c, s, h, r = trunk_layer(
            xp,
            jnp.zeros((bp, M_CONV - 1, M_CONV_DIM), dt_),
            jnp.zeros((bp, M_HEADS, M_HEADDIM, M_STATE), dt_),
            jnp.zeros((bp, H_HEADS, H_KDIM, H_VDIM), dt_),
            jnp.zeros((bp, R_HEADS, R_KDIM, R_VDIM), dt_),
            pos_p, N_META, lbs[l], p)
        conv_p.append(c); ssm_p.append(s); hgrn_p.append(h); ret_p.append(r)
        xs, c, s, h, r = trunk_layer(xs, state_conv[l], state_ssm[l], state_hgrn[l], state_ret[l],
                                     pos_s, 0, lbs[l], p)
        conv_s.append(c); ssm_s.append(s); hgrn_s.append(h); ret_s.append(r)

    y_prompt = xp[:, N_META:]
    y_sample = xs
    return (y_prompt, y_sample,
            jnp.stack(ssm_p), jnp.stack(conv_p), jnp.stack(hgrn_p), jnp.stack(ret_p),
            jnp.stack(ssm_s), jnp.stack(conv_s), jnp.stack(hgrn_s), jnp.stack(ret_s))
```

```python
import numpy as np
import concourse.bass as bass
import concourse.mybir as mybir
from concourse.bass_utils import run_bass_kernel_spmd

F32 = mybir.dt.float32
BF16 = mybir.dt.bfloat16
AF = mybir.ActivationFunctionType
ALU = mybir.AluOpType
AX = mybir.AxisListType

D = 1024
KC = 8
DEPTH = 2
NS = 16
NMETA = 16
IN_DIM = 16400
DFF = 2816
ALPHA = float((2 * DEPTH) ** 0.25)
EPS_LN = 1e-5
DSIZE = {F32: 4, BF16: 2}

ENGS = ["pe", "act", "dve", "pool", "sp"]


class Buf:
    def __init__(self, name, off, nbytes, space):
        self.name, self.off, self.nbytes, self.space = name, off, nbytes, space
        self.recs = []


class Prog:
    def __init__(self, nc, sb_bytes, n_dma_sems=24):
        self.nc = nc
        self.sb_bytes = sb_bytes
        self.q = {e: [] for e in ENGS}
        self.cnt = {e: 0 for e in ENGS}
        self.waited = {e: {} for e in ENGS}
        self.sems = {}
        self.dma_tot = [0] * n_dma_sems
        self.n_dma_sems = n_dma_sems
        self.dma_rr = 0
        self.bufs = []
        self.freed = []
        self.top = 0
        self.stack = []
        self.out_events = []
        self.phases = []

    def phase(self, name):
        self.phases.append((name, dict(self.cnt)))

    def alloc(self, name, nbytes):
        nbytes = (nbytes + 63) // 64 * 64
        b = Buf(name, self.top, nbytes, "sb")
        self.top += nbytes
        assert self.top <= self.sb_bytes, (name, self.top, self.sb_bytes)
        ev = {}
        keep = []
        for (o, n, evs) in self.freed:
            if o < b.off + nbytes and b.off < o + n:
                for k, v in evs.items():
                    ev[k] = max(ev.get(k, 0), v)
                if not (b.off <= o and o + n <= b.off + nbytes):
                    keep.append((o, n, evs))
            else:
                keep.append((o, n, evs))
        self.freed = keep
        for k, v in ev.items():
            b.recs.append([0, 128, b.off, b.off + nbytes, "W", "seed", k, v])
        self.bufs.append(b)
        return b

    def mark(self):
        self.stack.append((self.top, len(self.bufs)))

    def release(self):
        top, nb = self.stack.pop()
        for b in self.bufs[nb:]:
            evs = {}
            for r in b.recs:
                evs[r[6]] = max(evs.get(r[6], 0), r[7])
            if evs:
                self.freed.append((b.off, b.nbytes, evs))
        self.bufs = self.bufs[:nb]
        self.top = top

    def view(self, buf, dtype, shape, boff=0):
        es = DSIZE[dtype]
        n = int(np.prod(shape))
        assert boff + n * es <= buf.nbytes, (buf.name, boff, n * es, buf.nbytes)
        if buf.space == "sb":
            base = self.sb
            start = (buf.off + boff) // 4
            words = (n * es + 3) // 4
            ap = base[:, start:start + words]
        else:
            base = self.ps
            start = (buf.off + boff) // 4
            words = (n * es + 3) // 4
            ap = base[:, start:start + words]
        if dtype != F32:
            ap = ap.bitcast(dtype)
            ap = ap[:, 0:n]
        if len(shape) == 2:
            ap = ap.rearrange("p (a b) -> p a b", a=shape[0])
        elif len(shape) == 3:
            ap = ap.rearrange("p (a b c) -> p a b c", a=shape[0], b=shape[1])
        return ap

    def _range(self, ap):
        t = ap.tensor
        es = DSIZE[ap.dtype]
        aps = ap.ap
        pstep = aps[0][0]
        off = int(ap.offset)
        if pstep == 0:
            pstep = self.pitch_elems[(t.name, es)]
        plo = off // pstep
        phi = plo + aps[0][1]
        flo = off % pstep
        fhi = flo + 1
        for (s, c) in aps[1:]:
            fhi += (c - 1) * abs(s)
        return plo, phi, flo * es, fhi * es

    def _find(self, ap):
        name = ap.tensor.name
        plo, phi, blo, bhi = self._range(ap)
        if name == "PS":
            b0 = blo // 2048
            b1 = (bhi - 1) // 2048
            return [(self.psbanks[b], 0, 128, b * 2048, (b + 1) * 2048) for b in range(b0, b1 + 1)]
        if name != "SB":
            return []
        res = []
        for b in self.bufs:
            if b.off < bhi and blo < b.off + b.nbytes:
                res.append((b, plo, phi, max(blo, b.off), min(bhi, b.off + b.nbytes)))
        assert res, ("no buf for ap", name, blo, bhi)
        return res

    def _deps(self, eng, accesses, is_dma):
        need = {}
        touched = []
        for ap, kind in accesses:
            for (b, plo, phi, blo, bhi) in self._find(ap):
                ps = b.space == "ps"
                for r in b.recs:
                    if r[0] < phi and plo < r[1] and r[2] < bhi and blo < r[3]:
                        if not (ps or r[4] == "W" or kind == "W"):
                            continue
                        if r[5] == eng and not is_dma:
                            if eng == "pe":
                                continue
                            if r[4] == "R" and not ps:
                                continue
                            if r[4] == "R" and ps and kind == "R":
                                continue
                        need[r[6]] = max(need.get(r[6], 0), r[7])
                touched.append((b, plo, phi, blo, bhi, kind))
        return need, touched

    def _record(self, touched, eng, key, val):
        for (b, plo, phi, blo, bhi, kind) in touched:
            if kind == "W":
                b.recs = [r for r in b.recs
                          if not (plo <= r[0] and r[1] <= phi and blo <= r[2] and r[3] <= bhi)]
                b.recs.append([plo, phi, blo, bhi, "W", eng, key, val])
            else:
                for r in b.recs:
                    if r[4] == "R" and r[5] == eng and r[6] == key and r[0] == plo and r[1] == phi \
                            and r[2] == blo and r[3] == bhi:
                        r[7] = val
                        break
                else:
                    b.recs.append([plo, phi, blo, bhi, "R", eng, key, val])

    def _emit_waits(self, eng, need):
        w = self.waited[eng]
        for k, v in need.items():
            if w.get(k, 0) >= v:
                continue
            w[k] = v
            self.q[eng].append(("wait", k, v))

    def I(self, eng, meth, reads=(), writes=(), **kw):
        acc = []
        for k, v in kw.items():
            if isinstance(v, bass.AP):
                if v.tensor.name not in ("SB", "PS"):
                    continue
                acc.append((v, "W" if k in ("out", "accum_out", "ap") else "R"))
        for a in reads:
            acc.append((a, "R"))
        for a in writes:
            acc.append((a, "W"))
        need, touched = self._deps(eng, acc, False)
        self._emit_waits(eng, need)
        self.cnt[eng] += 1
        self._record(touched, eng, eng, self.cnt[eng])
        self.q[eng].append(("ins", meth, kw))

    def dma(self, eng, pairs, slow=False):
        acc = []
        for (o, i) in pairs:
            if o.tensor.name == "SB":
                acc.append((o, "W"))
            if i.tensor.name == "SB":
                acc.append((i, "R"))
        need, touched = self._deps(eng, acc, True)
        s = self.dma_rr
        self.dma_rr = (self.dma_rr + 1) % self.n_dma_sems
        key = ("dma", s)
        need[key] = max(need.get(key, 0), self.dma_tot[s])
        self._emit_waits(eng, need)
        self.dma_tot[s] += 16 * len(pairs)
        self._record(touched, "dma", key, self.dma_tot[s])
        self.q[eng].append(("dma", s, pairs, slow))
        return (key, self.dma_tot[s])

    def final_wait(self, eng="sp"):
        need = {("dma", s): self.dma_tot[s] for s in range(self.n_dma_sems) if self.dma_tot[s]}
        for e in ("pe", "act", "dve", "pool"):
            if self.cnt[e]:
                need[e] = self.cnt[e]
        self._emit_waits(eng, need)

    def lower(self, block):
        nc = self.nc
        engmap = {"pe": "tensor", "act": "scalar", "dve": "vector", "pool": "gpsimd", "sp": "sync"}

        def semof(k):
            return self.sems[k]

        def run(ename, engine):
            for it in self.q[ename]:
                if it[0] == "wait":
                    engine.wait_ge(semof(it[1]), it[2])
                elif it[0] == "ins":
                    getattr(engine, it[1])(**it[2]).then_inc(semof(ename), 1)
                else:
                    s = semof(("dma", it[1]))
                    for (o, i) in it[2]:
                        if it[3]:
                            engine.dma_start(out=o, in_=i, allow_slow_non_contiguous=True).then_inc(s, 16)
                        else:
                            engine.dma_start(out=o, in_=i).then_inc(s, 16)

        for ename in ENGS:
            if not self.q[ename]:
                continue
            dec = getattr(block, engmap[ename])

            def mk(en):
                def f(engine):
                    run(en, engine)
                return f
            dec(mk(ename))


O_MZ, O_MX, O_MB, O_MC, O_MDT = 0, 1024, 2048, 2560, 3072
O_HQ, O_HF, O_HI, O_HG = 3088, 4112, 5136, 6160
O_RQ, O_RK, O_RV, O_RG = 7184, 8208, 9232, 11280
O_GATE = 13328
GAMMA = [1.0 - 2.0 ** (-5.0 - h) for h in range(4)]


def build(NCH, layers=DEPTH):
    SEQ = 128 * NCH
    NTP = NMETA + SEQ
    NT = NTP + NS
    nc = bass.Bass("TRN2", target_bir_lowering=False, dynamic_dma_scratch_size=8192)
    dr = {}

    def din(name, shape, dt=F32):
        dr[name] = nc.dram_tensor(name, list(shape), dt, kind="ExternalInput").ap()

    def dout(name, shape, dt=F32):
        dr[name] = nc.dram_tensor(name, list(shape), dt, kind="ExternalOutput").ap()

    din("x_prompt", [SEQ, D]); din("x_sample", [NS, D]); din("meta_tokens", [NMETA, D])
    din("state_ssm", [DEPTH, NS, 16, 64, 128]); din("state_conv", [DEPTH, NS, 3, 2048])
    din("state_hgrn", [DEPTH, NS, 8, 128, 128]); din("state_ret", [DEPTH, NS, 4, 256, 512])
    din("ln_in_g", [D]); din("ln_in_b", [D])
    din("w_in", [DEPTH, D, IN_DIM]); din("conv_w", [DEPTH, 4, 2048]); din("conv_b", [DEPTH, 2048])
    din("dt_bias", [DEPTH, 16]); din("a_log", [DEPTH, 16]); din("d_skip", [DEPTH, 16])
    din("m_norm_w", [DEPTH, D]); din("hgrn_lb_logits", [DEPTH, D]); din("h_norm_w", [DEPTH, D])
    din("w_br_m", [DEPTH, D, D]); din("w_br_h", [DEPTH, D, D]); din("w_br_r", [DEPTH, 2 * D, D])
    din("w_out", [DEPTH, D, D])
    din("w_ffn_in", [DEPTH, D, 2 * DFF]); din("w_ffn_out", [DEPTH, DFF, D])
    din("ln1_g", [DEPTH, D]); din("ln1_b", [DEPTH, D]); din("ln2_g", [DEPTH, D]); din("ln2_b", [DEPTH, D])
    for nm in ["c_ident", "c_ones", "c_triu", "c_strl", "c_bd"]:
        din(nm, [128, 128])
    din("c_rowm", [128, 4]); din("c_eye16", [128, 256])
    din("c_dmask", [4, 128, 128]); din("c_g1", [4, 128, 128]); din("c_kd", [4, 128, 2])
    din("c_cos", [128, NT]); din("c_sin", [128, NT])
    dout("y_prompt", [SEQ, D]); dout("y_sample", [NS, D])
    dout("nssm_p", [DEPTH, 16, 64, 128]); dout("nconv_p", [DEPTH, 3, 2048])
    dout("nhgrn_p", [DEPTH, 8, 128, 128]); dout("nret_p", [DEPTH, 4, 256, 512])
    dout("nssm_s", [DEPTH, NS, 16, 64, 128]); dout("nconv_s", [DEPTH, NS, 3, 2048])
    dout("nhgrn_s", [DEPTH, NS, 8, 128, 128]); dout("nret_s", [DEPTH, NS, 4, 256, 512])

    SB_BYTES = 184 * 1024
    P = Prog(nc, SB_BYTES)
    P.pitch_elems = {}

    with (
        nc.sbuf_tensor("SB", [128, SB_BYTES // 4], F32) as SBt,
        nc.psum_tensor("PS", [128, 8 * 512], F32) as PSt,
    ):
        P.sb = SBt[:]
        P.ps = PSt[:]
        P.pitch_elems[("SB", 4)] = SBt[:].ap[0][0]
        P.pitch_elems[("SB", 2)] = SBt[:].ap[0][0] * 2
        P.pitch_elems[("PS", 4)] = PSt[:].ap[0][0]
        P.pitch_elems[("PS", 2)] = PSt[:].ap[0][0] * 2
        P.psbanks = [Buf(f"bank{b}", b * 2048, 2048, "ps") for b in range(8)]

        def bank(b, dtype=F32, shape=None):
            if shape is None:
                shape = [512] if dtype == F32 else [1024]
            return P.view(P.psbanks[b], dtype, shape)

        def A(name, dtype, shape):
            b = P.alloc(name, int(np.prod(shape)) * DSIZE[dtype])
            return P.view(b, dtype, list(shape))

        def mm(out, lhsT, rhs, start=True, stop=True):
            P.I("pe", "matmul", out=out, lhsT=lhsT, rhs=rhs, start=start, stop=stop)

        def tr(out, in_, idn):
            P.I("pe", "transpose", out=out, in_=in_, identity=idn)

        def act(out, in_, func, **kw):
            P.I("act", "activation", out=out, in_=in_, func=func, **kw)

        def tt(out, in0, in1, op, eng="dve"):
            P.I(eng, "tensor_tensor", out=out, in0=in0, in1=in1, op=op)

        def ts(out, in0, s1, s2, op0, op1=None, eng="dve"):
            if op1 is None:
                P.I(eng, "tensor_scalar", out=out, in0=in0, scalar1=s1, scalar2=None, op0=op0)
            else:
                P.I(eng, "tensor_scalar", out=out, in0=in0, scalar1=s1, scalar2=s2, op0=op0, op1=op1)

        def stt(out, in0, scalar, in1, op0, op1, **kw):
            P.I("dve", "scalar_tensor_tensor", out=out, in0=in0, scalar=scalar, in1=in1, op0=op0, op1=op1, **kw)

        def sigm(out, in_):
            act(out, in_, AF.Exp, scale=-1.0)
            act(out, out, AF.Ln, bias=1.0, scale=1.0)
            act(out, out, AF.Exp, scale=-1.0)

        def cp(out, in_, eng="dve"):
            if eng == "act_copy":
                P.I("act", "activation", out=out, in_=in_, func=AF.Copy)
            else:
                P.I(eng, "tensor_copy", out=out, in_=in_)

        xT = A("xT", F32, [KC, NT])
        xbf = A("xbf", BF16, [KC, NT])
        yT = A("yT", BF16, [KC, NT])
        ident = A("ident", F32, [128]); ones = A("ones", F32, [128]); triu = A("triu", F32, [128])
        strl = A("strl", F32, [128]); bdm = A("bdm", F32, [128]); rowm = A("rowm", F32, [4])
        eye16 = A("eye16", BF16, [16, 16]); identb = A("identb", BF16, [128]); onesb = A("onesb", BF16, [128])
        lnp = A("lnp", F32, [2 + 4 * DEPTH, KC])
        P.dma("sp", [(ident, dr["c_ident"]), (ones, dr["c_ones"]), (triu, dr["c_triu"]), (strl, dr["c_strl"]),
                     (bdm, dr["c_bd"]), (rowm, dr["c_rowm"])])
        P.dma("pool", [(eye16, dr["c_eye16"].rearrange("p (a b) -> p a b", a=16))])
        prs = [(lnp[:, 0, :], dr["ln_in_g"].rearrange("(k p) -> p k", p=128)),
               (lnp[:, 1, :], dr["ln_in_b"].rearrange("(k p) -> p k", p=128))]
        for l in range(DEPTH):
            for j, nm in enumerate(["ln1_g", "ln1_b", "ln2_g", "ln2_b"]):
                prs.append((lnp[:, 2 + 4 * l + j, :], dr[nm][l].rearrange("(k p) -> p k", p=128)))
        P.dma("sp", prs, slow=True)
        cp(identb, ident)
        cp(onesb, ones)

        ttiles = [(0, NMETA, dr["meta_tokens"][:, :])]
        for c in range(NCH):
            ttiles.append((NMETA + 128 * c, 128, dr["x_prompt"][128 * c:128 * (c + 1), :]))
        ttiles.append((NTP, NS, dr["x_sample"][:, :]))
        pchunks = [(c0_, T_) for (c0_, T_, _) in ttiles[:-1]]

        def mkblocks(lo, hi):
            res = []
            c0 = lo
            while c0 < hi:
                n = min(512, hi - c0)
                res.append((c0, n))
                c0 += n
            return res
        tblocks = mkblocks(0, NT)

        P.mark()
        tin = [A(f"tin{i}", F32, [D]) for i in range(2)]
        tnr = [A(f"tnr{i}", F32, [D]) for i in range(2)]
        lst = [A(f"lnst{i}", F32, [16]) for i in range(2)]
        for i, (col0, T, src) in enumerate(ttiles):
            a, nr = tin[i % 2], tnr[i % 2]
            stats = lst[i % 2][:, 0:12].rearrange("p (a b) -> p a b", a=2)
            mv = lst[i % 2][:, 12:16]
            P.dma("sp", [(a[:T], src)])
            P.I("dve", "bn_stats", out=stats[:T, 0, :], in_=a[:T, 0:512])
            P.I("dve", "bn_stats", out=stats[:T, 1, :], in_=a[:T, 512:1024])
            P.I("dve", "bn_aggr", out=mv[:T, 0:2], in_=lst[i % 2][:T, 0:12])
            act(mv[:T, 2:3], mv[:T, 1:2], AF.Ln, bias=EPS_LN, scale=1.0)
            act(mv[:T, 3:4], mv[:T, 2:3], AF.Exp, scale=-0.5)
            ts(nr[:T], a[:T], mv[:T, 0:1], mv[:T, 3:4], ALU.subtract, ALU.mult)
            pb = 2 * (i % 2)
            for kc in range(KC):
                tr(bank(pb + kc // 4)[:, (kc % 4) * 128:(kc % 4) * 128 + T], nr[:T, kc * 128:(kc + 1) * 128], ident[:T, :T])
            for kc in range(KC):
                act(xT[:, kc, col0:col0 + T], bank(pb + kc // 4)[:, (kc % 4) * 128:(kc % 4) * 128 + T],
                    AF.Identity, scale=lnp[:, 0, kc:kc + 1], bias=lnp[:, 1, kc:kc + 1])
            cp(xbf[:, :, col0:col0 + T], xT[:, :, col0:col0 + T])
        P.release()

        def layer_norm_fm(gi, bi, last=False):
            P.phase(f"ln{gi}")
            P.mark()
            sq = A("lnsq", F32, [2, 512]); mmb = A("lnm", F32, [4, 512]); ttb = A("lnt", F32, [2, 512])
            for (c0, n) in tblocks:
                s1, s2 = bank(4), bank(5)
                for kc in range(KC):
                    mm(s1[:, :n], ones, xT[:, kc, c0:c0 + n], kc == 0, kc == KC - 1)
                for kc in range(KC):
                    act(sq[:, kc % 2, :n], xT[:, kc, c0:c0 + n], AF.Square)
                    mm(s2[:, :n], ones, sq[:, kc % 2, :n], kc == 0, kc == KC - 1)
                mean, msq, var, rstd = mmb[:, 0, :n], mmb[:, 1, :n], mmb[:, 2, :n], mmb[:, 3, :n]
                act(mean, s1[:, :n], AF.Copy, scale=1.0 / D)
                act(msq, s1[:, :n], AF.Square, scale=1.0 / D)
                stt(var, s2[:, :n], 1.0 / D, msq, ALU.mult, ALU.subtract)
                act(var, var, AF.Ln, bias=EPS_LN, scale=1.0)
                act(rstd, var, AF.Exp, scale=-0.5)
                for kc in range(KC):
                    t = ttb[:, kc % 2, :n]
                    tt(t, xT[:, kc, c0:c0 + n], mean, ALU.subtract)
                    tt(t, t, rstd, ALU.mult)
                    act(xT[:, kc, c0:c0 + n], t, AF.Identity, scale=lnp[:, gi, kc:kc + 1], bias=lnp[:, bi, kc:kc + 1])
                    if not last:
                        cp(xbf[:, kc, c0:c0 + n], xT[:, kc, c0:c0 + n], eng="pool")
            P.release()

        def ffn(l):
            P.phase(f"ffn{l}")
            P.mark()
            passes = [(0, 8), (8, 15), (15, 22)]
            hT = yT
            wgs = [A(f"wg{i}", BF16, [KC, 128]) for i in range(2)]
            wus = [A(f"wu{i}", BF16, [KC, 128]) for i in range(2)]
            wos = [A(f"wo{i}", BF16, [8, 128]) for i in range(2)]
            sgs = [A(f"sg{i}", F32, [512]) for i in range(2)]
            win = dr["w_ffn_in"][l].rearrange("(k p) c -> p k c", p=128)
            wout = dr["w_ffn_out"][l].rearrange("(j p) c -> p j c", p=128)
            it = 0
            for ps_, (j0, j1) in enumerate(passes):
                TPP = j1 - j0
                for j in range(TPP):
                    jt = j0 + j
                    wg, wu = wgs[jt % 2], wus[jt % 2]
                    P.dma("pool", [(wg, win[:, :, jt * 128:(jt + 1) * 128]),
                                   (wu, win[:, :, DFF + jt * 128:DFF + (jt + 1) * 128])])
                    for (c0, n) in tblocks:
                        pg, pu, sg = bank(it % 2), bank(2 + it % 2), sgs[it % 2]
                        it += 1
                        for kc in range(KC):
                            mm(pg[:, :n], wg[:, kc, :], xbf[:, kc, c0:c0 + n], kc == 0, kc == KC - 1)
                        for kc in range(KC):
                            mm(pu[:, :n], wu[:, kc, :], xbf[:, kc, c0:c0 + n], kc == 0, kc == KC - 1)
                        act(sg[:, :n], pg[:, :n], AF.Silu)
                        tt(hT[:, j, c0:c0 + n], sg[:, :n], pu[:, :n], ALU.mult)
                for do in range(KC):
                    wo = wos[do % 2]
                    P.dma("pool", [(wo[:, 0:TPP, :], wout[:, j0:j1, do * 128:(do + 1) * 128])])
                    for (c0, n) in tblocks:
                        po = bank(6 + it % 2)
                        it += 1
                        for j in range(TPP):
                            mm(po[:, :n], wo[:, j, :], hT[:, j, c0:c0 + n], j == 0, j == TPP - 1)
                        if ps_ == 0:
                            stt(xT[:, do, c0:c0 + n], xT[:, do, c0:c0 + n], ALPHA, po[:, :n], ALU.mult, ALU.add)
                        else:
                            tt(xT[:, do, c0:c0 + n], xT[:, do, c0:c0 + n], po[:, :n], ALU.add)
            P.release()

        def run_lanes(gens):
            gens = list(gens)
            while gens:
                for g_ in list(gens):
                    try:
                        next(g_)
                    except StopIteration:
                        gens.remove(g_)

        def finish(ysrc, T, F, gate, ychunks, col0, junk, tmp_bf, sc, pt):
            act(junk[:T, :F], ysrc, AF.Square, accum_out=sc[:T, 0:1])
            act(sc[:T, 1:2], sc[:T, 0:1], AF.Ln, scale=1.0 / F, bias=1e-6)
            act(sc[:T, 2:3], sc[:T, 1:2], AF.Exp, scale=-0.5)
            yield
            stt(tmp_bf[:T, :F], ysrc, sc[:T, 2:3], gate, ALU.mult, ALU.mult)
            yield
            for j in range(F // 128):
                tr(pt[:, j * 128:j * 128 + T], tmp_bf[:T, j * 128:(j + 1) * 128], identb[:T, :T])
            yield
            for j, yc in enumerate(ychunks):
                cp(yT[:, yc, col0:col0 + T], pt[:, j * 128:j * 128 + T], eng="dve" if j % 2 else "act_copy")
            yield

        wv = lambda l: dr["w_in"][l].rearrange("(k p) c -> p k c", p=128)

        def ssd_unit(l, g):
            P.phase(f"ssd{l}.{g}")
            P.mark()
            slab = A("slab", BF16, [KC, 772])
            W = wv(l)
            P.dma("pool", [(slab[:, :, 0:256], W[:, :, O_MX + g * 256:O_MX + (g + 1) * 256]),
                           (slab[:, :, 256:384], W[:, :, O_MB + g * 128:O_MB + (g + 1) * 128]),
                           (slab[:, :, 384:512], W[:, :, O_MC + g * 128:O_MC + (g + 1) * 128]),
                           (slab[:, :, 512:768], W[:, :, O_MZ + g * 256:O_MZ + (g + 1) * 256]),
                           (slab[:, :, 768:772], W[:, :, O_MDT + 4 * g:O_MDT + 4 * g + 4])])
            cw = A("cw", F32, [4, 4]); cb = A("cb", F32, [4])
            cbase = [g * 256, g * 256 + 128, 1024 + g * 128, 1536 + g * 128]
            prs = []
            for j in range(4):
                prs.append((cw[:, j, :], dr["conv_w"][l].rearrange("k c -> c k")[cbase[j]:cbase[j] + 128, :]))
                prs.append((cb[:, j:j + 1], dr["conv_b"][l].rearrange("(c o) -> c o", o=1)[cbase[j]:cbase[j] + 128, :]))
            P.dma("sp", prs, slow=True)
            hp = A("hp", F32, [3, 4])
            mw = A("mw", F32, [256])
            P.dma("sp", [(hp[:, 0, :], dr["dt_bias"][l, 4 * g:4 * g + 4].partition_broadcast(128)),
                         (hp[:, 1, :], dr["a_log"][l, 4 * g:4 * g + 4].partition_broadcast(128)),
                         (hp[:, 2, :], dr["d_skip"][l, 4 * g:4 * g + 4].partition_broadcast(128)),
                         (mw, dr["m_norm_w"][l, g * 256:(g + 1) * 256].partition_broadcast(128))])
            act(hp[:, 1, :], hp[:, 1, :], AF.Exp)
            ts(hp[:, 1, :], hp[:, 1, :], -1.0, None, ALU.mult)
            dsk_bc = hp[:, 2, :].unsqueeze(2).to_broadcast([128, 4, 64])
            S = A("S", F32, [256]); Sbf = A("Sbf", BF16, [256])
            xt3 = lambda a, T: a[:T, :].rearrange("p (h q) -> p h q", h=4)

            class Lane:
                pass

            def mklane(bk):
                L = Lane()
                L.bk = bk
                L.raw = A("raw", F32, [4, 131]); L.cv = A("cv", F32, [4, 128]); L.xcb = A("xcb", BF16, [4, 128])
                L.xtok = A("xtok", BF16, [256]); L.xdt = A("xdt", BF16, [256]); L.xw = A("xw", BF16, [256]); L.btok = A("btok", BF16, [128])
                L.dtt = A("dtt", F32, [16]); L.cc = A("cc", F32, [16])
                L.cbm = A("cbm", F32, [128]); L.Lh = A("Lh", F32, [4, 128]); L.dec = A("dec", F32, [4, 128]); L.wT = A("wT", BF16, [4, 128])
                L.yi = A("yi", F32, [256]); L.tmpf = A("tmpf", F32, [256]); L.zs = A("zs", F32, [256]); L.ynb = A("ynb", BF16, [256])
                L.sc = A("sc", F32, [4])
                return L

            def proj(L, col0, T):
                pf = bank(L.bk)
                for j in range(4):
                    for kc in range(KC):
                        mm(pf[:, j * 128:j * 128 + T], slab[:, kc, j * 128:(j + 1) * 128], xbf[:, kc, col0:col0 + T],
                           kc == 0, kc == KC - 1)
                pz = bank(L.bk + 1)
                for kc in range(KC):
                    mm(pz[:T, 0:260], xbf[:, kc, col0:col0 + T], slab[:, kc, 512:772], kc == 0, kc == KC - 1)
                return pf, pz

            def dt_calc(L, pz, T):
                dtt = L.dtt
                tt(dtt[:T, 8:12], pz[:T, 256:260], hp[:T, 0, :], ALU.add)
                act(dtt[:T, 8:12], dtt[:T, 8:12], AF.Exp)
                act(dtt[:T, 0:4], dtt[:T, 8:12], AF.Ln, bias=1.0, scale=1.0)
                tt(dtt[:T, 4:8], dtt[:T, 0:4], hp[:T, 1, :], ALU.mult)

            def conv_silu(L, T, taps):
                cv = L.cv
                for j in range(4):
                    act(cv[:, j, :T], taps(j, 0), AF.Identity, scale=cw[:, j, 0:1], bias=cb[:, j:j + 1])
                for k in range(1, 4):
                    for j in range(4):
                        stt(cv[:, j, :T], taps(j, k), cw[:, j, k:k + 1], cv[:, j, :T], ALU.mult, ALU.add)
                sigm(L.dec[:, :, :T], cv[:, :, :T])
                tt(L.xcb[:, :, :T], cv[:, :, :T], L.dec[:, :, :T], ALU.mult)

            def gate_finish(L, pz, T, col0):
                ysb = L.yi
                tt(xt3(L.tmpf, T), xt3(L.xtok, T), dsk_bc[:T], ALU.mult)
                tt(ysb[:T, :], ysb[:T, :], L.tmpf[:T, :], ALU.add)
                yield
                tt(ysb[:T, :], ysb[:T, :], L.zs[:T, :], ALU.mult)
                yield
                yield from finish(ysb[:T, :], T, 256, mw[:T, :], [2 * g, 2 * g + 1], col0, L.tmpf, L.ynb, L.sc,
                                  bank(L.bk + 3, BF16)[:, 512:1024])

            def chunk_gen(L, Lprev, ci, col0, T):
                first = ci == 0
                raw, xcb, dtt, cc = L.raw, L.xcb, L.dtt, L.cc
                if first:
                    P.I("dve", "memset", ap=raw[:, :, 0:3], constant=0.0)
                pf, pz = proj(L, col0, T)
                yield
                act(raw[:, :, 3:3 + T], pf[:, :].rearrange("p (a b) -> p a b", a=4)[:, :, :T], AF.Copy)
                dt_calc(L, pz, T)
                sigm(L.zs[:T, :], pz[:T, 0:256])
                tt(L.zs[:T, :], L.zs[:T, :], pz[:T, 0:256], ALU.mult)
                yield
                if ci == len(pchunks) - 1:
                    prs = [(dr["nconv_p"][l].rearrange("k c -> c k")[cbase[j]:cbase[j] + 128, :], raw[:, j, T:T + 3])
                           for j in range(4)]
                    P.dma("sp", prs, slow=True)
                if not first:
                    pass
                conv_silu(L, T, lambda j, k: raw[:, j, k:k + T])
                yield
                pc = bank(L.bk + 1)[:, 300:312]
                mm(pc[:T, 0:4], triu[:T, :T], dtt[:T, 4:8])
                mm(pc[:, 4:8], ones[:T, :], dtt[:T, 4:8])
                ptb = bank(L.bk, BF16)
                for j in range(3):
                    tr(ptb[:T, j * 128:(j + 1) * 128], xcb[:, j, :T], identb)
                yield
                if T == 128:
                    cp(cc[:, 0:8], pc[:, 0:8], eng="act_copy")
                else:
                    cp(cc[:T, 0:4], pc[:T, 0:4], eng="act_copy")
                    cp(cc[:, 4:8], pc[:, 4:8], eng="act_copy")
                cp(L.xtok[:T, :], ptb[:T, 0:256], eng="act_copy")
                cp(L.btok[:T, :], ptb[:T, 256:384], eng="act_copy")
                yield
                act(cc[:T, 8:12], cc[:T, 0:4], AF.Exp)
                act(cc[:, 12:16], cc[:, 4:8], AF.Exp)
                tt(dtt[:T, 8:12], cc[:T, 4:8], cc[:T, 0:4], ALU.subtract)
                tt(xt3(L.xdt, T), xt3(L.xtok, T), dtt[:T, 0:4].unsqueeze(2).to_broadcast([T, 4, 64]), ALU.mult)
                yield
                act(dtt[:T, 8:12], dtt[:T, 8:12], AF.Exp)
                pcb = bank(L.bk)
                mm(pcb[:T, :T], xcb[:, 2, :T], xcb[:, 3, :T])
                for h in range(4):
                    act(L.Lh[:T, h, :T], strl[:T, :T], AF.Identity, scale=dtt[:T, 4 + h:5 + h])
                yield
                tt(dtt[:T, 12:16], dtt[:T, 8:12], dtt[:T, 0:4], ALU.mult)
                tt(L.cbm[:T, :T], pcb[:T, :T], triu[:T, :T], ALU.mult)
                yield
                tt(xt3(L.xw, T), xt3(L.xtok, T), dtt[:T, 12:16].unsqueeze(2).to_broadcast([T, 4, 64]), ALU.mult)
                pd = bank(L.bk)
                for h in range(4):
                    mm(pd[:T, h * 128:h * 128 + T], L.Lh[:T, h, :T], triu[:T, :T])
                yield
                pyy = bank(L.bk + 2)
                if not first:
                    mm(pyy[:T, 256:512], xcb[:, 3, :T], Sbf[:, :])
                pS = bank(L.bk + 3)
                mm(pS[:, 0:256], L.btok[:T, :], L.xw[:T, :])
                act(L.dec[:T, :, :T], pd[:T, :].rearrange("p (a b) -> p a b", a=4)[:, :, :T], AF.Exp)
                yield
                if not first:
                    tt(xt3(L.tmpf, T), pyy[:T, 256:512].rearrange("p (h q) -> p h q", h=4),
                       cc[:T, 8:12].unsqueeze(2).to_broadcast([T, 4, 64]), ALU.mult)
                if first:
                    cp(S[:, :], pS[:, 0:256])
                else:
                    tt(S[:, :].rearrange("p (h q) -> p h q", h=4), S[:, :].rearrange("p (h q) -> p h q", h=4),
                       cc[:, 12:16].unsqueeze(2).to_broadcast([128, 4, 64]), ALU.mult)
                    tt(S[:, :], S[:, :], pS[:, 0:256], ALU.add)
                cp(Sbf[:, :], S[:, :])
                yield
                tt(L.wT[:T, :, :T], L.dec[:T, :, :T], L.cbm[:T, :T].unsqueeze(1).to_broadcast([T, 4, T]), ALU.mult)
                yield
                for h in range(4):
                    mm(pyy[:T, h * 64:(h + 1) * 64], L.wT[:T, h, :T], L.xdt[:T, h * 64:(h + 1) * 64])
                yield
                if first:
                    cp(L.yi[:T, :], pyy[:T, 0:256], eng="act_copy")
                else:
                    tt(L.yi[:T, :], pyy[:T, 0:256], L.tmpf[:T, :], ALU.add)
                yield
                yield from gate_finish(L, pz, T, col0)
                if ci == len(pchunks) - 1:
                    pt = bank(L.bk + 2)
                    for j in range(2):
                        tr(pt[:, j * 128:(j + 1) * 128], S[:, j * 128:(j + 1) * 128], ident)
                    cp(L.tmpf[:, :], pt[:, 0:256])
                    P.dma("sp", [(dr["nssm_p"][l, 4 * g:4 * g + 4].rearrange("h p n -> (h p) n").rearrange("(t q) n -> q t n", q=128),
                                  L.tmpf[:, :].rearrange("p (t n) -> p t n", t=2))])

            P.mark()
            lanes = [mklane(0), mklane(4)]

            def lane_gen(li):
                L = lanes[li]
                for ci in range(li, len(pchunks), 2):
                    col0, T = pchunks[ci]
                    if ci > 0:
                        Lp = lanes[1 - li]
                        Tp = pchunks[ci - 1][1]
                        cp(L.raw[:, :, 0:3], Lp.raw[:, :, Tp:Tp + 3], eng="pool")
                    yield from chunk_gen(L, None, ci, col0, T)
            def skewed(li):
                if li == 1:
                    for _ in range(10):
                        yield
                yield from lane_gen(li)
            run_lanes([skewed(0), skewed(1)])
            P.release()

            P.phase("sample")
            T = NS
            col0 = NTP
            L = mklane(0)
            raw, xcb, dtt = L.raw, L.xcb, L.dtt
            pf, pz = proj(L, col0, T)
            act(raw[:, :, 3:3 + T], pf[:, :].rearrange("p (a b) -> p a b", a=4)[:, :, :T], AF.Copy)
            pxb = bank(4)
            for kc in range(KC):
                mm(pxb[:T, :], xbf[:, kc, col0:col0 + T], slab[:, kc, 0:512], kc == 0, kc == KC - 1)
            cst = A("cst", F32, [3, 512]); rtk = A("rtk", F32, [512]); halo = A("halo", F32, [4, 3, 16])
            cp(rtk[:T, :], pxb[:T, :], eng="act_copy")
            P.dma("sp", [(cst[:T, :, j * 128:(j + 1) * 128], dr["state_conv"][l, :, :, cbase[j]:cbase[j] + 128]) for j in range(4)])
            P.dma("sp", [(dr["nconv_s"][l, :, 0:2, cbase[j]:cbase[j] + 128], cst[:T, 1:3, j * 128:(j + 1) * 128]) for j in range(4)] +
                        [(dr["nconv_s"][l, :, 2, cbase[j]:cbase[j] + 128], rtk[:T, j * 128:(j + 1) * 128]) for j in range(4)])
            ph = bank(5)
            for j in range(4):
                for k in range(3):
                    tr(ph[:, (j * 3 + k) * 16:(j * 3 + k + 1) * 16], cst[:T, k, j * 128:(j + 1) * 128], ident[:T, :T])
            cp(halo[:, :, :, :], ph[:, 0:192].rearrange("p (a b c) -> p a b c", a=4, b=3))
            conv_silu(L, T, lambda j, k: halo[:, j, k, :] if k < 3 else raw[:, j, 3:3 + T])
            dt_calc(L, pz, T)
            sigm(L.zs[:T, :], pz[:T, 0:256])
            tt(L.zs[:T, :], L.zs[:T, :], pz[:T, 0:256], ALU.mult)
            act(dtt[:T, 8:12], dtt[:T, 4:8], AF.Exp)
            ptb = bank(2, BF16)
            for j in range(4):
                tr(ptb[:T, j * 128:(j + 1) * 128], xcb[:, j, :T], identb)
            cp(L.xtok[:T, :], ptb[:T, 0:256], eng="act_copy")
            bct = A("bct", F32, [256]); bcm = A("bcm", BF16, [256])
            cp(bct[:T, :], ptb[:T, 256:512], eng="act_copy")
            xe = A("xe", F32, [512])
            tt(xe[:T, 0:256].rearrange("p (h q) -> p h q", h=4), xt3(L.xtok, T),
               dtt[:T, 0:4].unsqueeze(2).to_broadcast([T, 4, 64]), ALU.mult)
            cp(xe[:T, 256:512].rearrange("p (h q) -> p h q", h=4), dtt[:T, 8:12].unsqueeze(2).to_broadcast([T, 4, 64]))
            pq = bank(3)
            for j in range(4):
                tr(pq[:, j * 16:(j + 1) * 16], xe[:T, j * 128:(j + 1) * 128], ident[:T, :T])
            scs = A("scs", F32, [4, 16])
            cp(scs[:, :, :], pq[:, 0:64].rearrange("p (a b) -> p a b", a=4))
            sts = [A(f"st{i}", F32, [2, 128]) for i in range(2)]
            stns = [A(f"stn{i}", F32, [2, 128]) for i in range(2)]
            yfm = A("yfm", F32, [2, 16]); tm2s = [A(f"tm2{i}", F32, [128]) for i in range(2)]
            junks = [A(f"junk{i}", F32, [128]) for i in range(2)]
            bcms = [bcm, A("bcm1", BF16, [256])]

            def samp_gen(par):
                st, stn, tm2, junk, bcm_ = sts[par], stns[par], tm2s[par], junks[par], bcms[par]
                for b in range(par, NS, 2):
                    sview = lambda nm: dr[nm][l, b, 4 * g:4 * g + 4].rearrange("h p n -> (h p) n").rearrange("(t q) n -> q t n", q=128)
                    P.dma("sp", [(st[:, :, :], sview("state_ssm"))])
                    ts(bcm_[:T, :], bct[:T, :], ident[:T, b:b + 1], None, ALU.mult)
                    yield
                    pb_ = bank(6 + par)
                    mm(pb_[:, 0:256], onesb[:T, :], bcm_[:T, :])
                    yield
                    for j in range(2):
                        ts(tm2[:, :], pb_[:, 0:128], scs[:, j, b:b + 1], None, ALU.mult)
                        yield
                        stt(stn[:, j, :], st[:, j, :], scs[:, 2 + j, b:b + 1], tm2[:, :], ALU.mult, ALU.add)
                        yield
                        stt(junk[:, :], stn[:, j, :], 1.0, pb_[:, 128:256], ALU.mult, ALU.mult, accum_out=yfm[:, j, b:b + 1])
                        yield
                    P.dma("pool", [(sview("nssm_s"), stn[:, :, :])])
            run_lanes([samp_gen(0), samp_gen(1)])
            py = bank(4)
            for j in range(2):
                tr(py[:T, j * 128:(j + 1) * 128], yfm[:, j, :], ident)
            cp(L.yi[:T, :], py[:T, 0:256], eng="act_copy")
            for _ in gate_finish(L, pz, T, col0):
                pass
            P.release()

        def hgrn_pair(l, hs):
            P.phase(f"hgrn{l}.{hs[0]}")
            P.mark()
            W = wv(l)

            class Lane:
                pass

            def mklane(h, bk):
                L = Lane()
                L.h, L.bk = h, bk
                L.slab = A("slab", BF16, [KC, 512])
                P.dma("pool", [(L.slab[:, :, 0:128], W[:, :, O_HQ + h * 128:O_HQ + (h + 1) * 128]),
                               (L.slab[:, :, 128:256], W[:, :, O_HF + h * 128:O_HF + (h + 1) * 128]),
                               (L.slab[:, :, 256:384], W[:, :, O_HI + h * 128:O_HI + (h + 1) * 128]),
                               (L.slab[:, :, 384:512], W[:, :, O_HG + h * 128:O_HG + (h + 1) * 128])])
                lbp = L.lbp = A("lbp", F32, [4])
                L.hw = A("hw", F32, [128])
                P.dma("sp", [(lbp[:, 0:2], dr["hgrn_lb_logits"].rearrange("l c -> c l")[h * 128:(h + 1) * 128, :])], slow=True)
                P.dma("sp", [(L.hw, dr["h_norm_w"][l, h * 128:(h + 1) * 128].partition_broadcast(128))])
                if l == 0:
                    P.I("dve", "memset", ap=lbp[:, 2:3], constant=0.0)
                    P.I("dve", "memset", ap=lbp[:, 3:4], constant=1.0)
                else:
                    tt(lbp[:, 2:3], lbp[:, 0:1], lbp[:, 1:2], ALU.subtract)
                    act(lbp[:, 2:3], lbp[:, 2:3], AF.Exp)
                    ts(lbp[:, 2:3], lbp[:, 2:3], 1.0, None, ALU.add)
                    P.I("dve", "reciprocal", out=lbp[:, 2:3], in_=lbp[:, 2:3])
                    ts(lbp[:, 3:4], lbp[:, 2:3], -1.0, 1.0, ALU.mult, ALU.add)
                L.fv = A("fv", F32, [128]); L.kk = A("kk", F32, [128]); L.cum = A("cum", F32, [128]); L.e1 = A("e1", F32, [128])
                L.ksf = A("ksf", F32, [128]); L.qt = A("qt", BF16, [128]); L.qtm = A("qtm", BF16, [4, 128]); L.kt = A("kt", BF16, [128])
                L.khT = A("khT", BF16, [128]); L.khm = A("khm", BF16, [4, 128]); L.vtok = A("vtok", BF16, [128]); L.scm = A("scm", BF16, [128])
                L.gs = A("gs", F32, [128]); L.tmpf = A("tmpf", F32, [128]); L.ynb = A("ynb", BF16, [128]); L.sc = A("sc", F32, [4])
                L.S = A("S", F32, [128]); L.Sbf = A("Sbf", BF16, [128])
                L.qs = A("qs", F32, [16]); L.qmask = A("qmask", F32, [16, 16]); L.itk = A("itk", F32, [128]); L.itm = A("itm", BF16, [128])
                L.st = [A(f"st{i}", F32, [128]) for i in range(2)]; L.stn = [A(f"stn{i}", F32, [128]) for i in range(2)]
                L.tm2 = A("tm2", F32, [128])
                P.I("dve", "memset", ap=L.qtm[:, :, :], constant=0.0)
                return L

            def proj(L, col0, T):
                pf = bank(L.bk)
                for j in range(2):
                    for kc in range(KC):
                        mm(pf[:, j * 128:j * 128 + T], L.slab[:, kc, j * 128:(j + 1) * 128], xbf[:, kc, col0:col0 + T],
                           kc == 0, kc == KC - 1)
                pz = bank(L.bk + 1)
                for kc in range(KC):
                    mm(pz[:T, 0:256], xbf[:, kc, col0:col0 + T], L.slab[:, kc, 256:512], kc == 0, kc == KC - 1)
                return pf, pz

            def fk(L, pf, T):
                sigm(L.fv[:, :T], pf[:, 128:128 + T])
                ts(L.fv[:, :T], L.fv[:, :T], L.lbp[:, 3:4], L.lbp[:, 2:3], ALU.mult, ALU.add)
                ts(L.kk[:, :T], L.fv[:, :T], -1.0, 1.0, ALU.mult, ALU.add)

            def gate_early(L, pz, T):
                sigm(L.gs[:T, :], pz[:T, 128:256])
                tt(L.gs[:T, :], L.gs[:T, :], L.hw[:T, :], ALU.mult)

            def gate_finish(L, po, pz, T, col0):
                yield from finish(po, T, 128, L.gs[:T, :], [L.h], col0, L.tmpf, L.ynb, L.sc, bank(L.bk + 3, BF16)[:, 512:1024])

            def head_gen(L):
                h = L.h
                fv, kk, cum, e1, ksf, qt, qtm, kt, khT, khm, vtok, scm, S, Sbf = (
                    L.fv, L.kk, L.cum, L.e1, L.ksf, L.qt, L.qtm, L.kt, L.khT, L.khm, L.vtok, L.scm, L.S, L.Sbf)
                for ci, (col0, T) in enumerate(pchunks):
                    first = ci == 0
                    subs = [(0, 16)] if T == 16 else [(32 * j, 32) for j in range(4)]
                    pf, pz = proj(L, col0, T)
                    yield
                    fk(L, pf, T)
                    cp(vtok[:T, :], pz[:T, 0:128], eng="act_copy")
                    yield
                    gate_early(L, pz, T)
                    act(cum[:, :T], fv[:, :T], AF.Ln)
                    yield
                    for (s0, tsz) in subs:
                        P.I("dve", "tensor_tensor_scan", out=cum[:, s0:s0 + tsz], data0=ones[:, :tsz], data1=cum[:, s0:s0 + tsz],
                            initial=0.0, op0=ALU.mult, op1=ALU.add)
                    yield
                    act(e1[:, :T], cum[:, :T], AF.Exp)
                    act(cum[:, :T], cum[:, :T], AF.Exp, scale=-1.0)
                    yield
                    stt(qt[:, :T], pf[:, 0:T], float(128 ** -0.5), e1[:, :T], ALU.mult, ALU.mult)
                    tt(ksf[:, :T], kk[:, :T], cum[:, :T], ALU.mult)
                    yield
                    if T == 128:
                        qd = bass.AP(qtm.tensor, qtm.offset, [list(qtm.ap[0]), [160, 4], [1, 32]])
                        cp(qd, qt[:, 0:128].rearrange("p (a b) -> p a b", a=4))
                    else:
                        cp(qtm[:, 0, 0:T], qt[:, 0:T])
                    cp(kt[:, :T], ksf[:, :T])
                    for (s0, tsz) in subs:
                        ts(khT[:, s0:s0 + tsz], ksf[:, s0:s0 + tsz], e1[:, s0 + tsz - 1:s0 + tsz], None, ALU.mult)
                    yield
                    psc = bank(L.bk)
                    mm(psc[:T, 0:T], kt[:, :T], qt[:, :T])
                    ptk = bank(L.bk, BF16)[:, 512:1024]
                    tr(ptk[:T, 0:128], khT[:, :T], identb)
                    yield
                    tt(scm[:T, :T], psc[:T, 0:T], bdm[:T, :T], ALU.mult)
                    for j, (s0, tsz) in enumerate(subs):
                        if T == 128:
                            act(khm[:T, j, :], ptk[:T, 0:128], AF.Identity, scale=rowm[:T, j:j + 1])
                        else:
                            cp(khm[:T, j, :], ptk[:T, 0:128], eng="act_copy")
                    yield
                    po = bank(L.bk + 2)
                    mm(po[:T, 0:128], scm[:T, :T], vtok[:T, :], True, first)
                    for j, (s0, tsz) in enumerate(subs):
                        if not first:
                            mm(po[:T, 0:128], qtm[:, j, :T], Sbf[:, :], False, j == len(subs) - 1)
                        pS = bank(L.bk + 3)
                        mm(pS[:, 0:128], khm[:T, j, :], vtok[:T, :])
                        yield
                        if first:
                            cp(S[:, :], pS[:, 0:128])
                        else:
                            stt(S[:, :], S[:, :], e1[:, s0 + tsz - 1:s0 + tsz], pS[:, 0:128], ALU.mult, ALU.add)
                        yield
                        cp(Sbf[:, :], S[:, :])
                        yield
                    yield from gate_finish(L, po[:T, 0:128], pz, T, col0)
                P.dma("sp", [(dr["nhgrn_p"][l, h], S[:, :])])
                T = NS
                col0 = NTP
                pf, pz = proj(L, col0, T)
                yield
                fk(L, pf, T)
                act(L.qs[:, :], pf[:, 0:T], AF.Copy, scale=float(128 ** -0.5))
                cp(L.itk[:T, :], pz[:T, 0:128], eng="act_copy")
                gate_early(L, pz, T)
                yield
                tt(L.qmask[:, :, :], L.qs[:, :].unsqueeze(1).to_broadcast([128, 16, 16]), eye16[:, :, :], ALU.mult)
                po = bank(L.bk + 2)
                for b in range(NS):
                    st, stn = L.st[b % 2], L.stn[b % 2]
                    P.dma("sp", [(st[:, :], dr["state_hgrn"][l, b, h])])
                    ts(L.itm[:T, :], L.itk[:T, :], ident[:T, b:b + 1], None, ALU.mult)
                    yield
                    pb_ = bank(L.bk + (0 if b % 2 else 3))
                    mm(pb_[:, 0:128], onesb[:T, :], L.itm[:T, :])
                    yield
                    ts(L.tm2[:, :], pb_[:, 0:128], kk[:, b:b + 1], None, ALU.mult)
                    stt(stn[:, :], st[:, :], fv[:, b:b + 1], L.tm2[:, :], ALU.mult, ALU.add)
                    yield
                    mm(po[:T, 0:128], L.qmask[:, b, :], stn[:, :], b == 0, b == NS - 1)
                    P.dma("pool", [(dr["nhgrn_s"][l, b, h], stn[:, :])])
                    yield
                yield from gate_finish(L, po[:T, 0:128], pz, T, col0)

            lanes = [mklane(h, 4 * i) for i, h in enumerate(hs)]

            def hskew(i, L):
                for _ in range(0 * i):
                    yield
                yield from head_gen(L)
            run_lanes([hskew(i, L) for i, L in enumerate(lanes)])
            P.release()

        def ret_unit(l, h):
            P.phase(f"ret{l}.{h}")
            P.mark()
            gam = GAMMA[h]
            slab = A("slab", BF16, [KC, 1536])
            W = wv(l)
            P.dma("pool", [(slab[:, :, 0:256], W[:, :, O_RQ + h * 256:O_RQ + (h + 1) * 256]),
                           (slab[:, :, 256:512], W[:, :, O_RK + h * 256:O_RK + (h + 1) * 256])])
            P.dma("pool", [(slab[:, :, 512:1024], W[:, :, O_RV + h * 512:O_RV + (h + 1) * 512])])
            P.dma("pool", [(slab[:, :, 1024:1536], W[:, :, O_RG + h * 512:O_RG + (h + 1) * 512])])
            dmk = A("dmk", F32, [128]); g1 = A("g1", F32, [128]); kd = A("kd", F32, [2])
            P.dma("sp", [(dmk, dr["c_dmask"][h]), (g1, dr["c_g1"][h]), (kd, dr["c_kd"][h])])
            S = A("S", F32, [2, 512]); Sbf = A("Sbf", BF16, [2, 512])

            class Lane:
                pass

            def mklane(bk):
                L = Lane()
                L.bk = bk
                L.cs = A("cs", F32, [2, 128]); L.t12 = A("t12", F32, [4, 128])
                L.qkr = A("qkr", BF16, [4, 128]); L.qg = A("qg", BF16, [2, 128]); L.vtok = A("vtok", BF16, [512])
                L.scm = A("scm", BF16, [128]); L.kdec = A("kdec", BF16, [256])
                L.gs = A("gs", F32, [512]); L.ynb = A("ynb", BF16, [512]); L.sc = A("sc", F32, [4])
                return L

            def proj(L, col0, T):
                pf = bank(L.bk)
                for j in range(4):
                    for kc in range(KC):
                        mm(pf[:, j * 128:j * 128 + T], slab[:, kc, j * 128:(j + 1) * 128], xbf[:, kc, col0:col0 + T],
                           kc == 0, kc == KC - 1)
                pv = bank(L.bk + 1)
                for kc in range(KC):
                    mm(pv[:T, :], xbf[:, kc, col0:col0 + T], slab[:, kc, 512:1024], kc == 0, kc == KC - 1)
                pg = bank(L.bk + 2)
                for kc in range(KC):
                    mm(pg[:T, :], xbf[:, kc, col0:col0 + T], slab[:, kc, 1024:1536], kc == 0, kc == KC - 1)
                return pf, pv, pg

            def rotary(L, pf, col0, T, outb):
                cs, t1, t2 = L.cs, L.t12[:, 0:2, :], L.t12[:, 2:4, :]
                P.dma("sp", [(cs[:, 0, :T], dr["c_cos"][:, col0:col0 + T]), (cs[:, 1, :T], dr["c_sin"][:, col0:col0 + T])])
                p4 = pf[:, :].rearrange("p (a b) -> p a b", a=4)
                x1, x2 = p4[:, 0:4:2, :T], p4[:, 1:4:2, :T]
                cosb = cs[:, 0, :T].unsqueeze(1).to_broadcast([128, 2, T])
                sinb = cs[:, 1, :T].unsqueeze(1).to_broadcast([128, 2, T])
                tt(t1[:, :, :T], x1, cosb, ALU.mult)
                tt(t2[:, :, :T], x2, sinb, ALU.mult)
                yield
                tt(outb[:, 0:4:2, :T], t1[:, :, :T], t2[:, :, :T], ALU.subtract)
                tt(t1[:, :, :T], x2, cosb, ALU.mult)
                tt(t2[:, :, :T], x1, sinb, ALU.mult)
                yield
                tt(outb[:, 1:4:2, :T], t1[:, :, :T], t2[:, :, :T], ALU.add)

            def gate_early(L, pg, T):
                sigm(L.gs[:T, :], pg[:T, :])
                tt(L.gs[:T, :], L.gs[:T, :], pg[:T, :], ALU.mult)

            def gate_finish(L, po, pg, T, col0, pt):
                yield from finish(po, T, 512, L.gs[:T, :], [(h % 2) * 4 + j for j in range(4)], col0, L.ynb, L.ynb, L.sc, pt)

            def chunk_gen(L, ci, col0, T):
                first = ci == 0
                qkr, vtok, scm, kdec, qg = L.qkr, L.vtok, L.scm, L.kdec, L.qg
                pf, pv, pg = proj(L, col0, T)
                yield
                cp(vtok[:T, :], pv[:T, :], eng="act_copy")
                yield from rotary(L, pf, col0, T, qkr)
                yield
                gate_early(L, pg, T)
                psc = bank(L.bk)
                for j in range(2):
                    mm(psc[:T, 0:T], qkr[:, 2 + j, :T], qkr[:, j, :T], j == 0, j == 1)
                ptk = bank(L.bk, BF16)[:, 512:1024]
                for j in range(2):
                    tr(ptk[:T, j * 128:(j + 1) * 128], qkr[:, 2 + j, :T], identb)
                if not first:
                    tt(qg[:, :, :T], qkr[:, 0:2, :T], g1[:, :T].unsqueeze(1).to_broadcast([128, 2, T]), ALU.mult)
                yield
                tt(scm[:T, :T], psc[:T, 0:T], dmk[:T, :T], ALU.mult)
                act(kdec[:T, :], ptk[:T, 0:256], AF.Identity, scale=kd[:T, (0 if T == 128 else 1):(1 if T == 128 else 2)])
                yield
                po = bank(L.bk + 3)
                mm(po[:T, :], scm[:T, :T], vtok[:T, :], True, first)
                if not first:
                    for j in range(2):
                        mm(po[:T, :], qg[:, j, :T], Sbf[:, j, :], False, j == 1)
                pS = [bank(L.bk + 1), bank(L.bk)]
                for j in range(2):
                    mm(pS[j][:, :], kdec[:T, j * 128:(j + 1) * 128], vtok[:T, :])
                yield
                for j in range(2):
                    if first:
                        cp(S[:, j, :], pS[j][:, :])
                    else:
                        stt(S[:, j, :], S[:, j, :], float(gam ** T), pS[j][:, :], ALU.mult, ALU.add)
                    cp(Sbf[:, j, :], S[:, j, :], eng="act_copy")
                yield
                yield from gate_finish(L, po[:T, :], pg, T, col0, bank(L.bk + 1, BF16))
                if ci == len(pchunks) - 1:
                    P.dma("sp", [(dr["nret_p"][l, h].rearrange("(j p) v -> p j v", p=128), S[:, :, :])])

            P.mark()
            lanes = [mklane(0), mklane(4)]

            def lane_gen(li):
                if li == 1:
                    for _ in range(6):
                        yield
                for ci in range(li, len(pchunks), 2):
                    col0, T = pchunks[ci]
                    yield from chunk_gen(lanes[li], ci, col0, T)
            run_lanes([lane_gen(0), lane_gen(1)])
            P.release()

            T = NS
            col0 = NTP
            L = mklane(0)
            pf, pv, pg = proj(L, col0, T)
            qkf = A("qkf", F32, [4, 16])
            for _ in rotary(L, pf, col0, T, qkf):
                pass
            ts(qkf[:, 2:4, :], qkf[:, 2:4, :], 1.0 / 16.0, None, ALU.mult)
            qmask = A("qmask", F32, [2, 16, 16]); vtk = L.gs; vtm = L.ynb
            sts = [S, A("st1", F32, [2, 512])]
            for j in range(2):
                tt(qmask[:, j, :, :], qkf[:, j, :].unsqueeze(1).to_broadcast([128, 16, 16]), eye16[:, :, :], ALU.mult)
            cp(vtk[:T, :], pv[:T, :], eng="act_copy")
            po = bank(3)
            vtms = [vtm, A("vtm1", BF16, [512])]

            def samp_gen(par):
                st, vtm_ = sts[par], vtms[par]
                for b in range(par, NS, 2):
                    sview = lambda nm: dr[nm][l, b, h].rearrange("(j p) v -> p j v", p=128)
                    P.dma("sp", [(st[:, :, :], sview("state_ret"))])
                    ts(vtm_[:T, :], vtk[:T, :], ident[:T, b:b + 1], None, ALU.mult)
                    yield
                    pb_ = bank(6 + par)
                    mm(pb_[:, :], onesb[:T, :], vtm_[:T, :])
                    for j in range(2):
                        act(st[:, j, :], st[:, j, :], AF.Copy, scale=float(gam))
                    yield
                    for j in range(2):
                        stt(st[:, j, :], pb_[:, :], qkf[:, 2 + j, b:b + 1], st[:, j, :], ALU.mult, ALU.add)
                        yield
                    for j in range(2):
                        mm(po[:T, :], qmask[:, j, b, :], st[:, j, :], b == 0 and j == 0, b == NS - 1 and j == 1)
                    P.dma("pool", [(sview("nret_s"), st[:, :, :])])
                    yield
            run_lanes([samp_gen(0), samp_gen(1)])
            gate_early(L, pg, T)
            for _ in gate_finish(L, po[:T, :], pg, T, col0, bank(1, BF16)):
                pass
            P.release()

        def branch_pass(l, wbr, gcol, first):
            P.phase(f"bp{l}")
            W = wv(l)
            wo_v = dr["w_out"][l].rearrange("(k p) c -> p k c", p=128)
            P.mark()
            gT = A("gT", BF16, [KC, NT])
            wgs = [A(f"bg{i}", BF16, [KC, 128]) for i in range(2)]
            wbs = [A(f"bb{i}", BF16, [KC, 128]) for i in range(2)]
            wos = [A(f"bo{i}", BF16, [KC, 128]) for i in range(2)]
            sgs = [A(f"bs{i}", F32, [512]) for i in range(2)]
            it = 0
            for dc in range(KC):
                wg, wb = wgs[dc % 2], wbs[dc % 2]
                P.dma("pool", [(wg, W[:, :, gcol + dc * 128:gcol + (dc + 1) * 128]),
                               (wb, wbr[:, :, dc * 128:(dc + 1) * 128])])
                if dc == KC - 2:
                    for do in range(2):
                        P.dma("pool", [(wos[do], wo_v[:, :, do * 128:(do + 1) * 128])])
                for (c0, n) in tblocks:
                    pg, pb_, sg = bank(it % 2), bank(2 + it % 2), sgs[it % 2]
                    it += 1
                    for kc in range(KC):
                        mm(pg[:, :n], wg[:, kc, :], xbf[:, kc, c0:c0 + n], kc == 0, kc == KC - 1)
                    for kc in range(KC):
                        mm(pb_[:, :n], wb[:, kc, :], yT[:, kc, c0:c0 + n], kc == 0, kc == KC - 1)
                    act(sg[:, :n], pg[:, :n], AF.Sigmoid)
                    tt(gT[:, dc, c0:c0 + n], sg[:, :n], pb_[:, :n], ALU.mult)
            for do in range(KC):
                wo = wos[do % 2]
                if do >= 2:
                    P.dma("pool", [(wo, wo_v[:, :, do * 128:(do + 1) * 128])])
                for (c0, n) in tblocks:
                    po = bank(4 + it % 4)
                    it += 1
                    for kc in range(KC):
                        mm(po[:, :n], wo[:, kc, :], gT[:, kc, c0:c0 + n], kc == 0, kc == KC - 1)
                    if first:
                        stt(xT[:, do, c0:c0 + n], xT[:, do, c0:c0 + n], ALPHA, po[:, :n], ALU.mult, ALU.add)
                    else:
                        tt(xT[:, do, c0:c0 + n], xT[:, do, c0:c0 + n], po[:, :n], ALU.add)
            P.release()

        for l in range(layers):
            for g in range(4):
                ssd_unit(l, g)
            branch_pass(l, dr["w_br_m"][l].rearrange("(k p) c -> p k c", p=128), O_GATE, True)
            for h in range(0, 8, 2):
                hgrn_pair(l, [h, h + 1])
            branch_pass(l, dr["w_br_h"][l].rearrange("(k p) c -> p k c", p=128), O_GATE + 1024, False)
            for hh in range(2):
                for h in range(2 * hh, 2 * hh + 2):
                    ret_unit(l, h)
                branch_pass(l, dr["w_br_r"][l, hh * 1024:(hh + 1) * 1024].rearrange("(k p) c -> p k c", p=128),
                            O_GATE + 2048, False)
            layer_norm_fm(2 + 4 * l + 0, 2 + 4 * l + 1)
            ffn(l)
            layer_norm_fm(2 + 4 * l + 2, 2 + 4 * l + 3, last=(l == layers - 1))

        P.phase("out")
        P.mark()
        ob = [A(f"ob{i}", F32, [D]) for i in range(2)]
        for i, (col0, T, src) in enumerate(ttiles):
            if i == 0:
                continue
            o = ob[i % 2]
            pb = 2 * (i % 2)
            for kc in range(KC):
                tr(bank(pb + kc // 4)[:T, (kc % 4) * 128:(kc % 4 + 1) * 128], xT[:, kc, col0:col0 + T], ident)
            cp(o[:T, 0:512], bank(pb)[:T, :], eng="act_copy")
            cp(o[:T, 512:1024], bank(pb + 1)[:T, :])
            dst = dr["y_sample"][:, :] if i == len(ttiles) - 1 else dr["y_prompt"][128 * (i - 1):128 * i, :]
            P.dma("sp", [(dst, o[:T])])
        P.release()
        P.final_wait("sp")

        import contextlib
        with contextlib.ExitStack() as es:
            for e in ("pe", "act", "dve", "pool"):
                P.sems[e] = es.enter_context(nc.semaphore(f"s_{e}"))
            for s in range(P.n_dma_sems):
                P.sems[("dma", s)] = es.enter_context(nc.semaphore(f"s_dma{s}"))
            P.sems["sp"] = es.enter_context(nc.semaphore("s_sp"))
            block = es.enter_context(nc.Block())
            P.lower(block)
    return nc, P


def consts(NCH):
    SEQ = 128 * NCH
    NT = NMETA + SEQ + NS
    c = {}
    r = np.arange(128)
    c["c_ident"] = np.eye(128, dtype=np.float32)
    c["c_ones"] = np.ones((128, 128), np.float32)
    c["c_triu"] = (r[:, None] <= r[None, :]).astype(np.float32)
    c["c_strl"] = (r[:, None] > r[None, :]).astype(np.float32)
    c["c_bd"] = ((r[:, None] <= r[None, :]) & (r[:, None] // 32 == r[None, :] // 32)).astype(np.float32)
    c["c_rowm"] = (r[:, None] // 32 == np.arange(4)[None, :]).astype(np.float32)
    c["c_eye16"] = np.tile(np.eye(16, dtype=np.float32).reshape(1, 256), (128, 1))
    dm = np.zeros((4, 128, 128), np.float32); g1 = np.zeros((4, 128, 128), np.float32); kd = np.zeros((4, 128, 2), np.float32)
    for h in range(4):
        lg = np.log(np.float32(GAMMA[h])).astype(np.float32)
        diff = (r[None, :] - r[:, None]).astype(np.float32)
        dm[h] = np.where(r[:, None] <= r[None, :], np.exp(diff * lg), 0.0) / 16.0
        g1[h] = np.exp((r[None, :] + 1.0) * lg) * np.ones((128, 1), np.float32)
        kd[h, :, 0] = np.exp((127.0 - r) * lg) / 16.0
        kd[h, :, 1] = np.exp((15.0 - r) * lg) / 16.0
    c["c_dmask"], c["c_g1"], c["c_kd"] = dm, g1, kd.astype(np.float32)
    pos = np.concatenate([np.arange(NMETA + SEQ, dtype=np.float32), np.full(NS, 16384.0, np.float32)])
    inv = (1.0 / (np.float32(10000.0) ** np.linspace(0.0, 1.0, 128, dtype=np.float32))).astype(np.float32)
    ang = (inv[:, None] * pos[None, :]).astype(np.float32)
    c["c_cos"] = np.cos(ang.astype(np.float64)).astype(np.float32)
    c["c_sin"] = np.sin(ang.astype(np.float64)).astype(np.float32)
    return c


_CACHE = {}


def kernel(**inputs):
    NCH = inputs["x_prompt"].shape[1] // 128
    NB = inputs["x_prompt"].shape[0]
    if NCH not in _CACHE:
        _CACHE[NCH] = build(NCH)[0]
    nc = _CACHE[NCH]
    cs = consts(NCH)
    f = lambda a: np.ascontiguousarray(np.asarray(a, dtype=np.float32))
    shared = {k: f(inputs[k]) for k in ["meta_tokens", "ln_in_g", "ln_in_b", "w_in", "conv_w", "conv_b", "dt_bias", "a_log",
                                        "d_skip", "m_norm_w", "hgrn_lb_logits", "h_norm_w", "w_br_m", "w_br_h", "w_br_r",
                                        "w_out", "w_ffn_in", "w_ffn_out", "ln1_g", "ln1_b", "ln2_g", "ln2_b"]}
    shared.update(cs)
    in_maps = []
    for c in range(NB):
        m = dict(shared)
        m["x_prompt"] = f(inputs["x_prompt"][c])
        m["x_sample"] = f(inputs["x_sample"][c * NS:(c + 1) * NS, 0])
        for nm in ["state_ssm", "state_conv", "state_hgrn", "state_ret"]:
            m[nm] = f(inputs[nm][:, c * NS:(c + 1) * NS])
        in_maps.append(m)
    res = run_bass_kernel_spmd(nc, in_maps, core_ids=list(range(NB))).results
    y_prompt = np.stack([r["y_prompt"] for r in res], 0)
    y_sample = np.concatenate([r["y_sample"] for r in res], 0)[:, None, :]
    outs = [y_prompt, y_sample]
    for nm in ["nssm_p", "nconv_p", "nhgrn_p", "nret_p"]:
        outs.append(np.stack([r[nm] for r in res], 1))
    for nm in ["nssm_s", "nconv_s", "nhgrn_s", "nret_s"]:
        outs.append(np.concatenate([r[nm] for r in res], 1))
    return tuple(np.ascontiguousarray(o, dtype=np.float32) for o in outs)
```

```python
import numpy as np
import concourse.bass as bass
import concourse.mybir as mybir
from concourse.bass_utils import run_bass_kernel_spmd

F32 = mybir.dt.float32
BF16 = mybir.dt.bfloat16
AF = mybir.ActivationFunctionType
ALU = mybir.AluOpType
AX = mybir.AxisListType

D = 1024
KC = 8
DEPTH = 2
NS = 16
NMETA = 16
IN_DIM = 16400
DFF = 2816
ALPHA = float((2 * DEPTH) ** 0.25)
EPS_LN = 1e-5
DSIZE = {F32: 4, BF16: 2}

ENGS = ["pe", "act", "dve", "pool", "sp"]


class Buf:
    def __init__(self, name, off, nbytes, space):
        self.name, self.off, self.nbytes, self.space = name, off, nbytes, space
        self.recs = []


class Prog:
    def __init__(self, nc, sb_bytes, n_dma_sems=24):
        self.nc = nc
        self.sb_bytes = sb_bytes
        self.q = {e: [] for e in ENGS}
        self.cnt = {e: 0 for e in ENGS}
        self.waited = {e: {} for e in ENGS}
        self.sems = {}
        self.dma_tot = [0] * n_dma_sems
        self.n_dma_sems = n_dma_sems
        self.dma_rr = 0
        self.bufs = []
        self.freed = []
        self.top = 0
        self.stack = []
        self.out_events = []
        self.phases = []

    def phase(self, name):
        self.phases.append((name, dict(self.cnt)))

    def alloc(self, name, nbytes):
        nbytes = (nbytes + 63) // 64 * 64
        b = Buf(name, self.top, nbytes, "sb")
        self.top += nbytes
        assert self.top <= self.sb_bytes, (name, self.top, self.sb_bytes)
        ev = {}
        keep = []
        for (o, n, evs) in self.freed:
            if o < b.off + nbytes and b.off < o + n:
                for k, v in evs.items():
                    ev[k] = max(ev.get(k, 0), v)
                if not (b.off <= o and o + n <= b.off + nbytes):
                    keep.append((o, n, evs))
            else:
                keep.append((o, n, evs))
        self.freed = keep
        for k, v in ev.items():
            b.recs.append([0, 128, b.off, b.off + nbytes, "W", "seed", k, v])
        self.bufs.append(b)
        return b

    def mark(self):
        self.stack.append((self.top, len(self.bufs)))

    def release(self):
        top, nb = self.stack.pop()
        for b in self.bufs[nb:]:
            evs = {}
            for r in b.recs:
                evs[r[6]] = max(evs.get(r[6], 0), r[7])
            if evs:
                self.freed.append((b.off, b.nbytes, evs))
        self.bufs = self.bufs[:nb]
        self.top = top

    def view(self, buf, dtype, shape, boff=0):
        es = DSIZE[dtype]
        n = int(np.prod(shape))
        assert boff + n * es <= buf.nbytes, (buf.name, boff, n * es, buf.nbytes)
        if buf.space == "sb":
            base = self.sb
            start = (buf.off + boff) // 4
            words = (n * es + 3) // 4
            ap = base[:, start:start + words]
        else:
            base = self.ps
            start = (buf.off + boff) // 4
            words = (n * es + 3) // 4
            ap = base[:, start:start + words]
        if dtype != F32:
            ap = ap.bitcast(dtype)
            ap = ap[:, 0:n]
        if len(shape) == 2:
            ap = ap.rearrange("p (a b) -> p a b", a=shape[0])
        elif len(shape) == 3:
            ap = ap.rearrange("p (a b c) -> p a b c", a=shape[0], b=shape[1])
        return ap

    def _range(self, ap):
        t = ap.tensor
        es = DSIZE[ap.dtype]
        aps = ap.ap
        pstep = aps[0][0]
        off = int(ap.offset)
        if pstep == 0:
            pstep = self.pitch_elems[(t.name, es)]
        plo = off // pstep
        phi = plo + aps[0][1]
        flo = off % pstep
        fhi = flo + 1
        for (s, c) in aps[1:]:
            fhi += (c - 1) * abs(s)
        return plo, phi, flo * es, fhi * es

    def _find(self, ap):
        name = ap.tensor.name
        plo, phi, blo, bhi = self._range(ap)
        if name == "PS":
            b0 = blo // 2048
            b1 = (bhi - 1) // 2048
            return [(self.psbanks[b], 0, 128, b * 2048, (b + 1) * 2048) for b in range(b0, b1 + 1)]
        if name != "SB":
            return []
        res = []
        for b in self.bufs:
            if b.off < bhi and blo < b.off + b.nbytes:
                res.append((b, plo, phi, max(blo, b.off), min(bhi, b.off + b.nbytes)))
        assert res, ("no buf for ap", name, blo, bhi)
        return res

    def _deps(self, eng, accesses, is_dma):
        need = {}
        touched = []
        for ap, kind in accesses:
            for (b, plo, phi, blo, bhi) in self._find(ap):
                ps = b.space == "ps"
                for r in b.recs:
                    if r[0] < phi and plo < r[1] and r[2] < bhi and blo < r[3]:
                        if not (ps or r[4] == "W" or kind == "W"):
                            continue
                        if r[5] == eng and not is_dma:
                            if eng == "pe":
                                continue
                            if r[4] == "R" and not ps:
                                continue
                            if r[4] == "R" and ps and kind == "R":
                                continue
                        need[r[6]] = max(need.get(r[6], 0), r[7])
                touched.append((b, plo, phi, blo, bhi, kind))
        return need, touched

    def _record(self, touched, eng, key, val):
        for (b, plo, phi, blo, bhi, kind) in touched:
            if kind == "W":
                b.recs = [r for r in b.recs
                          if not (plo <= r[0] and r[1] <= phi and blo <= r[2] and r[3] <= bhi)]
                b.recs.append([plo, phi, blo, bhi, "W", eng, key, val])
            else:
                for r in b.recs:
                    if r[4] == "R" and r[5] == eng and r[6] == key and r[0] == plo and r[1] == phi \
                            and r[2] == blo and r[3] == bhi:
                        r[7] = val
                        break
                else:
                    b.recs.append([plo, phi, blo, bhi, "R", eng, key, val])

    def _emit_waits(self, eng, need):
        w = self.waited[eng]
        for k, v in need.items():
            if w.get(k, 0) >= v:
                continue
            w[k] = v
            self.q[eng].append(("wait", k, v))

    def I(self, eng, meth, reads=(), writes=(), **kw):
        acc = []
        for k, v in kw.items():
            if isinstance(v, bass.AP):
                if v.tensor.name not in ("SB", "PS"):
                    continue
                acc.append((v, "W" if k in ("out", "accum_out", "ap") else "R"))
        for a in reads:
            acc.append((a, "R"))
        for a in writes:
            acc.append((a, "W"))
        need, touched = self._deps(eng, acc, False)
        self._emit_waits(eng, need)
        self.cnt[eng] += 1
        self._record(touched, eng, eng, self.cnt[eng])
        self.q[eng].append(("ins", meth, kw))

    def dma(self, eng, pairs, slow=False):
        acc = []
        for (o, i) in pairs:
            if o.tensor.name == "SB":
                acc.append((o, "W"))
            if i.tensor.name == "SB":
                acc.append((i, "R"))
        need, touched = self._deps(eng, acc, True)
        s = self.dma_rr
        self.dma_rr = (self.dma_rr + 1) % self.n_dma_sems
        key = ("dma", s)
        need[key] = max(need.get(key, 0), self.dma_tot[s])
        self._emit_waits(eng, need)
        self.dma_tot[s] += 16 * len(pairs)
        self._record(touched, "dma", key, self.dma_tot[s])
        self.q[eng].append(("dma", s, pairs, slow))
        return (key, self.dma_tot[s])

    def final_wait(self, eng="sp"):
        need = {("dma", s): self.dma_tot[s] for s in range(self.n_dma_sems) if self.dma_tot[s]}
        for e in ("pe", "act", "dve", "pool"):
            if self.cnt[e]:
                need[e] = self.cnt[e]
        self._emit_waits(eng, need)

    def lower(self, block):
        nc = self.nc
        engmap = {"pe": "tensor", "act": "scalar", "dve": "vector", "pool": "gpsimd", "sp": "sync"}

        def semof(k):
            return self.sems[k]

        def run(ename, engine):
            for it in self.q[ename]:
                if it[0] == "wait":
                    engine.wait_ge(semof(it[1]), it[2])
                elif it[0] == "ins":
                    getattr(engine, it[1])(**it[2]).then_inc(semof(ename), 1)
                else:
                    s = semof(("dma", it[1]))
                    for (o, i) in it[2]:
                        if it[3]:
                            engine.dma_start(out=o, in_=i, allow_slow_non_contiguous=True).then_inc(s, 16)
                        else:
                            engine.dma_start(out=o, in_=i).then_inc(s, 16)

        for ename in ENGS:
            if not self.q[ename]:
                continue
            dec = getattr(block, engmap[ename])

            def mk(en):
                def f(engine):
                    run(en, engine)
                return f
            dec(mk(ename))


O_MZ, O_MX, O_MB, O_MC, O_MDT = 0, 1024, 2048, 2560, 3072
O_HQ, O_HF, O_HI, O_HG = 3088, 4112, 5136, 6160
O_RQ, O_RK, O_RV, O_RG = 7184, 8208, 9232, 11280
O_GATE = 13328
GAMMA = [1.0 - 2.0 ** (-5.0 - h) for h in range(4)]


def build(NCH, layers=DEPTH):
    SEQ = 128 * NCH
    NTP = NMETA + SEQ
    NT = NTP + NS
    nc = bass.Bass("TRN2", target_bir_lowering=False, dynamic_dma_scratch_size=8192)
    dr = {}

    def din(name, shape, dt=F32):
        dr[name] = nc.dram_tensor(name, list(shape), dt, kind="ExternalInput").ap()

    def dout(name, shape, dt=F32):
        dr[name] = nc.dram_tensor(name, list(shape), dt, kind="ExternalOutput").ap()

    din("x_prompt", [SEQ, D]); din("x_sample", [NS, D]); din("meta_tokens", [NMETA, D])
    din("state_ssm", [DEPTH, NS, 16, 64, 128]); din("state_conv", [DEPTH, NS, 3, 2048])
    din("state_hgrn", [DEPTH, NS, 8, 128, 128]); din("state_ret", [DEPTH, NS, 4, 256, 512])
    din("ln_in_g", [D]); din("ln_in_b", [D])
    din("w_in", [DEPTH, D, IN_DIM]); din("conv_w", [DEPTH, 4, 2048]); din("conv_b", [DEPTH, 2048])
    din("dt_bias", [DEPTH, 16]); din("a_log", [DEPTH, 16]); din("d_skip", [DEPTH, 16])
    din("m_norm_w", [DEPTH, D]); din("hgrn_lb_logits", [DEPTH, D]); din("h_norm_w", [DEPTH, D])
    din("w_br_m", [DEPTH, D, D]); din("w_br_h", [DEPTH, D, D]); din("w_br_r", [DEPTH, 2 * D, D])
    din("w_out", [DEPTH, D, D])
    din("w_ffn_in", [DEPTH, D, 2 * DFF]); din("w_ffn_out", [DEPTH, DFF, D])
    din("ln1_g", [DEPTH, D]); din("ln1_b", [DEPTH, D]); din("ln2_g", [DEPTH, D]); din("ln2_b", [DEPTH, D])
    for nm in ["c_ident", "c_ones", "c_triu", "c_strl", "c_bd"]:
        din(nm, [128, 128])
    din("c_rowm", [128, 4]); din("c_eye16", [128, 256])
    din("c_dmask", [4, 128, 128]); din("c_g1", [4, 128, 128]); din("c_kd", [4, 128, 2])
    din("c_cos", [128, NT]); din("c_sin", [128, NT])
    dout("y_prompt", [SEQ, D]); dout("y_sample", [NS, D])
    dout("nssm_p", [DEPTH, 16, 64, 128]); dout("nconv_p", [DEPTH, 3, 2048])
    dout("nhgrn_p", [DEPTH, 8, 128, 128]); dout("nret_p", [DEPTH, 4, 256, 512])
    dout("nssm_s", [DEPTH, NS, 16, 64, 128]); dout("nconv_s", [DEPTH, NS, 3, 2048])
    dout("nhgrn_s", [DEPTH, NS, 8, 128, 128]); dout("nret_s", [DEPTH, NS, 4, 256, 512])

    SB_BYTES = 184 * 1024
    P = Prog(nc, SB_BYTES)
    P.pitch_elems = {}

    with (
        nc.sbuf_tensor("SB", [128, SB_BYTES // 4], F32) as SBt,
        nc.psum_tensor("PS", [128, 8 * 512], F32) as PSt,
    ):
        P.sb = SBt[:]
        P.ps = PSt[:]
        P.pitch_elems[("SB", 4)] = SBt[:].ap[0][0]
        P.pitch_elems[("SB", 2)] = SBt[:].ap[0][0] * 2
        P.pitch_elems[("PS", 4)] = PSt[:].ap[0][0]
        P.pitch_elems[("PS", 2)] = PSt[:].ap[0][0] * 2
        P.psbanks = [Buf(f"bank{b}", b * 2048, 2048, "ps") for b in range(8)]

        def bank(b, dtype=F32, shape=None):
            if shape is None:
                shape = [512] if dtype == F32 else [1024]
            return P.view(P.psbanks[b], dtype, shape)

        def A(name, dtype, shape):
            b = P.alloc(name, int(np.prod(shape)) * DSIZE[dtype])
            return P.view(b, dtype, list(shape))

        def mm(out, lhsT, rhs, start=True, stop=True):
            P.I("pe", "matmul", out=out, lhsT=lhsT, rhs=rhs, start=start, stop=stop)

        def tr(out, in_, idn):
            P.I("pe", "transpose", out=out, in_=in_, identity=idn)

        def act(out, in_, func, **kw):
            P.I("act", "activation", out=out, in_=in_, func=func, **kw)

        def tt(out, in0, in1, op, eng="dve"):
            P.I(eng, "tensor_tensor", out=out, in0=in0, in1=in1, op=op)

        def ts(out, in0, s1, s2, op0, op1=None, eng="dve"):
            if op1 is None:
                P.I(eng, "tensor_scalar", out=out, in0=in0, scalar1=s1, scalar2=None, op0=op0)
            else:
                P.I(eng, "tensor_scalar", out=out, in0=in0, scalar1=s1, scalar2=s2, op0=op0, op1=op1)

        def stt(out, in0, scalar, in1, op0, op1, **kw):
            P.I("dve", "scalar_tensor_tensor", out=out, in0=in0, scalar=scalar, in1=in1, op0=op0, op1=op1, **kw)

        def sigm(out, in_):
            act(out, in_, AF.Exp, scale=-1.0)
            act(out, out, AF.Ln, bias=1.0, scale=1.0)
            act(out, out, AF.Exp, scale=-1.0)

        def cp(out, in_, eng="dve"):
            if eng == "act_copy":
                P.I("act", "activation", out=out, in_=in_, func=AF.Copy)
            else:
                P.I(eng, "tensor_copy", out=out, in_=in_)

        xT = A("xT", F32, [KC, NT])
        xbf = A("xbf", BF16, [KC, NT])
        yT = A("yT", BF16, [KC, NT])
        ident = A("ident", F32, [128]); ones = A("ones", F32, [128]); triu = A("triu", F32, [128])
        strl = A("strl", F32, [128]); bdm = A("bdm", F32, [128]); rowm = A("rowm", F32, [4])
        eye16 = A("eye16", BF16, [16, 16]); identb = A("identb", BF16, [128]); onesb = A("onesb", BF16, [128])
        lnp = A("lnp", F32, [2 + 4 * DEPTH, KC])
        P.dma("sp", [(ident, dr["c_ident"]), (ones, dr["c_ones"]), (triu, dr["c_triu"]), (strl, dr["c_strl"]),
                     (bdm, dr["c_bd"]), (rowm, dr["c_rowm"])])
        P.dma("pool", [(eye16, dr["c_eye16"].rearrange("p (a b) -> p a b", a=16))])
        prs = [(lnp[:, 0, :], dr["ln_in_g"].rearrange("(k p) -> p k", p=128)),
               (lnp[:, 1, :], dr["ln_in_b"].rearrange("(k p) -> p k", p=128))]
        for l in range(DEPTH):
            for j, nm in enumerate(["ln1_g", "ln1_b", "ln2_g", "ln2_b"]):
                prs.append((lnp[:, 2 + 4 * l + j, :], dr[nm][l].rearrange("(k p) -> p k", p=128)))
        P.dma("sp", prs, slow=True)
        cp(identb, ident)
        cp(onesb, ones)

        ttiles = [(0, NMETA, dr["meta_tokens"][:, :])]
        for c in range(NCH):
            ttiles.append((NMETA + 128 * c, 128, dr["x_prompt"][128 * c:128 * (c + 1), :]))
        ttiles.append((NTP, NS, dr["x_sample"][:, :]))
        pchunks = [(c0_, T_) for (c0_, T_, _) in ttiles[:-1]]

        def mkblocks(lo, hi):
            res = []
            c0 = lo
            while c0 < hi:
                n = min(512, hi - c0)
                res.append((c0, n))
                c0 += n
            return res
        tblocks = mkblocks(0, NT)

        P.mark()
        tin = [A(f"tin{i}", F32, [D]) for i in range(2)]
        tnr = [A(f"tnr{i}", F32, [D]) for i in range(2)]
        lst = [A(f"lnst{i}", F32, [16]) for i in range(2)]
        for i, (col0, T, src) in enumerate(ttiles):
            a, nr = tin[i % 2], tnr[i % 2]
            stats = lst[i % 2][:, 0:12].rearrange("p (a b) -> p a b", a=2)
            mv = lst[i % 2][:, 12:16]
            P.dma("sp", [(a[:T], src)])
            P.I("dve", "bn_stats", out=stats[:T, 0, :], in_=a[:T, 0:512])
            P.I("dve", "bn_stats", out=stats[:T, 1, :], in_=a[:T, 512:1024])
            P.I("dve", "bn_aggr", out=mv[:T, 0:2], in_=lst[i % 2][:T, 0:12])
            act(mv[:T, 2:3], mv[:T, 1:2], AF.Ln, bias=EPS_LN, scale=1.0)
            act(mv[:T, 3:4], mv[:T, 2:3], AF.Exp, scale=-0.5)
            ts(nr[:T], a[:T], mv[:T, 0:1], mv[:T, 3:4], ALU.subtract, ALU.mult)
            pb = 2 * (i % 2)
            for kc in range(KC):
                tr(bank(pb + kc // 4)[:, (kc % 4) * 128:(kc % 4) * 128 + T], nr[:T, kc * 128:(kc + 1) * 128], ident[:T, :T])
            for kc in range(KC):
                act(xT[:, kc, col0:col0 + T], bank(pb + kc // 4)[:, (kc % 4) * 128:(kc % 4) * 128 + T],
                    AF.Identity, scale=lnp[:, 0, kc:kc + 1], bias=lnp[:, 1, kc:kc + 1])
            cp(xbf[:, :, col0:col0 + T], xT[:, :, col0:col0 + T])
        P.release()

        def layer_norm_fm(gi, bi, last=False):
            P.phase(f"ln{gi}")
            P.mark()
            sq = A("lnsq", F32, [2, 512]); mmb = A("lnm", F32, [4, 512]); ttb = A("lnt", F32, [2, 512])
            for (c0, n) in tblocks:
                s1, s2 = bank(4), bank(5)
                for kc in range(KC):
                    mm(s1[:, :n], ones, xT[:, kc, c0:c0 + n], kc == 0, kc == KC - 1)
                for kc in range(KC):
                    act(sq[:, kc % 2, :n], xT[:, kc, c0:c0 + n], AF.Square)
                    mm(s2[:, :n], ones, sq[:, kc % 2, :n], kc == 0, kc == KC - 1)
                mean, msq, var, rstd = mmb[:, 0, :n], mmb[:, 1, :n], mmb[:, 2, :n], mmb[:, 3, :n]
                act(mean, s1[:, :n], AF.Copy, scale=1.0 / D)
                act(msq, s1[:, :n], AF.Square, scale=1.0 / D)
                stt(var, s2[:, :n], 1.0 / D, msq, ALU.mult, ALU.subtract)
                act(var, var, AF.Ln, bias=EPS_LN, scale=1.0)
                act(rstd, var, AF.Exp, scale=-0.5)
                for kc in range(KC):
                    t = ttb[:, kc % 2, :n]
                    tt(t, xT[:, kc, c0:c0 + n], mean, ALU.subtract)
                    tt(t, t, rstd, ALU.mult)
                    act(xT[:, kc, c0:c0 + n], t, AF.Identity, scale=lnp[:, gi, kc:kc + 1], bias=lnp[:, bi, kc:kc + 1])
                    if not last:
                        cp(xbf[:, kc, c0:c0 + n], xT[:, kc, c0:c0 + n], eng="pool")
            P.release()

        def ffn(l):
            P.phase(f"ffn{l}")
            P.mark()
            passes = [(0, 8), (8, 15), (15, 22)]
            hT = yT
            wgs = [A(f"wg{i}", BF16, [KC, 128]) for i in range(2)]
            wus = [A(f"wu{i}", BF16, [KC, 128]) for i in range(2)]
            wos = [A(f"wo{i}", BF16, [8, 128]) for i in range(2)]
            sgs = [A(f"sg{i}", F32, [512]) for i in range(2)]
            win = dr["w_ffn_in"][l].rearrange("(k p) c -> p k c", p=128)
            wout = dr["w_ffn_out"][l].rearrange("(j p) c -> p j c", p=128)
            it = 0
            for ps_, (j0, j1) in enumerate(passes):
                TPP = j1 - j0
                for j in range(TPP):
                    jt = j0 + j
                    wg, wu = wgs[jt % 2], wus[jt % 2]
                    P.dma("pool", [(wg, win[:, :, jt * 128:(jt + 1) * 128]),
                                   (wu, win[:, :, DFF + jt * 128:DFF + (jt + 1) * 128])])
                    for (c0, n) in tblocks:
                        pg, pu, sg = bank(it % 2), bank(2 + it % 2), sgs[it % 2]
                        it += 1
                        for kc in range(KC):
                            mm(pg[:, :n], wg[:, kc, :], xbf[:, kc, c0:c0 + n], kc == 0, kc == KC - 1)
                        for kc in range(KC):
                            mm(pu[:, :n], wu[:, kc, :], xbf[:, kc, c0:c0 + n], kc == 0, kc == KC - 1)
                        act(sg[:, :n], pg[:, :n], AF.Silu)
                        tt(hT[:, j, c0:c0 + n], sg[:, :n], pu[:, :n], ALU.mult)
                for do in range(KC):
                    wo = wos[do % 2]
                    P.dma("pool", [(wo[:, 0:TPP, :], wout[:, j0:j1, do * 128:(do + 1) * 128])])
                    for (c0, n) in tblocks:
                        po = bank(6 + it % 2)
                        it += 1
                        for j in range(TPP):
                            mm(po[:, :n], wo[:, j, :], hT[:, j, c0:c0 + n], j == 0, j == TPP - 1)
                        if ps_ == 0:
                            stt(xT[:, do, c0:c0 + n], xT[:, do, c0:c0 + n], ALPHA, po[:, :n], ALU.mult, ALU.add)
                        else:
                            tt(xT[:, do, c0:c0 + n], xT[:, do, c0:c0 + n], po[:, :n], ALU.add)
            P.release()

        def run_lanes(gens):
            gens = list(gens)
            while gens:
                for g_ in list(gens):
                    try:
                        next(g_)
                    except StopIteration:
                        gens.remove(g_)

        def finish(ysrc, T, F, gate, ychunks, col0, junk, tmp_bf, sc, pt):
            act(junk[:T, :F], ysrc, AF.Square, accum_out=sc[:T, 0:1])
            act(sc[:T, 1:2], sc[:T, 0:1], AF.Ln, scale=1.0 / F, bias=1e-6)
            act(sc[:T, 2:3], sc[:T, 1:2], AF.Exp, scale=-0.5)
            yield
            stt(tmp_bf[:T, :F], ysrc, sc[:T, 2:3], gate, ALU.mult, ALU.mult)
            yield
            for j in range(F // 128):
                tr(pt[:, j * 128:j * 128 + T], tmp_bf[:T, j * 128:(j + 1) * 128], identb[:T, :T])
            yield
            for j, yc in enumerate(ychunks):
                cp(yT[:, yc, col0:col0 + T], pt[:, j * 128:j * 128 + T], eng="dve" if j % 2 else "act_copy")
            yield

        wv = lambda l: dr["w_in"][l].rearrange("(k p) c -> p k c", p=128)

        def ssd_unit(l, g):
            P.phase(f"ssd{l}.{g}")
            P.mark()
            slab = A("slab", BF16, [KC, 772])
            W = wv(l)
            P.dma("pool", [(slab[:, :, 0:256], W[:, :, O_MX + g * 256:O_MX + (g + 1) * 256]),
                           (slab[:, :, 256:384], W[:, :, O_MB + g * 128:O_MB + (g + 1) * 128]),
                           (slab[:, :, 384:512], W[:, :, O_MC + g * 128:O_MC + (g + 1) * 128]),
                           (slab[:, :, 512:768], W[:, :, O_MZ + g * 256:O_MZ + (g + 1) * 256]),
                           (slab[:, :, 768:772], W[:, :, O_MDT + 4 * g:O_MDT + 4 * g + 4])])
            cw = A("cw", F32, [4, 4]); cb = A("cb", F32, [4])
            cbase = [g * 256, g * 256 + 128, 1024 + g * 128, 1536 + g * 128]
            prs = []
            for j in range(4):
                prs.append((cw[:, j, :], dr["conv_w"][l].rearrange("k c -> c k")[cbase[j]:cbase[j] + 128, :]))
                prs.append((cb[:, j:j + 1], dr["conv_b"][l].rearrange("(c o) -> c o", o=1)[cbase[j]:cbase[j] + 128, :]))
            P.dma("sp", prs, slow=True)
            hp = A("hp", F32, [3, 4])
            mw = A("mw", F32, [256])
            P.dma("sp", [(hp[:, 0, :], dr["dt_bias"][l, 4 * g:4 * g + 4].partition_broadcast(128)),
                         (hp[:, 1, :], dr["a_log"][l, 4 * g:4 * g + 4].partition_broadcast(128)),
                         (hp[:, 2, :], dr["d_skip"][l, 4 * g:4 * g + 4].partition_broadcast(128)),
                         (mw, dr["m_norm_w"][l, g * 256:(g + 1) * 256].partition_broadcast(128))])
            act(hp[:, 1, :], hp[:, 1, :], AF.Exp)
            ts(hp[:, 1, :], hp[:, 1, :], -1.0, None, ALU.mult)
            dsk_bc = hp[:, 2, :].unsqueeze(2).to_broadcast([128, 4, 64])
            S = A("S", F32, [256]); Sbf = A("Sbf", BF16, [256])
            xt3 = lambda a, T: a[:T, :].rearrange("p (h q) -> p h q", h=4)

            class Lane:
                pass

            def mklane(bk):
                L = Lane()
                L.bk = bk
                L.raw = A("raw", F32, [4, 131]); L.cv = A("cv", F32, [4, 128]); L.xcb = A("xcb", BF16, [4, 128])
                L.xtok = A("xtok", BF16, [256]); L.xdt = A("xdt", BF16, [256]); L.xw = A("xw", BF16, [256]); L.btok = A("btok", BF16, [128])
                L.dtt = A("dtt", F32, [16]); L.cc = A("cc", F32, [16])
                L.cbm = A("cbm", F32, [128]); L.Lh = A("Lh", F32, [4, 128]); L.dec = A("dec", F32, [4, 128]); L.wT = A("wT", BF16, [4, 128])
                L.yi = A("yi", F32, [256]); L.tmpf = A("tmpf", F32, [256]); L.zs = A("zs", F32, [256]); L.ynb = A("ynb", BF16, [256])
                L.sc = A("sc", F32, [4])
                return L

            def proj(L, col0, T):
                pf = bank(L.bk)
                for j in range(4):
                    for kc in range(KC):
                        mm(pf[:, j * 128:j * 128 + T], slab[:, kc, j * 128:(j + 1) * 128], xbf[:, kc, col0:col0 + T],
                           kc == 0, kc == KC - 1)
                pz = bank(L.bk + 1)
                for kc in range(KC):
                    mm(pz[:T, 0:260], xbf[:, kc, col0:col0 + T], slab[:, kc, 512:772], kc == 0, kc == KC - 1)
                return pf, pz

            def dt_calc(L, pz, T):
                dtt = L.dtt
                tt(dtt[:T, 8:12], pz[:T, 256:260], hp[:T, 0, :], ALU.add)
                act(dtt[:T, 8:12], dtt[:T, 8:12], AF.Exp)
                act(dtt[:T, 0:4], dtt[:T, 8:12], AF.Ln, bias=1.0, scale=1.0)
                tt(dtt[:T, 4:8], dtt[:T, 0:4], hp[:T, 1, :], ALU.mult)

            def conv_silu(L, T, taps):
                cv = L.cv
                for j in range(4):
                    act(cv[:, j, :T], taps(j, 0), AF.Identity, scale=cw[:, j, 0:1], bias=cb[:, j:j + 1])
                for k in range(1, 4):
                    for j in range(4):
                        stt(cv[:, j, :T], taps(j, k), cw[:, j, k:k + 1], cv[:, j, :T], ALU.mult, ALU.add)
                sigm(L.dec[:, :, :T], cv[:, :, :T])
                tt(L.xcb[:, :, :T], cv[:, :, :T], L.dec[:, :, :T], ALU.mult)

            def gate_finish(L, pz, T, col0):
                ysb = L.yi
                tt(xt3(L.tmpf, T), xt3(L.xtok, T), dsk_bc[:T], ALU.mult)
                tt(ysb[:T, :], ysb[:T, :], L.tmpf[:T, :], ALU.add)
                sigm(L.zs[:T, :], pz[:T, 0:256])
                tt(ysb[:T, :], ysb[:T, :], pz[:T, 0:256], ALU.mult)
                yield
                tt(ysb[:T, :], ysb[:T, :], L.zs[:T, :], ALU.mult)
                yield
                yield from finish(ysb[:T, :], T, 256, mw[:T, :], [2 * g, 2 * g + 1], col0, L.tmpf, L.ynb, L.sc,
                                  bank(L.bk + 3, BF16)[:, 512:1024])

            def chunk_gen(L, Lprev, ci, col0, T):
                first = ci == 0
                raw, xcb, dtt, cc = L.raw, L.xcb, L.dtt, L.cc
                if first:
                    P.I("dve", "memset", ap=raw[:, :, 0:3], constant=0.0)
                pf, pz = proj(L, col0, T)
                yield
                act(raw[:, :, 3:3 + T], pf[:, :].rearrange("p (a b) -> p a b", a=4)[:, :, :T], AF.Copy)
                dt_calc(L, pz, T)
                yield
                if ci == len(pchunks) - 1:
                    prs = [(dr["nconv_p"][l].rearrange("k c -> c k")[cbase[j]:cbase[j] + 128, :], raw[:, j, T:T + 3])
                           for j in range(4)]
                    P.dma("sp", prs, slow=True)
                if not first:
                    pass
                conv_silu(L, T, lambda j, k: raw[:, j, k:k + T])
                yield
                pc = bank(L.bk + 1)[:, 300:312]
                mm(pc[:T, 0:4], triu[:T, :T], dtt[:T, 4:8])
                mm(pc[:, 4:8], ones[:T, :], dtt[:T, 4:8])
                ptb = bank(L.bk, BF16)
                for j in range(3):
                    tr(ptb[:T, j * 128:(j + 1) * 128], xcb[:, j, :T], identb)
                yield
                if T == 128:
                    cp(cc[:, 0:8], pc[:, 0:8], eng="act_copy")
                else:
                    cp(cc[:T, 0:4], pc[:T, 0:4], eng="act_copy")
                    cp(cc[:, 4:8], pc[:, 4:8], eng="act_copy")
                cp(L.xtok[:T, :], ptb[:T, 0:256], eng="act_copy")
                cp(L.btok[:T, :], ptb[:T, 256:384], eng="act_copy")
                yield
                act(cc[:T, 8:12], cc[:T, 0:4], AF.Exp)
                act(cc[:, 12:16], cc[:, 4:8], AF.Exp)
                tt(dtt[:T, 8:12], cc[:T, 4:8], cc[:T, 0:4], ALU.subtract)
                tt(xt3(L.xdt, T), xt3(L.xtok, T), dtt[:T, 0:4].unsqueeze(2).to_broadcast([T, 4, 64]), ALU.mult)
                yield
                act(dtt[:T, 8:12], dtt[:T, 8:12], AF.Exp)
                pcb = bank(L.bk)
                mm(pcb[:T, :T], xcb[:, 2, :T], xcb[:, 3, :T])
                for h in range(4):
                    act(L.Lh[:T, h, :T], strl[:T, :T], AF.Identity, scale=dtt[:T, 4 + h:5 + h])
                yield
                tt(dtt[:T, 12:16], dtt[:T, 8:12], dtt[:T, 0:4], ALU.mult)
                tt(L.cbm[:T, :T], pcb[:T, :T], triu[:T, :T], ALU.mult)
                yield
                tt(xt3(L.xw, T), xt3(L.xtok, T), dtt[:T, 12:16].unsqueeze(2).to_broadcast([T, 4, 64]), ALU.mult)
                pd = bank(L.bk)
                for h in range(4):
                    mm(pd[:T, h * 128:h * 128 + T], L.Lh[:T, h, :T], triu[:T, :T])
                yield
                pyy = bank(L.bk + 2)
                if not first:
                    mm(pyy[:T, 256:512], xcb[:, 3, :T], Sbf[:, :])
                pS = bank(L.bk + 3)
                mm(pS[:, 0:256], L.btok[:T, :], L.xw[:T, :])
                act(L.dec[:T, :, :T], pd[:T, :].rearrange("p (a b) -> p a b", a=4)[:, :, :T], AF.Exp)
                yield
                if not first:
                    tt(xt3(L.tmpf, T), pyy[:T, 256:512].rearrange("p (h q) -> p h q", h=4),
                       cc[:T, 8:12].unsqueeze(2).to_broadcast([T, 4, 64]), ALU.mult)
                if first:
                    cp(S[:, :], pS[:, 0:256])
                else:
                    tt(S[:, :].rearrange("p (h q) -> p h q", h=4), S[:, :].rearrange("p (h q) -> p h q", h=4),
                       cc[:, 12:16].unsqueeze(2).to_broadcast([128, 4, 64]), ALU.mult)
                    tt(S[:, :], S[:, :], pS[:, 0:256], ALU.add)
                cp(Sbf[:, :], S[:, :])
                yield
                tt(L.wT[:T, :, :T], L.dec[:T, :, :T], L.cbm[:T, :T].unsqueeze(1).to_broadcast([T, 4, T]), ALU.mult)
                yield
                for h in range(4):
                    mm(pyy[:T, h * 64:(h + 1) * 64], L.wT[:T, h, :T], L.xdt[:T, h * 64:(h + 1) * 64])
                yield
                if first:
                    cp(L.yi[:T, :], pyy[:T, 0:256], eng="act_copy")
                else:
                    tt(L.yi[:T, :], pyy[:T, 0:256], L.tmpf[:T, :], ALU.add)
                yield
                yield from gate_finish(L, pz, T, col0)
                if ci == len(pchunks) - 1:
                    pt = bank(L.bk + 2)
                    for j in range(2):
                        tr(pt[:, j * 128:(j + 1) * 128], S[:, j * 128:(j + 1) * 128], ident)
                    cp(L.tmpf[:, :], pt[:, 0:256])
                    P.dma("sp", [(dr["nssm_p"][l, 4 * g:4 * g + 4].rearrange("h p n -> (h p) n").rearrange("(t q) n -> q t n", q=128),
                                  L.tmpf[:, :].rearrange("p (t n) -> p t n", t=2))])

            P.mark()
            lanes = [mklane(0), mklane(4)]

            def lane_gen(li):
                L = lanes[li]
                for ci in range(li, len(pchunks), 2):
                    col0, T = pchunks[ci]
                    if ci > 0:
                        Lp = lanes[1 - li]
                        Tp = pchunks[ci - 1][1]
                        cp(L.raw[:, :, 0:3], Lp.raw[:, :, Tp:Tp + 3], eng="pool")
                    yield from chunk_gen(L, None, ci, col0, T)
            def skewed(li):
                if li == 1:
                    for _ in range(14):
                        yield
                yield from lane_gen(li)
            run_lanes([skewed(0), skewed(1)])
            P.release()

            P.phase("sample")
            T = NS
            col0 = NTP
            L = mklane(0)
            raw, xcb, dtt = L.raw, L.xcb, L.dtt
            pf, pz = proj(L, col0, T)
            act(raw[:, :, 3:3 + T], pf[:, :].rearrange("p (a b) -> p a b", a=4)[:, :, :T], AF.Copy)
            pxb = bank(4)
            for kc in range(KC):
                mm(pxb[:T, :], xbf[:, kc, col0:col0 + T], slab[:, kc, 0:512], kc == 0, kc == KC - 1)
            cst = A("cst", F32, [3, 512]); rtk = A("rtk", F32, [512]); halo = A("halo", F32, [4, 3, 16])
            cp(rtk[:T, :], pxb[:T, :], eng="act_copy")
            P.dma("sp", [(cst[:T, :, j * 128:(j + 1) * 128], dr["state_conv"][l, :, :, cbase[j]:cbase[j] + 128]) for j in range(4)])
            P.dma("sp", [(dr["nconv_s"][l, :, 0:2, cbase[j]:cbase[j] + 128], cst[:T, 1:3, j * 128:(j + 1) * 128]) for j in range(4)] +
                        [(dr["nconv_s"][l, :, 2, cbase[j]:cbase[j] + 128], rtk[:T, j * 128:(j + 1) * 128]) for j in range(4)])
            ph = bank(5)
            for j in range(4):
                for k in range(3):
                    tr(ph[:, (j * 3 + k) * 16:(j * 3 + k + 1) * 16], cst[:T, k, j * 128:(j + 1) * 128], ident[:T, :T])
            cp(halo[:, :, :, :], ph[:, 0:192].rearrange("p (a b c) -> p a b c", a=4, b=3))
            conv_silu(L, T, lambda j, k: halo[:, j, k, :] if k < 3 else raw[:, j, 3:3 + T])
            dt_calc(L, pz, T)
            act(dtt[:T, 8:12], dtt[:T, 4:8], AF.Exp)
            ptb = bank(2, BF16)
            for j in range(4):
                tr(ptb[:T, j * 128:(j + 1) * 128], xcb[:, j, :T], identb)
            cp(L.xtok[:T, :], ptb[:T, 0:256], eng="act_copy")
            bct = A("bct", F32, [256]); bcm = A("bcm", BF16, [256])
            cp(bct[:T, :], ptb[:T, 256:512], eng="act_copy")
            xe = A("xe", F32, [512])
            tt(xe[:T, 0:256].rearrange("p (h q) -> p h q", h=4), xt3(L.xtok, T),
               dtt[:T, 0:4].unsqueeze(2).to_broadcast([T, 4, 64]), ALU.mult)
            cp(xe[:T, 256:512].rearrange("p (h q) -> p h q", h=4), dtt[:T, 8:12].unsqueeze(2).to_broadcast([T, 4, 64]))
            pq = bank(3)
            for j in range(4):
                tr(pq[:, j * 16:(j + 1) * 16], xe[:T, j * 128:(j + 1) * 128], ident[:T, :T])
            scs = A("scs", F32, [4, 16])
            cp(scs[:, :, :], pq[:, 0:64].rearrange("p (a b) -> p a b", a=4))
            sts = [A(f"st{i}", F32, [2, 128]) for i in range(2)]
            stns = [A(f"stn{i}", F32, [2, 128]) for i in range(2)]
            yfm = A("yfm", F32, [2, 16]); tm2s = [A(f"tm2{i}", F32, [128]) for i in range(2)]
            junks = [A(f"junk{i}", F32, [128]) for i in range(2)]
            bcms = [bcm, A("bcm1", BF16, [256])]

            def samp_gen(par):
                st, stn, tm2, junk, bcm_ = sts[par], stns[par], tm2s[par], junks[par], bcms[par]
                for b in range(par, NS, 2):
                    sview = lambda nm: dr[nm][l, b, 4 * g:4 * g + 4].rearrange("h p n -> (h p) n").rearrange("(t q) n -> q t n", q=128)
                    P.dma("sp", [(st[:, :, :], sview("state_ssm"))])
                    ts(bcm_[:T, :], bct[:T, :], ident[:T, b:b + 1], None, ALU.mult)
                    yield
                    pb_ = bank(6 + par)
                    mm(pb_[:, 0:256], onesb[:T, :], bcm_[:T, :])
                    yield
                    for j in range(2):
                        ts(tm2[:, :], pb_[:, 0:128], scs[:, j, b:b + 1], None, ALU.mult)
                        yield
                        stt(stn[:, j, :], st[:, j, :], scs[:, 2 + j, b:b + 1], tm2[:, :], ALU.mult, ALU.add)
                        yield
                        stt(junk[:, :], stn[:, j, :], 1.0, pb_[:, 128:256], ALU.mult, ALU.mult, accum_out=yfm[:, j, b:b + 1])
                        yield
                    P.dma("pool", [(sview("nssm_s"), stn[:, :, :])])
            run_lanes([samp_gen(0), samp_gen(1)])
            py = bank(4)
            for j in range(2):
                tr(py[:T, j * 128:(j + 1) * 128], yfm[:, j, :], ident)
            cp(L.yi[:T, :], py[:T, 0:256], eng="act_copy")
            for _ in gate_finish(L, pz, T, col0):
                pass
            P.release()

        def hgrn_pair(l, hs):
            P.phase(f"hgrn{l}.{hs[0]}")
            P.mark()
            W = wv(l)

            class Lane:
                pass

            def mklane(h, bk):
                L = Lane()
                L.h, L.bk = h, bk
                L.slab = A("slab", BF16, [KC, 512])
                P.dma("pool", [(L.slab[:, :, 0:128], W[:, :, O_HQ + h * 128:O_HQ + (h + 1) * 128]),
                               (L.slab[:, :, 128:256], W[:, :, O_HF + h * 128:O_HF + (h + 1) * 128]),
                               (L.slab[:, :, 256:384], W[:, :, O_HI + h * 128:O_HI + (h + 1) * 128]),
                               (L.slab[:, :, 384:512], W[:, :, O_HG + h * 128:O_HG + (h + 1) * 128])])
                lbp = L.lbp = A("lbp", F32, [4])
                L.hw = A("hw", F32, [128])
                P.dma("sp", [(lbp[:, 0:2], dr["hgrn_lb_logits"].rearrange("l c -> c l")[h * 128:(h + 1) * 128, :])], slow=True)
                P.dma("sp", [(L.hw, dr["h_norm_w"][l, h * 128:(h + 1) * 128].partition_broadcast(128))])
                if l == 0:
                    P.I("dve", "memset", ap=lbp[:, 2:3], constant=0.0)
                    P.I("dve", "memset", ap=lbp[:, 3:4], constant=1.0)
                else:
                    tt(lbp[:, 2:3], lbp[:, 0:1], lbp[:, 1:2], ALU.subtract)
                    act(lbp[:, 2:3], lbp[:, 2:3], AF.Exp)
                    ts(lbp[:, 2:3], lbp[:, 2:3], 1.0, None, ALU.add)
                    P.I("dve", "reciprocal", out=lbp[:, 2:3], in_=lbp[:, 2:3])
                    ts(lbp[:, 3:4], lbp[:, 2:3], -1.0, 1.0, ALU.mult, ALU.add)
                L.fv = A("fv", F32, [128]); L.kk = A("kk", F32, [128]); L.cum = A("cum", F32, [128]); L.e1 = A("e1", F32, [128])
                L.ksf = A("ksf", F32, [128]); L.qt = A("qt", BF16, [128]); L.qtm = A("qtm", BF16, [4, 128]); L.kt = A("kt", BF16, [128])
                L.khT = A("khT", BF16, [128]); L.khm = A("khm", BF16, [4, 128]); L.vtok = A("vtok", BF16, [128]); L.scm = A("scm", BF16, [128])
                L.gs = A("gs", F32, [128]); L.tmpf = A("tmpf", F32, [128]); L.ynb = A("ynb", BF16, [128]); L.sc = A("sc", F32, [4])
                L.S = A("S", F32, [128]); L.Sbf = A("Sbf", BF16, [128])
                L.qs = A("qs", F32, [16]); L.qmask = A("qmask", F32, [16, 16]); L.itk = A("itk", F32, [128]); L.itm = A("itm", BF16, [128])
                L.st = [A(f"st{i}", F32, [128]) for i in range(2)]; L.stn = [A(f"stn{i}", F32, [128]) for i in range(2)]
                L.tm2 = A("tm2", F32, [128])
                P.I("dve", "memset", ap=L.qtm[:, :, :], constant=0.0)
                return L

            def proj(L, col0, T):
                pf = bank(L.bk)
                for j in range(2):
                    for kc in range(KC):
                        mm(pf[:, j * 128:j * 128 + T], L.slab[:, kc, j * 128:(j + 1) * 128], xbf[:, kc, col0:col0 + T],
                           kc == 0, kc == KC - 1)
                pz = bank(L.bk + 1)
                for kc in range(KC):
                    mm(pz[:T, 0:256], xbf[:, kc, col0:col0 + T], L.slab[:, kc, 256:512], kc == 0, kc == KC - 1)
                return pf, pz

            def fk(L, pf, T):
                sigm(L.fv[:, :T], pf[:, 128:128 + T])
                ts(L.fv[:, :T], L.fv[:, :T], L.lbp[:, 3:4], L.lbp[:, 2:3], ALU.mult, ALU.add)
                ts(L.kk[:, :T], L.fv[:, :T], -1.0, 1.0, ALU.mult, ALU.add)

            def gate_finish(L, po, pz, T, col0):
                sigm(L.gs[:T, :], pz[:T, 128:256])
                tt(L.gs[:T, :], L.gs[:T, :], L.hw[:T, :], ALU.mult)
                yield
                yield from finish(po, T, 128, L.gs[:T, :], [L.h], col0, L.tmpf, L.ynb, L.sc, bank(L.bk + 3, BF16)[:, 512:1024])

            def head_gen(L):
                h = L.h
                fv, kk, cum, e1, ksf, qt, qtm, kt, khT, khm, vtok, scm, S, Sbf = (
                    L.fv, L.kk, L.cum, L.e1, L.ksf, L.qt, L.qtm, L.kt, L.khT, L.khm, L.vtok, L.scm, L.S, L.Sbf)
                for ci, (col0, T) in enumerate(pchunks):
                    first = ci == 0
                    subs = [(0, 16)] if T == 16 else [(32 * j, 32) for j in range(4)]
                    pf, pz = proj(L, col0, T)
                    yield
                    fk(L, pf, T)
                    cp(vtok[:T, :], pz[:T, 0:128], eng="act_copy")
                    yield
                    act(cum[:, :T], fv[:, :T], AF.Ln)
                    yield
                    for (s0, tsz) in subs:
                        P.I("dve", "tensor_tensor_scan", out=cum[:, s0:s0 + tsz], data0=ones[:, :tsz], data1=cum[:, s0:s0 + tsz],
                            initial=0.0, op0=ALU.mult, op1=ALU.add)
                    yield
                    act(e1[:, :T], cum[:, :T], AF.Exp)
                    act(cum[:, :T], cum[:, :T], AF.Exp, scale=-1.0)
                    yield
                    stt(qt[:, :T], pf[:, 0:T], float(128 ** -0.5), e1[:, :T], ALU.mult, ALU.mult)
                    tt(ksf[:, :T], kk[:, :T], cum[:, :T], ALU.mult)
                    yield
                    if T == 128:
                        qd = bass.AP(qtm.tensor, qtm.offset, [list(qtm.ap[0]), [160, 4], [1, 32]])
                        cp(qd, qt[:, 0:128].rearrange("p (a b) -> p a b", a=4))
                    else:
                        cp(qtm[:, 0, 0:T], qt[:, 0:T])
                    cp(kt[:, :T], ksf[:, :T])
                    for (s0, tsz) in subs:
                        ts(khT[:, s0:s0 + tsz], ksf[:, s0:s0 + tsz], e1[:, s0 + tsz - 1:s0 + tsz], None, ALU.mult)
                    yield
                    psc = bank(L.bk)
                    mm(psc[:T, 0:T], kt[:, :T], qt[:, :T])
                    ptk = bank(L.bk, BF16)[:, 512:1024]
                    tr(ptk[:T, 0:128], khT[:, :T], identb)
                    yield
                    tt(scm[:T, :T], psc[:T, 0:T], bdm[:T, :T], ALU.mult)
                    for j, (s0, tsz) in enumerate(subs):
                        if T == 128:
                            act(khm[:T, j, :], ptk[:T, 0:128], AF.Identity, scale=rowm[:T, j:j + 1])
                        else:
                            cp(khm[:T, j, :], ptk[:T, 0:128], eng="act_copy")
                    yield
                    po = bank(L.bk + 2)
                    mm(po[:T, 0:128], scm[:T, :T], vtok[:T, :], True, first)
                    for j, (s0, tsz) in enumerate(subs):
                        if not first:
                            mm(po[:T, 0:128], qtm[:, j, :T], Sbf[:, :], False, j == len(subs) - 1)
                        pS = bank(L.bk + 3)
                        mm(pS[:, 0:128], khm[:T, j, :], vtok[:T, :])
                        yield
                        if first:
                            cp(S[:, :], pS[:, 0:128])
                        else:
                            stt(S[:, :], S[:, :], e1[:, s0 + tsz - 1:s0 + tsz], pS[:, 0:128], ALU.mult, ALU.add)
                        yield
                        cp(Sbf[:, :], S[:, :])
                        yield
                    yield from gate_finish(L, po[:T, 0:128], pz, T, col0)
                P.dma("sp", [(dr["nhgrn_p"][l, h], S[:, :])])
                T = NS
                col0 = NTP
                pf, pz = proj(L, col0, T)
                yield
                fk(L, pf, T)
                act(L.qs[:, :], pf[:, 0:T], AF.Copy, scale=float(128 ** -0.5))
                cp(L.itk[:T, :], pz[:T, 0:128], eng="act_copy")
                yield
                tt(L.qmask[:, :, :], L.qs[:, :].unsqueeze(1).to_broadcast([128, 16, 16]), eye16[:, :, :], ALU.mult)
                po = bank(L.bk + 2)
                for b in range(NS):
                    st, stn = L.st[b % 2], L.stn[b % 2]
                    P.dma("sp", [(st[:, :], dr["state_hgrn"][l, b, h])])
                    ts(L.itm[:T, :], L.itk[:T, :], ident[:T, b:b + 1], None, ALU.mult)
                    yield
                    pb_ = bank(L.bk + (0 if b % 2 else 3))
                    mm(pb_[:, 0:128], onesb[:T, :], L.itm[:T, :])
                    yield
                    ts(L.tm2[:, :], pb_[:, 0:128], kk[:, b:b + 1], None, ALU.mult)
                    stt(stn[:, :], st[:, :], fv[:, b:b + 1], L.tm2[:, :], ALU.mult, ALU.add)
                    yield
                    mm(po[:T, 0:128], L.qmask[:, b, :], stn[:, :], b == 0, b == NS - 1)
                    P.dma("pool", [(dr["nhgrn_s"][l, b, h], stn[:, :])])
                    yield
                yield from gate_finish(L, po[:T, 0:128], pz, T, col0)

            lanes = [mklane(h, 4 * i) for i, h in enumerate(hs)]

            def hskew(i, L):
                for _ in range(0 * i):
                    yield
                yield from head_gen(L)
            run_lanes([hskew(i, L) for i, L in enumerate(lanes)])
            P.release()

        def ret_unit(l, h):
            P.phase(f"ret{l}.{h}")
            P.mark()
            gam = GAMMA[h]
            slab = A("slab", BF16, [KC, 1536])
            W = wv(l)
            P.dma("pool", [(slab[:, :, 0:256], W[:, :, O_RQ + h * 256:O_RQ + (h + 1) * 256]),
                           (slab[:, :, 256:512], W[:, :, O_RK + h * 256:O_RK + (h + 1) * 256])])
            P.dma("pool", [(slab[:, :, 512:1024], W[:, :, O_RV + h * 512:O_RV + (h + 1) * 512])])
            P.dma("pool", [(slab[:, :, 1024:1536], W[:, :, O_RG + h * 512:O_RG + (h + 1) * 512])])
            dmk = A("dmk", F32, [128]); g1 = A("g1", F32, [128]); kd = A("kd", F32, [2])
            P.dma("sp", [(dmk, dr["c_dmask"][h]), (g1, dr["c_g1"][h]), (kd, dr["c_kd"][h])])
            S = A("S", F32, [2, 512]); Sbf = A("Sbf", BF16, [2, 512])

            class Lane:
                pass

            def mklane(bk):
                L = Lane()
                L.bk = bk
                L.cs = A("cs", F32, [2, 128]); L.t12 = A("t12", F32, [4, 128])
                L.qkr = A("qkr", BF16, [4, 128]); L.qg = A("qg", BF16, [2, 128]); L.vtok = A("vtok", BF16, [512])
                L.scm = A("scm", BF16, [128]); L.kdec = A("kdec", BF16, [256])
                L.gs = A("gs", F32, [512]); L.ynb = A("ynb", BF16, [512]); L.sc = A("sc", F32, [4])
                return L

            def proj(L, col0, T):
                pf = bank(L.bk)
                for j in range(4):
                    for kc in range(KC):
                        mm(pf[:, j * 128:j * 128 + T], slab[:, kc, j * 128:(j + 1) * 128], xbf[:, kc, col0:col0 + T],
                           kc == 0, kc == KC - 1)
                pv = bank(L.bk + 1)
                for kc in range(KC):
                    mm(pv[:T, :], xbf[:, kc, col0:col0 + T], slab[:, kc, 512:1024], kc == 0, kc == KC - 1)
                pg = bank(L.bk + 2)
                for kc in range(KC):
                    mm(pg[:T, :], xbf[:, kc, col0:col0 + T], slab[:, kc, 1024:1536], kc == 0, kc == KC - 1)
                return pf, pv, pg

            def rotary(L, pf, col0, T, outb):
                cs, t1, t2 = L.cs, L.t12[:, 0:2, :], L.t12[:, 2:4, :]
                P.dma("sp", [(cs[:, 0, :T], dr["c_cos"][:, col0:col0 + T]), (cs[:, 1, :T], dr["c_sin"][:, col0:col0 + T])])
                p4 = pf[:, :].rearrange("p (a b) -> p a b", a=4)
                x1, x2 = p4[:, 0:4:2, :T], p4[:, 1:4:2, :T]
                cosb = cs[:, 0, :T].unsqueeze(1).to_broadcast([128, 2, T])
                sinb = cs[:, 1, :T].unsqueeze(1).to_broadcast([128, 2, T])
                tt(t1[:, :, :T], x1, cosb, ALU.mult)
                tt(t2[:, :, :T], x2, sinb, ALU.mult)
                yield
                tt(outb[:, 0:4:2, :T], t1[:, :, :T], t2[:, :, :T], ALU.subtract)
                tt(t1[:, :, :T], x2, cosb, ALU.mult)
                tt(t2[:, :, :T], x1, sinb, ALU.mult)
                yield
                tt(outb[:, 1:4:2, :T], t1[:, :, :T], t2[:, :, :T], ALU.add)

            def gate_finish(L, po, pg, T, col0, pt):
                sigm(L.gs[:T, :], pg[:T, :])
                yield
                tt(L.gs[:T, :], L.gs[:T, :], pg[:T, :], ALU.mult)
                yield
                yield from finish(po, T, 512, L.gs[:T, :], [(h % 2) * 4 + j for j in range(4)], col0, L.ynb, L.ynb, L.sc, pt)

            def chunk_gen(L, ci, col0, T):
                first = ci == 0
                qkr, vtok, scm, kdec, qg = L.qkr, L.vtok, L.scm, L.kdec, L.qg
                pf, pv, pg = proj(L, col0, T)
                yield
                cp(vtok[:T, :], pv[:T, :], eng="act_copy")
                yield from rotary(L, pf, col0, T, qkr)
                yield
                psc = bank(L.bk)
                for j in range(2):
                    mm(psc[:T, 0:T], qkr[:, 2 + j, :T], qkr[:, j, :T], j == 0, j == 1)
                ptk = bank(L.bk, BF16)[:, 512:1024]
                for j in range(2):
                    tr(ptk[:T, j * 128:(j + 1) * 128], qkr[:, 2 + j, :T], identb)
                if not first:
                    tt(qg[:, :, :T], qkr[:, 0:2, :T], g1[:, :T].unsqueeze(1).to_broadcast([128, 2, T]), ALU.mult)
                yield
                tt(scm[:T, :T], psc[:T, 0:T], dmk[:T, :T], ALU.mult)
                act(kdec[:T, :], ptk[:T, 0:256], AF.Identity, scale=kd[:T, (0 if T == 128 else 1):(1 if T == 128 else 2)])
                yield
                po = bank(L.bk + 3)
                mm(po[:T, :], scm[:T, :T], vtok[:T, :], True, first)
                if not first:
                    for j in range(2):
                        mm(po[:T, :], qg[:, j, :T], Sbf[:, j, :], False, j == 1)
                pS = [bank(L.bk + 1), bank(L.bk)]
                for j in range(2):
                    mm(pS[j][:, :], kdec[:T, j * 128:(j + 1) * 128], vtok[:T, :])
                yield
                for j in range(2):
                    if first:
                        cp(S[:, j, :], pS[j][:, :])
                    else:
                        stt(S[:, j, :], S[:, j, :], float(gam ** T), pS[j][:, :], ALU.mult, ALU.add)
                    cp(Sbf[:, j, :], S[:, j, :], eng="act_copy")
                yield
                yield from gate_finish(L, po[:T, :], pg, T, col0, bank(L.bk + 1, BF16))
                if ci == len(pchunks) - 1:
                    P.dma("sp", [(dr["nret_p"][l, h].rearrange("(j p) v -> p j v", p=128), S[:, :, :])])

            P.mark()
            lanes = [mklane(0), mklane(4)]

            def lane_gen(li):
                if li == 1:
                    for _ in range(6):
                        yield
                for ci in range(li, len(pchunks), 2):
                    col0, T = pchunks[ci]
                    yield from chunk_gen(lanes[li], ci, col0, T)
            run_lanes([lane_gen(0), lane_gen(1)])
            P.release()

            T = NS
            col0 = NTP
            L = mklane(0)
            pf, pv, pg = proj(L, col0, T)
            qkf = A("qkf", F32, [4, 16])
            for _ in rotary(L, pf, col0, T, qkf):
                pass
            ts(qkf[:, 2:4, :], qkf[:, 2:4, :], 1.0 / 16.0, None, ALU.mult)
            qmask = A("qmask", F32, [2, 16, 16]); vtk = L.gs; vtm = L.ynb
            sts = [S, A("st1", F32, [2, 512])]
            for j in range(2):
                tt(qmask[:, j, :, :], qkf[:, j, :].unsqueeze(1).to_broadcast([128, 16, 16]), eye16[:, :, :], ALU.mult)
            cp(vtk[:T, :], pv[:T, :], eng="act_copy")
            po = bank(3)
            vtms = [vtm, A("vtm1", BF16, [512])]

            def samp_gen(par):
                st, vtm_ = sts[par], vtms[par]
                for b in range(par, NS, 2):
                    sview = lambda nm: dr[nm][l, b, h].rearrange("(j p) v -> p j v", p=128)
                    P.dma("sp", [(st[:, :, :], sview("state_ret"))])
                    ts(vtm_[:T, :], vtk[:T, :], ident[:T, b:b + 1], None, ALU.mult)
                    yield
                    pb_ = bank(6 + par)
                    mm(pb_[:, :], onesb[:T, :], vtm_[:T, :])
                    for j in range(2):
                        act(st[:, j, :], st[:, j, :], AF.Copy, scale=float(gam))
                    yield
                    for j in range(2):
                        stt(st[:, j, :], pb_[:, :], qkf[:, 2 + j, b:b + 1], st[:, j, :], ALU.mult, ALU.add)
                        yield
                    for j in range(2):
                        mm(po[:T, :], qmask[:, j, b, :], st[:, j, :], b == 0 and j == 0, b == NS - 1 and j == 1)
                    P.dma("pool", [(sview("nret_s"), st[:, :, :])])
                    yield
            run_lanes([samp_gen(0), samp_gen(1)])
            for _ in gate_finish(L, po[:T, :], pg, T, col0, bank(1, BF16)):
                pass
            P.release()

        def branch_pass(l, wbr, gcol, first):
            P.phase(f"bp{l}")
            W = wv(l)
            wo_v = dr["w_out"][l].rearrange("(k p) c -> p k c", p=128)
            P.mark()
            gT = A("gT", BF16, [KC, NT])
            wgs = [A(f"bg{i}", BF16, [KC, 128]) for i in range(2)]
            wbs = [A(f"bb{i}", BF16, [KC, 128]) for i in range(2)]
            wos = [A(f"bo{i}", BF16, [KC, 128]) for i in range(2)]
            sgs = [A(f"bs{i}", F32, [512]) for i in range(2)]
            it = 0
            for dc in range(KC):
                wg, wb = wgs[dc % 2], wbs[dc % 2]
                P.dma("pool", [(wg, W[:, :, gcol + dc * 128:gcol + (dc + 1) * 128]),
                               (wb, wbr[:, :, dc * 128:(dc + 1) * 128])])
                if dc == KC - 2:
                    for do in range(2):
                        P.dma("pool", [(wos[do], wo_v[:, :, do * 128:(do + 1) * 128])])
                for (c0, n) in tblocks:
                    pg, pb_, sg = bank(it % 2), bank(2 + it % 2), sgs[it % 2]
                    it += 1
                    for kc in range(KC):
                        mm(pg[:, :n], wg[:, kc, :], xbf[:, kc, c0:c0 + n], kc == 0, kc == KC - 1)
                    for kc in range(KC):
                        mm(pb_[:, :n], wb[:, kc, :], yT[:, kc, c0:c0 + n], kc == 0, kc == KC - 1)
                    act(sg[:, :n], pg[:, :n], AF.Sigmoid)
                    tt(gT[:, dc, c0:c0 + n], sg[:, :n], pb_[:, :n], ALU.mult)
            for do in range(KC):
                wo = wos[do % 2]
                if do >= 2:
                    P.dma("pool", [(wo, wo_v[:, :, do * 128:(do + 1) * 128])])
                for (c0, n) in tblocks:
                    po = bank(4 + it % 4)
                    it += 1
                    for kc in range(KC):
                        mm(po[:, :n], wo[:, kc, :], gT[:, kc, c0:c0 + n], kc == 0, kc == KC - 1)
                    if first:
                        stt(xT[:, do, c0:c0 + n], xT[:, do, c0:c0 + n], ALPHA, po[:, :n], ALU.mult, ALU.add)
                    else:
                        tt(xT[:, do, c0:c0 + n], xT[:, do, c0:c0 + n], po[:, :n], ALU.add)
            P.release()

        for l in range(layers):
            for g in range(4):
                ssd_unit(l, g)
            branch_pass(l, dr["w_br_m"][l].rearrange("(k p) c -> p k c", p=128), O_GATE, True)
            for h in range(0, 8, 2):
                hgrn_pair(l, [h, h + 1])
            branch_pass(l, dr["w_br_h"][l].rearrange("(k p) c -> p k c", p=128), O_GATE + 1024, False)
            for hh in range(2):
                for h in range(2 * hh, 2 * hh + 2):
                    ret_unit(l, h)
                branch_pass(l, dr["w_br_r"][l, hh * 1024:(hh + 1) * 1024].rearrange("(k p) c -> p k c", p=128),
                            O_GATE + 2048, False)
            layer_norm_fm(2 + 4 * l + 0, 2 + 4 * l + 1)
            ffn(l)
            layer_norm_fm(2 + 4 * l + 2, 2 + 4 * l + 3, last=(l == layers - 1))

        P.phase("out")
        P.mark()
        ob = [A(f"ob{i}", F32, [D]) for i in range(2)]
        for i, (col0, T, src) in enumerate(ttiles):
            if i == 0:
                continue
            o = ob[i % 2]
            pb = 2 * (i % 2)
            for kc in range(KC):
                tr(bank(pb + kc // 4)[:T, (kc % 4) * 128:(kc % 4 + 1) * 128], xT[:, kc, col0:col0 + T], ident)
            cp(o[:T, 0:512], bank(pb)[:T, :], eng="act_copy")
            cp(o[:T, 512:1024], bank(pb + 1)[:T, :])
            dst = dr["y_sample"][:, :] if i == len(ttiles) - 1 else dr["y_prompt"][128 * (i - 1):128 * i, :]
            P.dma("sp", [(dst, o[:T])])
        P.release()
        P.final_wait("sp")

        import contextlib
        with contextlib.ExitStack() as es:
            for e in ("pe", "act", "dve", "pool"):
                P.sems[e] = es.enter_context(nc.semaphore(f"s_{e}"))
            for s in range(P.n_dma_sems):
                P.sems[("dma", s)] = es.enter_context(nc.semaphore(f"s_dma{s}"))
            P.sems["sp"] = es.enter_context(nc.semaphore("s_sp"))
            block = es.enter_context(nc.Block())
            P.lower(block)
    return nc, P


def consts(NCH):
    SEQ = 128 * NCH
    NT = NMETA + SEQ + NS
    c = {}
    r = np.arange(128)
    c["c_ident"] = np.eye(128, dtype=np.float32)
    c["c_ones"] = np.ones((128, 128), np.float32)
    c["c_triu"] = (r[:, None] <= r[None, :]).astype(np.float32)
    c["c_strl"] = (r[:, None] > r[None, :]).astype(np.float32)
    c["c_bd"] = ((r[:, None] <= r[None, :]) & (r[:, None] // 32 == r[None, :] // 32)).astype(np.float32)
    c["c_rowm"] = (r[:, None] // 32 == np.arange(4)[None, :]).astype(np.float32)
    c["c_eye16"] = np.tile(np.eye(16, dtype=np.float32).reshape(1, 256), (128, 1))
    dm = np.zeros((4, 128, 128), np.float32); g1 = np.zeros((4, 128, 128), np.float32); kd = np.zeros((4, 128, 2), np.float32)
    for h in range(4):
        lg = np.log(np.float32(GAMMA[h])).astype(np.float32)
        diff = (r[None, :] - r[:, None]).astype(np.float32)
        dm[h] = np.where(r[:, None] <= r[None, :], np.exp(diff * lg), 0.0) / 16.0
        g1[h] = np.exp((r[None, :] + 1.0) * lg) * np.ones((128, 1), np.float32)
        kd[h, :, 0] = np.exp((127.0 - r) * lg) / 16.0
        kd[h, :, 1] = np.exp((15.0 - r) * lg) / 16.0
    c["c_dmask"], c["c_g1"], c["c_kd"] = dm, g1, kd.astype(np.float32)
    pos = np.concatenate([np.arange(NMETA + SEQ, dtype=np.float32), np.full(NS, 16384.0, np.float32)])
    inv = (1.0 / (np.float32(10000.0) ** np.linspace(0.0, 1.0, 128, dtype=np.float32))).astype(np.float32)
    ang = (inv[:, None] * pos[None, :]).astype(np.float32)
    c["c_cos"] = np.cos(ang.astype(np.float64)).astype(np.float32)
    c["c_sin"] = np.sin(ang.astype(np.float64)).astype(np.float32)
    return c


_CACHE = {}


def kernel(**inputs):
    NCH = inputs["x_prompt"].shape[1] // 128
    NB = inputs["x_prompt"].shape[0]
    if NCH not in _CACHE:
        _CACHE[NCH] = build(NCH)[0]
    nc = _CACHE[NCH]
    cs = consts(NCH)
    f = lambda a: np.ascontiguousarray(np.asarray(a, dtype=np.float32))
    shared = {k: f(inputs[k]) for k in ["meta_tokens", "ln_in_g", "ln_in_b", "w_in", "conv_w", "conv_b", "dt_bias", "a_log",
                                        "d_skip", "m_norm_w", "hgrn_lb_logits", "h_norm_w", "w_br_m", "w_br_h", "w_br_r",
                                        "w_out", "w_ffn_in", "w_ffn_out", "ln1_g", "ln1_b", "ln2_g", "ln2_b"]}
    shared.update(cs)
    in_maps = []
    for c in range(NB):
        m = dict(shared)
        m["x_prompt"] = f(inputs["x_prompt"][c])
        m["x_sample"] = f(inputs["x_sample"][c * NS:(c + 1) * NS, 0])
        for nm in ["state_ssm", "state_conv", "state_hgrn", "state_ret"]:
            m[nm] = f(inputs[nm][:, c * NS:(c + 1) * NS])
        in_maps.append(m)
    res = run_bass_kernel_spmd(nc, in_maps, core_ids=list(range(NB))).results
    y_prompt = np.stack([r["y_prompt"] for r in res], 0)
    y_sample = np.concatenate([r["y_sample"] for r in res], 0)[:, None, :]
    outs = [y_prompt, y_sample]
    for nm in ["nssm_p", "nconv_p", "nhgrn_p", "nret_p"]:
        outs.append(np.stack([r[nm] for r in res], 1))
    for nm in ["nssm_s", "nconv_s", "nhgrn_s", "nret_s"]:
        outs.append(np.concatenate([r[nm] for r in res], 1))
    return tuple(np.ascontiguousarray(o, dtype=np.float32) for o in outs)
```

```python
import numpy as np
import concourse.bass as bass
import concourse.mybir as mybir
from concourse.bass_utils import run_bass_kernel_spmd

F32 = mybir.dt.float32
BF16 = mybir.dt.bfloat16
AF = mybir.ActivationFunctionType
ALU = mybir.AluOpType
AX = mybir.AxisListType

D = 1024
KC = 8
DEPTH = 2
NS = 16
NMETA = 16
IN_DIM = 16400
DFF = 2816
ALPHA = float((2 * DEPTH) ** 0.25)
EPS_LN = 1e-5
DSIZE = {F32: 4, BF16: 2}

ENGS = ["pe", "act", "dve", "pool", "sp"]


class Buf:
    def __init__(self, name, off, nbytes, space):
        self.name, self.off, self.nbytes, self.space = name, off, nbytes, space
        self.recs = []


class Prog:
    def __init__(self, nc, sb_bytes, n_dma_sems=24):
        self.nc = nc
        self.sb_bytes = sb_bytes
        self.q = {e: [] for e in ENGS}
        self.cnt = {e: 0 for e in ENGS}
        self.waited = {e: {} for e in ENGS}
        self.sems = {}
        self.dma_tot = [0] * n_dma_sems
        self.n_dma_sems = n_dma_sems
        self.dma_rr = 0
        self.bufs = []
        self.freed = []
        self.top = 0
        self.stack = []
        self.out_events = []
        self.phases = []

    def phase(self, name):
        self.phases.append((name, dict(self.cnt)))

    def alloc(self, name, nbytes):
        nbytes = (nbytes + 63) // 64 * 64
        b = Buf(name, self.top, nbytes, "sb")
        self.top += nbytes
        assert self.top <= self.sb_bytes, (name, self.top, self.sb_bytes)
        ev = {}
        keep = []
        for (o, n, evs) in self.freed:
            if o < b.off + nbytes and b.off < o + n:
                for k, v in evs.items():
                    ev[k] = max(ev.get(k, 0), v)
                if not (b.off <= o and o + n <= b.off + nbytes):
                    keep.append((o, n, evs))
            else:
                keep.append((o, n, evs))
        self.freed = keep
        for k, v in ev.items():
            b.recs.append([0, 128, b.off, b.off + nbytes, "W", "seed", k, v])
        self.bufs.append(b)
        return b

    def mark(self):
        self.stack.append((self.top, len(self.bufs)))

    def release(self):
        top, nb = self.stack.pop()
        for b in self.bufs[nb:]:
            evs = {}
            for r in b.recs:
                evs[r[6]] = max(evs.get(r[6], 0), r[7])
            if evs:
                self.freed.append((b.off, b.nbytes, evs))
        self.bufs = self.bufs[:nb]
        self.top = top

    def view(self, buf, dtype, shape, boff=0):
        es = DSIZE[dtype]
        n = int(np.prod(shape))
        assert boff + n * es <= buf.nbytes, (buf.name, boff, n * es, buf.nbytes)
        if buf.space == "sb":
            base = self.sb
            start = (buf.off + boff) // 4
            words = (n * es + 3) // 4
            ap = base[:, start:start + words]
        else:
            base = self.ps
            start = (buf.off + boff) // 4
            words = (n * es + 3) // 4
            ap = base[:, start:start + words]
        if dtype != F32:
            ap = ap.bitcast(dtype)
            ap = ap[:, 0:n]
        if len(shape) == 2:
            ap = ap.rearrange("p (a b) -> p a b", a=shape[0])
        elif len(shape) == 3:
            ap = ap.rearrange("p (a b c) -> p a b c", a=shape[0], b=shape[1])
        return ap

    def _range(self, ap):
        t = ap.tensor
        es = DSIZE[ap.dtype]
        aps = ap.ap
        pstep = aps[0][0]
        off = int(ap.offset)
        if pstep == 0:
            pstep = self.pitch_elems[(t.name, es)]
        plo = off // pstep
        phi = plo + aps[0][1]
        flo = off % pstep
        fhi = flo + 1
        for (s, c) in aps[1:]:
            fhi += (c - 1) * abs(s)
        return plo, phi, flo * es, fhi * es

    def _find(self, ap):
        name = ap.tensor.name
        plo, phi, blo, bhi = self._range(ap)
        if name == "PS":
            b0 = blo // 2048
            b1 = (bhi - 1) // 2048
            return [(self.psbanks[b], 0, 128, b * 2048, (b + 1) * 2048) for b in range(b0, b1 + 1)]
        if name != "SB":
            return []
        res = []
        for b in self.bufs:
            if b.off < bhi and blo < b.off + b.nbytes:
                res.append((b, plo, phi, max(blo, b.off), min(bhi, b.off + b.nbytes)))
        assert res, ("no buf for ap", name, blo, bhi)
        return res

    def _deps(self, eng, accesses, is_dma):
        need = {}
        touched = []
        for ap, kind in accesses:
            for (b, plo, phi, blo, bhi) in self._find(ap):
                ps = b.space == "ps"
                for r in b.recs:
                    if r[0] < phi and plo < r[1] and r[2] < bhi and blo < r[3]:
                        if not (ps or r[4] == "W" or kind == "W"):
                            continue
                        if r[5] == eng and not is_dma:
                            if eng == "pe":
                                continue
                            if r[4] == "R" and not ps:
                                continue
                            if r[4] == "R" and ps and kind == "R":
                                continue
                        need[r[6]] = max(need.get(r[6], 0), r[7])
                touched.append((b, plo, phi, blo, bhi, kind))
        return need, touched

    def _record(self, touched, eng, key, val):
        for (b, plo, phi, blo, bhi, kind) in touched:
            if kind == "W":
                b.recs = [r for r in b.recs
                          if not (plo <= r[0] and r[1] <= phi and blo <= r[2] and r[3] <= bhi)]
                b.recs.append([plo, phi, blo, bhi, "W", eng, key, val])
            else:
                for r in b.recs:
                    if r[4] == "R" and r[5] == eng and r[6] == key and r[0] == plo and r[1] == phi \
                            and r[2] == blo and r[3] == bhi:
                        r[7] = val
                        break
                else:
                    b.recs.append([plo, phi, blo, bhi, "R", eng, key, val])

    def _emit_waits(self, eng, need):
        w = self.waited[eng]
        for k, v in need.items():
            if w.get(k, 0) >= v:
                continue
            w[k] = v
            self.q[eng].append(("wait", k, v))

    def I(self, eng, meth, reads=(), writes=(), **kw):
        acc = []
        for k, v in kw.items():
            if isinstance(v, bass.AP):
                if v.tensor.name not in ("SB", "PS"):
                    continue
                acc.append((v, "W" if k in ("out", "accum_out", "ap") else "R"))
        for a in reads:
            acc.append((a, "R"))
        for a in writes:
            acc.append((a, "W"))
        need, touched = self._deps(eng, acc, False)
        self._emit_waits(eng, need)
        self.cnt[eng] += 1
        self._record(touched, eng, eng, self.cnt[eng])
        self.q[eng].append(("ins", meth, kw))

    def dma(self, eng, pairs, slow=False):
        acc = []
        for (o, i) in pairs:
            if o.tensor.name == "SB":
                acc.append((o, "W"))
            if i.tensor.name == "SB":
                acc.append((i, "R"))
        need, touched = self._deps(eng, acc, True)
        s = self.dma_rr
        self.dma_rr = (self.dma_rr + 1) % self.n_dma_sems
        key = ("dma", s)
        need[key] = max(need.get(key, 0), self.dma_tot[s])
        self._emit_waits(eng, need)
        self.dma_tot[s] += 16 * len(pairs)
        self._record(touched, "dma", key, self.dma_tot[s])
        self.q[eng].append(("dma", s, pairs, slow))
        return (key, self.dma_tot[s])

    def final_wait(self, eng="sp"):
        need = {("dma", s): self.dma_tot[s] for s in range(self.n_dma_sems) if self.dma_tot[s]}
        for e in ("pe", "act", "dve", "pool"):
            if self.cnt[e]:
                need[e] = self.cnt[e]
        self._emit_waits(eng, need)

    def lower(self, block):
        nc = self.nc
        engmap = {"pe": "tensor", "act": "scalar", "dve": "vector", "pool": "gpsimd", "sp": "sync"}

        def semof(k):
            return self.sems[k]

        def run(ename, engine):
            for it in self.q[ename]:
                if it[0] == "wait":
                    engine.wait_ge(semof(it[1]), it[2])
                elif it[0] == "ins":
                    getattr(engine, it[1])(**it[2]).then_inc(semof(ename), 1)
                else:
                    s = semof(("dma", it[1]))
                    for (o, i) in it[2]:
                        if it[3]:
                            engine.dma_start(out=o, in_=i, allow_slow_non_contiguous=True).then_inc(s, 16)
                        else:
                            engine.dma_start(out=o, in_=i).then_inc(s, 16)

        for ename in ENGS:
            if not self.q[ename]:
                continue
            dec = getattr(block, engmap[ename])

            def mk(en):
                def f(engine):
                    run(en, engine)
                return f
            dec(mk(ename))


O_MZ, O_MX, O_MB, O_MC, O_MDT = 0, 1024, 2048, 2560, 3072
O_HQ, O_HF, O_HI, O_HG = 3088, 4112, 5136, 6160
O_RQ, O_RK, O_RV, O_RG = 7184, 8208, 9232, 11280
O_GATE = 13328
GAMMA = [1.0 - 2.0 ** (-5.0 - h) for h in range(4)]


def build(NCH, layers=DEPTH):
    SEQ = 128 * NCH
    NTP = NMETA + SEQ
    NT = NTP + NS
    nc = bass.Bass("TRN2", target_bir_lowering=False, dynamic_dma_scratch_size=8192)
    dr = {}

    def din(name, shape, dt=F32):
        dr[name] = nc.dram_tensor(name, list(shape), dt, kind="ExternalInput").ap()

    def dout(name, shape, dt=F32):
        dr[name] = nc.dram_tensor(name, list(shape), dt, kind="ExternalOutput").ap()

    din("x_prompt", [SEQ, D]); din("x_sample", [NS, D]); din("meta_tokens", [NMETA, D])
    din("state_ssm", [DEPTH, NS, 16, 64, 128]); din("state_conv", [DEPTH, NS, 3, 2048])
    din("state_hgrn", [DEPTH, NS, 8, 128, 128]); din("state_ret", [DEPTH, NS, 4, 256, 512])
    din("ln_in_g", [D]); din("ln_in_b", [D])
    din("w_in", [DEPTH, D, IN_DIM]); din("conv_w", [DEPTH, 4, 2048]); din("conv_b", [DEPTH, 2048])
    din("dt_bias", [DEPTH, 16]); din("a_log", [DEPTH, 16]); din("d_skip", [DEPTH, 16])
    din("m_norm_w", [DEPTH, D]); din("hgrn_lb_logits", [DEPTH, D]); din("h_norm_w", [DEPTH, D])
    din("w_br_m", [DEPTH, D, D]); din("w_br_h", [DEPTH, D, D]); din("w_br_r", [DEPTH, 2 * D, D])
    din("w_out", [DEPTH, D, D])
    din("w_ffn_in", [DEPTH, D, 2 * DFF]); din("w_ffn_out", [DEPTH, DFF, D])
    din("ln1_g", [DEPTH, D]); din("ln1_b", [DEPTH, D]); din("ln2_g", [DEPTH, D]); din("ln2_b", [DEPTH, D])
    for nm in ["c_ident", "c_ones", "c_triu", "c_strl", "c_bd"]:
        din(nm, [128, 128])
    din("c_rowm", [128, 4]); din("c_eye16", [128, 256])
    din("c_dmask", [4, 128, 128]); din("c_g1", [4, 128, 128]); din("c_kd", [4, 128, 2])
    din("c_cos", [128, NT]); din("c_sin", [128, NT])
    dout("y_prompt", [SEQ, D]); dout("y_sample", [NS, D])
    dout("nssm_p", [DEPTH, 16, 64, 128]); dout("nconv_p", [DEPTH, 3, 2048])
    dout("nhgrn_p", [DEPTH, 8, 128, 128]); dout("nret_p", [DEPTH, 4, 256, 512])
    dout("nssm_s", [DEPTH, NS, 16, 64, 128]); dout("nconv_s", [DEPTH, NS, 3, 2048])
    dout("nhgrn_s", [DEPTH, NS, 8, 128, 128]); dout("nret_s", [DEPTH, NS, 4, 256, 512])

    SB_BYTES = 184 * 1024
    P = Prog(nc, SB_BYTES)
    P.pitch_elems = {}

    with (
        nc.sbuf_tensor("SB", [128, SB_BYTES // 4], F32) as SBt,
        nc.psum_tensor("PS", [128, 8 * 512], F32) as PSt,
    ):
        P.sb = SBt[:]
        P.ps = PSt[:]
        P.pitch_elems[("SB", 4)] = SBt[:].ap[0][0]
        P.pitch_elems[("SB", 2)] = SBt[:].ap[0][0] * 2
        P.pitch_elems[("PS", 4)] = PSt[:].ap[0][0]
        P.pitch_elems[("PS", 2)] = PSt[:].ap[0][0] * 2
        P.psbanks = [Buf(f"bank{b}", b * 2048, 2048, "ps") for b in range(8)]

        def bank(b, dtype=F32, shape=None):
            if shape is None:
                shape = [512] if dtype == F32 else [1024]
            return P.view(P.psbanks[b], dtype, shape)

        def A(name, dtype, shape):
            b = P.alloc(name, int(np.prod(shape)) * DSIZE[dtype])
            return P.view(b, dtype, list(shape))

        def mm(out, lhsT, rhs, start=True, stop=True):
            P.I("pe", "matmul", out=out, lhsT=lhsT, rhs=rhs, start=start, stop=stop)

        def tr(out, in_, idn):
            P.I("pe", "transpose", out=out, in_=in_, identity=idn)

        def act(out, in_, func, **kw):
            P.I("act", "activation", out=out, in_=in_, func=func, **kw)

        def tt(out, in0, in1, op, eng="dve"):
            P.I(eng, "tensor_tensor", out=out, in0=in0, in1=in1, op=op)

        def ts(out, in0, s1, s2, op0, op1=None, eng="dve"):
            if op1 is None:
                P.I(eng, "tensor_scalar", out=out, in0=in0, scalar1=s1, scalar2=None, op0=op0)
            else:
                P.I(eng, "tensor_scalar", out=out, in0=in0, scalar1=s1, scalar2=s2, op0=op0, op1=op1)

        def stt(out, in0, scalar, in1, op0, op1, **kw):
            P.I("dve", "scalar_tensor_tensor", out=out, in0=in0, scalar=scalar, in1=in1, op0=op0, op1=op1, **kw)

        def sigm(out, in_):
            act(out, in_, AF.Exp, scale=-1.0)
            act(out, out, AF.Ln, bias=1.0, scale=1.0)
            act(out, out, AF.Exp, scale=-1.0)

        def cp(out, in_, eng="dve"):
            if eng == "act_copy":
                P.I("act", "activation", out=out, in_=in_, func=AF.Copy)
            else:
                P.I(eng, "tensor_copy", out=out, in_=in_)

        xT = A("xT", F32, [KC, NT])
        xbf = A("xbf", BF16, [KC, NT])
        yT = A("yT", BF16, [KC, NT])
        ident = A("ident", F32, [128]); ones = A("ones", F32, [128]); triu = A("triu", F32, [128])
        strl = A("strl", F32, [128]); bdm = A("bdm", F32, [128]); rowm = A("rowm", F32, [4])
        eye16 = A("eye16", BF16, [16, 16]); identb = A("identb", BF16, [128]); onesb = A("onesb", BF16, [128])
        lnp = A("lnp", F32, [2 + 4 * DEPTH, KC])
        P.dma("sp", [(ident, dr["c_ident"]), (ones, dr["c_ones"]), (triu, dr["c_triu"]), (strl, dr["c_strl"]),
                     (bdm, dr["c_bd"]), (rowm, dr["c_rowm"])])
        P.dma("pool", [(eye16, dr["c_eye16"].rearrange("p (a b) -> p a b", a=16))])
        prs = [(lnp[:, 0, :], dr["ln_in_g"].rearrange("(k p) -> p k", p=128)),
               (lnp[:, 1, :], dr["ln_in_b"].rearrange("(k p) -> p k", p=128))]
        for l in range(DEPTH):
            for j, nm in enumerate(["ln1_g", "ln1_b", "ln2_g", "ln2_b"]):
                prs.append((lnp[:, 2 + 4 * l + j, :], dr[nm][l].rearrange("(k p) -> p k", p=128)))
        P.dma("sp", prs, slow=True)
        cp(identb, ident)
        cp(onesb, ones)

        ttiles = [(0, NMETA, dr["meta_tokens"][:, :])]
        for c in range(NCH):
            ttiles.append((NMETA + 128 * c, 128, dr["x_prompt"][128 * c:128 * (c + 1), :]))
        ttiles.append((NTP, NS, dr["x_sample"][:, :]))
        pchunks = [(c0_, T_) for (c0_, T_, _) in ttiles[:-1]]

        def mkblocks(lo, hi):
            res = []
            c0 = lo
            while c0 < hi:
                n = min(512, hi - c0)
                res.append((c0, n))
                c0 += n
            return res
        tblocks = mkblocks(0, NT)

        P.mark()
        tin = [A(f"tin{i}", F32, [D]) for i in range(2)]
        tnr = [A(f"tnr{i}", F32, [D]) for i in range(2)]
        lst = [A(f"lnst{i}", F32, [16]) for i in range(2)]
        for i, (col0, T, src) in enumerate(ttiles):
            a, nr = tin[i % 2], tnr[i % 2]
            stats = lst[i % 2][:, 0:12].rearrange("p (a b) -> p a b", a=2)
            mv = lst[i % 2][:, 12:16]
            P.dma("sp", [(a[:T], src)])
            P.I("dve", "bn_stats", out=stats[:T, 0, :], in_=a[:T, 0:512])
            P.I("dve", "bn_stats", out=stats[:T, 1, :], in_=a[:T, 512:1024])
            P.I("dve", "bn_aggr", out=mv[:T, 0:2], in_=lst[i % 2][:T, 0:12])
            act(mv[:T, 2:3], mv[:T, 1:2], AF.Ln, bias=EPS_LN, scale=1.0)
            act(mv[:T, 3:4], mv[:T, 2:3], AF.Exp, scale=-0.5)
            ts(nr[:T], a[:T], mv[:T, 0:1], mv[:T, 3:4], ALU.subtract, ALU.mult)
            pb = 2 * (i % 2)
            for kc in range(KC):
                tr(bank(pb + kc // 4)[:, (kc % 4) * 128:(kc % 4) * 128 + T], nr[:T, kc * 128:(kc + 1) * 128], ident[:T, :T])
            for kc in range(KC):
                act(xT[:, kc, col0:col0 + T], bank(pb + kc // 4)[:, (kc % 4) * 128:(kc % 4) * 128 + T],
                    AF.Identity, scale=lnp[:, 0, kc:kc + 1], bias=lnp[:, 1, kc:kc + 1])
            cp(xbf[:, :, col0:col0 + T], xT[:, :, col0:col0 + T])
        P.release()

        def layer_norm_fm(gi, bi, last=False):
            P.phase(f"ln{gi}")
            P.mark()
            sq = A("lnsq", F32, [2, 512]); mmb = A("lnm", F32, [4, 512]); ttb = A("lnt", F32, [2, 512])
            for (c0, n) in tblocks:
                s1, s2 = bank(4), bank(5)
                for kc in range(KC):
                    mm(s1[:, :n], ones, xT[:, kc, c0:c0 + n], kc == 0, kc == KC - 1)
                for kc in range(KC):
                    act(sq[:, kc % 2, :n], xT[:, kc, c0:c0 + n], AF.Square)
                    mm(s2[:, :n], ones, sq[:, kc % 2, :n], kc == 0, kc == KC - 1)
                mean, msq, var, rstd = mmb[:, 0, :n], mmb[:, 1, :n], mmb[:, 2, :n], mmb[:, 3, :n]
                act(mean, s1[:, :n], AF.Copy, scale=1.0 / D)
                act(msq, s1[:, :n], AF.Square, scale=1.0 / D)
                stt(var, s2[:, :n], 1.0 / D, msq, ALU.mult, ALU.subtract)
                act(var, var, AF.Ln, bias=EPS_LN, scale=1.0)
                act(rstd, var, AF.Exp, scale=-0.5)
                for kc in range(KC):
                    t = ttb[:, kc % 2, :n]
                    tt(t, xT[:, kc, c0:c0 + n], mean, ALU.subtract)
                    tt(t, t, rstd, ALU.mult)
                    act(xT[:, kc, c0:c0 + n], t, AF.Identity, scale=lnp[:, gi, kc:kc + 1], bias=lnp[:, bi, kc:kc + 1])
                    if not last:
                        cp(xbf[:, kc, c0:c0 + n], xT[:, kc, c0:c0 + n], eng="pool")
            P.release()

        def ffn(l):
            P.phase(f"ffn{l}")
            P.mark()
            passes = [(0, 8), (8, 15), (15, 22)]
            hT = yT
            wgs = [A(f"wg{i}", BF16, [KC, 128]) for i in range(2)]
            wus = [A(f"wu{i}", BF16, [KC, 128]) for i in range(2)]
            wos = [A(f"wo{i}", BF16, [8, 128]) for i in range(2)]
            sgs = [A(f"sg{i}", F32, [512]) for i in range(2)]
            win = dr["w_ffn_in"][l].rearrange("(k p) c -> p k c", p=128)
            wout = dr["w_ffn_out"][l].rearrange("(j p) c -> p j c", p=128)
            it = 0
            for ps_, (j0, j1) in enumerate(passes):
                TPP = j1 - j0
                for j in range(TPP):
                    jt = j0 + j
                    wg, wu = wgs[jt % 2], wus[jt % 2]
                    P.dma("pool", [(wg, win[:, :, jt * 128:(jt + 1) * 128]),
                                   (wu, win[:, :, DFF + jt * 128:DFF + (jt + 1) * 128])])
                    for (c0, n) in tblocks:
                        pg, pu, sg = bank(it % 2), bank(2 + it % 2), sgs[it % 2]
                        it += 1
                        for kc in range(KC):
                            mm(pg[:, :n], wg[:, kc, :], xbf[:, kc, c0:c0 + n], kc == 0, kc == KC - 1)
                        for kc in range(KC):
                            mm(pu[:, :n], wu[:, kc, :], xbf[:, kc, c0:c0 + n], kc == 0, kc == KC - 1)
                        act(sg[:, :n], pg[:, :n], AF.Silu)
                        tt(hT[:, j, c0:c0 + n], sg[:, :n], pu[:, :n], ALU.mult)
                for do in range(KC):
                    wo = wos[do % 2]
                    P.dma("pool", [(wo[:, 0:TPP, :], wout[:, j0:j1, do * 128:(do + 1) * 128])])
                    for (c0, n) in tblocks:
                        po = bank(6 + it % 2)
                        it += 1
                        for j in range(TPP):
                            mm(po[:, :n], wo[:, j, :], hT[:, j, c0:c0 + n], j == 0, j == TPP - 1)
                        if ps_ == 0:
                            stt(xT[:, do, c0:c0 + n], xT[:, do, c0:c0 + n], ALPHA, po[:, :n], ALU.mult, ALU.add)
                        else:
                            tt(xT[:, do, c0:c0 + n], xT[:, do, c0:c0 + n], po[:, :n], ALU.add)
            P.release()

        def run_lanes(gens):
            gens = list(gens)
            while gens:
                for g_ in list(gens):
                    try:
                        next(g_)
                    except StopIteration:
                        gens.remove(g_)

        def finish(ysrc, T, F, gate, ychunks, col0, junk, tmp_bf, sc, pt):
            act(junk[:T, :F], ysrc, AF.Square, accum_out=sc[:T, 0:1])
            act(sc[:T, 1:2], sc[:T, 0:1], AF.Ln, scale=1.0 / F, bias=1e-6)
            act(sc[:T, 2:3], sc[:T, 1:2], AF.Exp, scale=-0.5)
            yield
            stt(tmp_bf[:T, :F], ysrc, sc[:T, 2:3], gate, ALU.mult, ALU.mult)
            yield
            for j in range(F // 128):
                tr(pt[:, j * 128:j * 128 + T], tmp_bf[:T, j * 128:(j + 1) * 128], identb[:T, :T])
            yield
            for j, yc in enumerate(ychunks):
                cp(yT[:, yc, col0:col0 + T], pt[:, j * 128:j * 128 + T], eng="dve" if j % 2 else "act_copy")
            yield

        wv = lambda l: dr["w_in"][l].rearrange("(k p) c -> p k c", p=128)

        def ssd_unit(l, g):
            P.phase(f"ssd{l}.{g}")
            P.mark()
            slab = A("slab", BF16, [KC, 772])
            W = wv(l)
            P.dma("pool", [(slab[:, :, 0:256], W[:, :, O_MX + g * 256:O_MX + (g + 1) * 256]),
                           (slab[:, :, 256:384], W[:, :, O_MB + g * 128:O_MB + (g + 1) * 128]),
                           (slab[:, :, 384:512], W[:, :, O_MC + g * 128:O_MC + (g + 1) * 128]),
                           (slab[:, :, 512:768], W[:, :, O_MZ + g * 256:O_MZ + (g + 1) * 256]),
                           (slab[:, :, 768:772], W[:, :, O_MDT + 4 * g:O_MDT + 4 * g + 4])])
            cw = A("cw", F32, [4, 4]); cb = A("cb", F32, [4])
            cbase = [g * 256, g * 256 + 128, 1024 + g * 128, 1536 + g * 128]
            prs = []
            for j in range(4):
                prs.append((cw[:, j, :], dr["conv_w"][l].rearrange("k c -> c k")[cbase[j]:cbase[j] + 128, :]))
                prs.append((cb[:, j:j + 1], dr["conv_b"][l].rearrange("(c o) -> c o", o=1)[cbase[j]:cbase[j] + 128, :]))
            P.dma("sp", prs, slow=True)
            hp = A("hp", F32, [3, 4])
            mw = A("mw", F32, [256])
            P.dma("sp", [(hp[:, 0, :], dr["dt_bias"][l, 4 * g:4 * g + 4].partition_broadcast(128)),
                         (hp[:, 1, :], dr["a_log"][l, 4 * g:4 * g + 4].partition_broadcast(128)),
                         (hp[:, 2, :], dr["d_skip"][l, 4 * g:4 * g + 4].partition_broadcast(128)),
                         (mw, dr["m_norm_w"][l, g * 256:(g + 1) * 256].partition_broadcast(128))])
            act(hp[:, 1, :], hp[:, 1, :], AF.Exp)
            ts(hp[:, 1, :], hp[:, 1, :], -1.0, None, ALU.mult)
            dsk_bc = hp[:, 2, :].unsqueeze(2).to_broadcast([128, 4, 64])
            S = A("S", F32, [256]); Sbf = A("Sbf", BF16, [256])
            xt3 = lambda a, T: a[:T, :].rearrange("p (h q) -> p h q", h=4)

            class Lane:
                pass

            def mklane(bk):
                L = Lane()
                L.bk = bk
                L.raw = A("raw", F32, [4, 131]); L.cv = A("cv", F32, [4, 128]); L.xcb = A("xcb", BF16, [4, 128])
                L.xtok = A("xtok", BF16, [256]); L.xdt = A("xdt", BF16, [256]); L.xw = A("xw", BF16, [256]); L.btok = A("btok", BF16, [128])
                L.dtt = A("dtt", F32, [16]); L.cc = A("cc", F32, [16])
                L.cbm = A("cbm", F32, [128]); L.Lh = A("Lh", F32, [4, 128]); L.dec = A("dec", F32, [4, 128]); L.wT = A("wT", BF16, [4, 128])
                L.yi = A("yi", F32, [256]); L.tmpf = A("tmpf", F32, [256]); L.zs = A("zs", F32, [256]); L.ynb = A("ynb", BF16, [256])
                L.sc = A("sc", F32, [4])
                return L

            def proj(L, col0, T):
                pf = bank(L.bk)
                for j in range(4):
                    for kc in range(KC):
                        mm(pf[:, j * 128:j * 128 + T], slab[:, kc, j * 128:(j + 1) * 128], xbf[:, kc, col0:col0 + T],
                           kc == 0, kc == KC - 1)
                pz = bank(L.bk + 1)
                for kc in range(KC):
                    mm(pz[:T, 0:260], xbf[:, kc, col0:col0 + T], slab[:, kc, 512:772], kc == 0, kc == KC - 1)
                return pf, pz

            def dt_calc(L, pz, T):
                dtt = L.dtt
                tt(dtt[:T, 8:12], pz[:T, 256:260], hp[:T, 0, :], ALU.add)
                act(dtt[:T, 8:12], dtt[:T, 8:12], AF.Exp)
                act(dtt[:T, 0:4], dtt[:T, 8:12], AF.Ln, bias=1.0, scale=1.0)
                tt(dtt[:T, 4:8], dtt[:T, 0:4], hp[:T, 1, :], ALU.mult)

            def conv_silu(L, T, taps):
                cv = L.cv
                for j in range(4):
                    act(cv[:, j, :T], taps(j, 0), AF.Identity, scale=cw[:, j, 0:1], bias=cb[:, j:j + 1])
                for k in range(1, 4):
                    for j in range(4):
                        stt(cv[:, j, :T], taps(j, k), cw[:, j, k:k + 1], cv[:, j, :T], ALU.mult, ALU.add)
                sigm(L.dec[:, :, :T], cv[:, :, :T])
                tt(L.xcb[:, :, :T], cv[:, :, :T], L.dec[:, :, :T], ALU.mult)

            def gate_finish(L, pz, T, col0):
                ysb = L.yi
                tt(xt3(L.tmpf, T), xt3(L.xtok, T), dsk_bc[:T], ALU.mult)
                tt(ysb[:T, :], ysb[:T, :], L.tmpf[:T, :], ALU.add)
                sigm(L.zs[:T, :], pz[:T, 0:256])
                tt(ysb[:T, :], ysb[:T, :], pz[:T, 0:256], ALU.mult)
                yield
                tt(ysb[:T, :], ysb[:T, :], L.zs[:T, :], ALU.mult)
                yield
                yield from finish(ysb[:T, :], T, 256, mw[:T, :], [2 * g, 2 * g + 1], col0, L.tmpf, L.ynb, L.sc,
                                  bank(L.bk + 3, BF16)[:, 512:1024])

            def chunk_gen(L, Lprev, ci, col0, T):
                first = ci == 0
                raw, xcb, dtt, cc = L.raw, L.xcb, L.dtt, L.cc
                if first:
                    P.I("dve", "memset", ap=raw[:, :, 0:3], constant=0.0)
                pf, pz = proj(L, col0, T)
                yield
                act(raw[:, :, 3:3 + T], pf[:, :].rearrange("p (a b) -> p a b", a=4)[:, :, :T], AF.Copy)
                dt_calc(L, pz, T)
                yield
                if ci == len(pchunks) - 1:
                    prs = [(dr["nconv_p"][l].rearrange("k c -> c k")[cbase[j]:cbase[j] + 128, :], raw[:, j, T:T + 3])
                           for j in range(4)]
                    P.dma("sp", prs, slow=True)
                if not first:
                    pass
                conv_silu(L, T, lambda j, k: raw[:, j, k:k + T])
                yield
                pc = bank(L.bk + 1)[:, 300:312]
                mm(pc[:T, 0:4], triu[:T, :T], dtt[:T, 4:8])
                mm(pc[:, 4:8], ones[:T, :], dtt[:T, 4:8])
                ptb = bank(L.bk, BF16)
                for j in range(3):
                    tr(ptb[:T, j * 128:(j + 1) * 128], xcb[:, j, :T], identb)
                yield
                if T == 128:
                    cp(cc[:, 0:8], pc[:, 0:8], eng="act_copy")
                else:
                    cp(cc[:T, 0:4], pc[:T, 0:4], eng="act_copy")
                    cp(cc[:, 4:8], pc[:, 4:8], eng="act_copy")
                cp(L.xtok[:T, :], ptb[:T, 0:256], eng="act_copy")
                cp(L.btok[:T, :], ptb[:T, 256:384], eng="act_copy")
                yield
                act(cc[:T, 8:12], cc[:T, 0:4], AF.Exp)
                act(cc[:, 12:16], cc[:, 4:8], AF.Exp)
                tt(dtt[:T, 8:12], cc[:T, 4:8], cc[:T, 0:4], ALU.subtract)
                tt(xt3(L.xdt, T), xt3(L.xtok, T), dtt[:T, 0:4].unsqueeze(2).to_broadcast([T, 4, 64]), ALU.mult)
                yield
                act(dtt[:T, 8:12], dtt[:T, 8:12], AF.Exp)
                pcb = bank(L.bk)
                mm(pcb[:T, :T], xcb[:, 2, :T], xcb[:, 3, :T])
                for h in range(4):
                    act(L.Lh[:T, h, :T], strl[:T, :T], AF.Identity, scale=dtt[:T, 4 + h:5 + h])
                yield
                tt(dtt[:T, 12:16], dtt[:T, 8:12], dtt[:T, 0:4], ALU.mult)
                tt(L.cbm[:T, :T], pcb[:T, :T], triu[:T, :T], ALU.mult)
                yield
                tt(xt3(L.xw, T), xt3(L.xtok, T), dtt[:T, 12:16].unsqueeze(2).to_broadcast([T, 4, 64]), ALU.mult)
                pd = bank(L.bk)
                for h in range(4):
                    mm(pd[:T, h * 128:h * 128 + T], L.Lh[:T, h, :T], triu[:T, :T])
                yield
                pyy = bank(L.bk + 2)
                if not first:
                    mm(pyy[:T, 256:512], xcb[:, 3, :T], Sbf[:, :])
                pS = bank(L.bk + 3)
                mm(pS[:, 0:256], L.btok[:T, :], L.xw[:T, :])
                act(L.dec[:T, :, :T], pd[:T, :].rearrange("p (a b) -> p a b", a=4)[:, :, :T], AF.Exp)
                yield
                if not first:
                    tt(xt3(L.tmpf, T), pyy[:T, 256:512].rearrange("p (h q) -> p h q", h=4),
                       cc[:T, 8:12].unsqueeze(2).to_broadcast([T, 4, 64]), ALU.mult)
                if first:
                    cp(S[:, :], pS[:, 0:256])
                else:
                    tt(S[:, :].rearrange("p (h q) -> p h q", h=4), S[:, :].rearrange("p (h q) -> p h q", h=4),
                       cc[:, 12:16].unsqueeze(2).to_broadcast([128, 4, 64]), ALU.mult)
                    tt(S[:, :], S[:, :], pS[:, 0:256], ALU.add)
                cp(Sbf[:, :], S[:, :])
                yield
                tt(L.wT[:T, :, :T], L.dec[:T, :, :T], L.cbm[:T, :T].unsqueeze(1).to_broadcast([T, 4, T]), ALU.mult)
                yield
                for h in range(4):
                    mm(pyy[:T, h * 64:(h + 1) * 64], L.wT[:T, h, :T], L.xdt[:T, h * 64:(h + 1) * 64])
                yield
                if first:
                    cp(L.yi[:T, :], pyy[:T, 0:256], eng="act_copy")
                else:
                    tt(L.yi[:T, :], pyy[:T, 0:256], L.tmpf[:T, :], ALU.add)
                yield
                yield from gate_finish(L, pz, T, col0)
                if ci == len(pchunks) - 1:
                    pt = bank(L.bk + 2)
                    for j in range(2):
                        tr(pt[:, j * 128:(j + 1) * 128], S[:, j * 128:(j + 1) * 128], ident)
                    cp(L.tmpf[:, :], pt[:, 0:256])
                    P.dma("sp", [(dr["nssm_p"][l, 4 * g:4 * g + 4].rearrange("h p n -> (h p) n").rearrange("(t q) n -> q t n", q=128),
                                  L.tmpf[:, :].rearrange("p (t n) -> p t n", t=2))])

            P.mark()
            lanes = [mklane(0), mklane(4)]

            def lane_gen(li):
                L = lanes[li]
                for ci in range(li, len(pchunks), 2):
                    col0, T = pchunks[ci]
                    if ci > 0:
                        Lp = lanes[1 - li]
                        Tp = pchunks[ci - 1][1]
                        cp(L.raw[:, :, 0:3], Lp.raw[:, :, Tp:Tp + 3], eng="pool")
                    yield from chunk_gen(L, None, ci, col0, T)
            def skewed(li):
                if li == 1:
                    for _ in range(10):
                        yield
                yield from lane_gen(li)
            run_lanes([skewed(0), skewed(1)])
            P.release()

            P.phase("sample")
            T = NS
            col0 = NTP
            L = mklane(0)
            raw, xcb, dtt = L.raw, L.xcb, L.dtt
            pf, pz = proj(L, col0, T)
            act(raw[:, :, 3:3 + T], pf[:, :].rearrange("p (a b) -> p a b", a=4)[:, :, :T], AF.Copy)
            pxb = bank(4)
            for kc in range(KC):
                mm(pxb[:T, :], xbf[:, kc, col0:col0 + T], slab[:, kc, 0:512], kc == 0, kc == KC - 1)
            cst = A("cst", F32, [3, 512]); rtk = A("rtk", F32, [512]); halo = A("halo", F32, [4, 3, 16])
            cp(rtk[:T, :], pxb[:T, :], eng="act_copy")
            P.dma("sp", [(cst[:T, :, j * 128:(j + 1) * 128], dr["state_conv"][l, :, :, cbase[j]:cbase[j] + 128]) for j in range(4)])
            P.dma("sp", [(dr["nconv_s"][l, :, 0:2, cbase[j]:cbase[j] + 128], cst[:T, 1:3, j * 128:(j + 1) * 128]) for j in range(4)] +
                        [(dr["nconv_s"][l, :, 2, cbase[j]:cbase[j] + 128], rtk[:T, j * 128:(j + 1) * 128]) for j in range(4)])
            ph = bank(5)
            for j in range(4):
                for k in range(3):
                    tr(ph[:, (j * 3 + k) * 16:(j * 3 + k + 1) * 16], cst[:T, k, j * 128:(j + 1) * 128], ident[:T, :T])
            cp(halo[:, :, :, :], ph[:, 0:192].rearrange("p (a b c) -> p a b c", a=4, b=3))
            conv_silu(L, T, lambda j, k: halo[:, j, k, :] if k < 3 else raw[:, j, 3:3 + T])
            dt_calc(L, pz, T)
            act(dtt[:T, 8:12], dtt[:T, 4:8], AF.Exp)
            ptb = bank(2, BF16)
            for j in range(4):
                tr(ptb[:T, j * 128:(j + 1) * 128], xcb[:, j, :T], identb)
            cp(L.xtok[:T, :], ptb[:T, 0:256], eng="act_copy")
            bct = A("bct", F32, [256]); bcm = A("bcm", BF16, [256])
            cp(bct[:T, :], ptb[:T, 256:512], eng="act_copy")
            xe = A("xe", F32, [512])
            tt(xe[:T, 0:256].rearrange("p (h q) -> p h q", h=4), xt3(L.xtok, T),
               dtt[:T, 0:4].unsqueeze(2).to_broadcast([T, 4, 64]), ALU.mult)
            cp(xe[:T, 256:512].rearrange("p (h q) -> p h q", h=4), dtt[:T, 8:12].unsqueeze(2).to_broadcast([T, 4, 64]))
            pq = bank(3)
            for j in range(4):
                tr(pq[:, j * 16:(j + 1) * 16], xe[:T, j * 128:(j + 1) * 128], ident[:T, :T])
            scs = A("scs", F32, [4, 16])
            cp(scs[:, :, :], pq[:, 0:64].rearrange("p (a b) -> p a b", a=4))
            sts = [A(f"st{i}", F32, [2, 128]) for i in range(2)]
            stns = [A(f"stn{i}", F32, [2, 128]) for i in range(2)]
            yfm = A("yfm", F32, [2, 16]); tm2s = [A(f"tm2{i}", F32, [128]) for i in range(2)]
            junks = [A(f"junk{i}", F32, [128]) for i in range(2)]
            bcms = [bcm, A("bcm1", BF16, [256])]

            def samp_gen(par):
                st, stn, tm2, junk, bcm_ = sts[par], stns[par], tm2s[par], junks[par], bcms[par]
                for b in range(par, NS, 2):
                    sview = lambda nm: dr[nm][l, b, 4 * g:4 * g + 4].rearrange("h p n -> (h p) n").rearrange("(t q) n -> q t n", q=128)
                    P.dma("sp", [(st[:, :, :], sview("state_ssm"))])
                    ts(bcm_[:T, :], bct[:T, :], ident[:T, b:b + 1], None, ALU.mult)
                    yield
                    pb_ = bank(6 + par)
                    mm(pb_[:, 0:256], onesb[:T, :], bcm_[:T, :])
                    yield
                    for j in range(2):
                        ts(tm2[:, :], pb_[:, 0:128], scs[:, j, b:b + 1], None, ALU.mult)
                        yield
                        stt(stn[:, j, :], st[:, j, :], scs[:, 2 + j, b:b + 1], tm2[:, :], ALU.mult, ALU.add)
                        yield
                        stt(junk[:, :], stn[:, j, :], 1.0, pb_[:, 128:256], ALU.mult, ALU.mult, accum_out=yfm[:, j, b:b + 1])
                        yield
                    P.dma("sp", [(sview("nssm_s"), stn[:, :, :])])
            run_lanes([samp_gen(0), samp_gen(1)])
            py = bank(4)
            for j in range(2):
                tr(py[:T, j * 128:(j + 1) * 128], yfm[:, j, :], ident)
            cp(L.yi[:T, :], py[:T, 0:256], eng="act_copy")
            for _ in gate_finish(L, pz, T, col0):
                pass
            P.release()

        def hgrn_pair(l, hs):
            P.phase(f"hgrn{l}.{hs[0]}")
            P.mark()
            W = wv(l)

            class Lane:
                pass

            def mklane(h, bk):
                L = Lane()
                L.h, L.bk = h, bk
                L.slab = A("slab", BF16, [KC, 512])
                P.dma("pool", [(L.slab[:, :, 0:128], W[:, :, O_HQ + h * 128:O_HQ + (h + 1) * 128]),
                               (L.slab[:, :, 128:256], W[:, :, O_HF + h * 128:O_HF + (h + 1) * 128]),
                               (L.slab[:, :, 256:384], W[:, :, O_HI + h * 128:O_HI + (h + 1) * 128]),
                               (L.slab[:, :, 384:512], W[:, :, O_HG + h * 128:O_HG + (h + 1) * 128])])
                lbp = L.lbp = A("lbp", F32, [4])
                L.hw = A("hw", F32, [128])
                P.dma("sp", [(lbp[:, 0:2], dr["hgrn_lb_logits"].rearrange("l c -> c l")[h * 128:(h + 1) * 128, :])], slow=True)
                P.dma("sp", [(L.hw, dr["h_norm_w"][l, h * 128:(h + 1) * 128].partition_broadcast(128))])
                if l == 0:
                    P.I("dve", "memset", ap=lbp[:, 2:3], constant=0.0)
                    P.I("dve", "memset", ap=lbp[:, 3:4], constant=1.0)
                else:
                    tt(lbp[:, 2:3], lbp[:, 0:1], lbp[:, 1:2], ALU.subtract)
                    act(lbp[:, 2:3], lbp[:, 2:3], AF.Exp)
                    ts(lbp[:, 2:3], lbp[:, 2:3], 1.0, None, ALU.add)
                    P.I("dve", "reciprocal", out=lbp[:, 2:3], in_=lbp[:, 2:3])
                    ts(lbp[:, 3:4], lbp[:, 2:3], -1.0, 1.0, ALU.mult, ALU.add)
                L.fv = A("fv", F32, [128]); L.kk = A("kk", F32, [128]); L.cum = A("cum", F32, [128]); L.e1 = A("e1", F32, [128])
                L.ksf = A("ksf", F32, [128]); L.qt = A("qt", BF16, [128]); L.qtm = A("qtm", BF16, [4, 128]); L.kt = A("kt", BF16, [128])
                L.khT = A("khT", BF16, [128]); L.khm = A("khm", BF16, [4, 128]); L.vtok = A("vtok", BF16, [128]); L.scm = A("scm", BF16, [128])
                L.gs = A("gs", F32, [128]); L.tmpf = A("tmpf", F32, [128]); L.ynb = A("ynb", BF16, [128]); L.sc = A("sc", F32, [4])
                L.S = A("S", F32, [128]); L.Sbf = A("Sbf", BF16, [128])
                L.qs = A("qs", F32, [16]); L.qmask = A("qmask", F32, [16, 16]); L.itk = A("itk", F32, [128]); L.itm = A("itm", BF16, [128])
                L.st = [A(f"st{i}", F32, [128]) for i in range(2)]; L.stn = [A(f"stn{i}", F32, [128]) for i in range(2)]
                L.tm2 = A("tm2", F32, [128])
                P.I("dve", "memset", ap=L.qtm[:, :, :], constant=0.0)
                return L

            def proj(L, col0, T):
                pf = bank(L.bk)
                for j in range(2):
                    for kc in range(KC):
                        mm(pf[:, j * 128:j * 128 + T], L.slab[:, kc, j * 128:(j + 1) * 128], xbf[:, kc, col0:col0 + T],
                           kc == 0, kc == KC - 1)
                pz = bank(L.bk + 1)
                for kc in range(KC):
                    mm(pz[:T, 0:256], xbf[:, kc, col0:col0 + T], L.slab[:, kc, 256:512], kc == 0, kc == KC - 1)
                return pf, pz

            def fk(L, pf, T):
                sigm(L.fv[:, :T], pf[:, 128:128 + T])
                ts(L.fv[:, :T], L.fv[:, :T], L.lbp[:, 3:4], L.lbp[:, 2:3], ALU.mult, ALU.add)
                ts(L.kk[:, :T], L.fv[:, :T], -1.0, 1.0, ALU.mult, ALU.add)

            def gate_finish(L, po, pz, T, col0):
                sigm(L.gs[:T, :], pz[:T, 128:256])
                tt(L.gs[:T, :], L.gs[:T, :], L.hw[:T, :], ALU.mult)
                yield
                yield from finish(po, T, 128, L.gs[:T, :], [L.h], col0, L.tmpf, L.ynb, L.sc, bank(L.bk + 3, BF16)[:, 512:1024])

            def head_gen(L):
                h = L.h
                fv, kk, cum, e1, ksf, qt, qtm, kt, khT, khm, vtok, scm, S, Sbf = (
                    L.fv, L.kk, L.cum, L.e1, L.ksf, L.qt, L.qtm, L.kt, L.khT, L.khm, L.vtok, L.scm, L.S, L.Sbf)
                for ci, (col0, T) in enumerate(pchunks):
                    first = ci == 0
                    subs = [(0, 16)] if T == 16 else [(32 * j, 32) for j in range(4)]
                    pf, pz = proj(L, col0, T)
                    yield
                    fk(L, pf, T)
                    cp(vtok[:T, :], pz[:T, 0:128], eng="act_copy")
                    yield
                    act(cum[:, :T], fv[:, :T], AF.Ln)
                    yield
                    for (s0, tsz) in subs:
                        P.I("dve", "tensor_tensor_scan", out=cum[:, s0:s0 + tsz], data0=ones[:, :tsz], data1=cum[:, s0:s0 + tsz],
                            initial=0.0, op0=ALU.mult, op1=ALU.add)
                    yield
                    act(e1[:, :T], cum[:, :T], AF.Exp)
                    act(cum[:, :T], cum[:, :T], AF.Exp, scale=-1.0)
                    yield
                    stt(qt[:, :T], pf[:, 0:T], float(128 ** -0.5), e1[:, :T], ALU.mult, ALU.mult)
                    tt(ksf[:, :T], kk[:, :T], cum[:, :T], ALU.mult)
                    yield
                    if T == 128:
                        qd = bass.AP(qtm.tensor, qtm.offset, [list(qtm.ap[0]), [160, 4], [1, 32]])
                        cp(qd, qt[:, 0:128].rearrange("p (a b) -> p a b", a=4))
                    else:
                        cp(qtm[:, 0, 0:T], qt[:, 0:T])
                    cp(kt[:, :T], ksf[:, :T])
                    for (s0, tsz) in subs:
                        ts(khT[:, s0:s0 + tsz], ksf[:, s0:s0 + tsz], e1[:, s0 + tsz - 1:s0 + tsz], None, ALU.mult)
                    yield
                    psc = bank(L.bk)
                    mm(psc[:T, 0:T], kt[:, :T], qt[:, :T])
                    ptk = bank(L.bk, BF16)[:, 512:1024]
                    tr(ptk[:T, 0:128], khT[:, :T], identb)
                    yield
                    tt(scm[:T, :T], psc[:T, 0:T], bdm[:T, :T], ALU.mult)
                    for j, (s0, tsz) in enumerate(subs):
                        if T == 128:
                            act(khm[:T, j, :], ptk[:T, 0:128], AF.Identity, scale=rowm[:T, j:j + 1])
                        else:
                            cp(khm[:T, j, :], ptk[:T, 0:128], eng="act_copy")
                    yield
                    po = bank(L.bk + 2)
                    mm(po[:T, 0:128], scm[:T, :T], vtok[:T, :], True, first)
                    for j, (s0, tsz) in enumerate(subs):
                        if not first:
                            mm(po[:T, 0:128], qtm[:, j, :T], Sbf[:, :], False, j == len(subs) - 1)
                        pS = bank(L.bk + 3)
                        mm(pS[:, 0:128], khm[:T, j, :], vtok[:T, :])
                        yield
                        if first:
                            cp(S[:, :], pS[:, 0:128])
                        else:
                            stt(S[:, :], S[:, :], e1[:, s0 + tsz - 1:s0 + tsz], pS[:, 0:128], ALU.mult, ALU.add)
                        yield
                        cp(Sbf[:, :], S[:, :])
                        yield
                    yield from gate_finish(L, po[:T, 0:128], pz, T, col0)
                P.dma("sp", [(dr["nhgrn_p"][l, h], S[:, :])])
                T = NS
                col0 = NTP
                pf, pz = proj(L, col0, T)
                yield
                fk(L, pf, T)
                act(L.qs[:, :], pf[:, 0:T], AF.Copy, scale=float(128 ** -0.5))
                cp(L.itk[:T, :], pz[:T, 0:128], eng="act_copy")
                yield
                tt(L.qmask[:, :, :], L.qs[:, :].unsqueeze(1).to_broadcast([128, 16, 16]), eye16[:, :, :], ALU.mult)
                po = bank(L.bk + 2)
                for b in range(NS):
                    st, stn = L.st[b % 2], L.stn[b % 2]
                    P.dma("sp", [(st[:, :], dr["state_hgrn"][l, b, h])])
                    ts(L.itm[:T, :], L.itk[:T, :], ident[:T, b:b + 1], None, ALU.mult)
                    yield
                    pb_ = bank(L.bk + (0 if b % 2 else 3))
                    mm(pb_[:, 0:128], onesb[:T, :], L.itm[:T, :])
                    yield
                    ts(L.tm2[:, :], pb_[:, 0:128], kk[:, b:b + 1], None, ALU.mult)
                    stt(stn[:, :], st[:, :], fv[:, b:b + 1], L.tm2[:, :], ALU.mult, ALU.add)
                    yield
                    mm(po[:T, 0:128], L.qmask[:, b, :], stn[:, :], b == 0, b == NS - 1)
                    P.dma("sp", [(dr["nhgrn_s"][l, b, h], stn[:, :])])
                    yield
                yield from gate_finish(L, po[:T, 0:128], pz, T, col0)

            lanes = [mklane(h, 4 * i) for i, h in enumerate(hs)]

            def hskew(i, L):
                for _ in range(0 * i):
                    yield
                yield from head_gen(L)
            run_lanes([hskew(i, L) for i, L in enumerate(lanes)])
            P.release()

        def ret_unit(l, h):
            P.phase(f"ret{l}.{h}")
            P.mark()
            gam = GAMMA[h]
            slab = A("slab", BF16, [KC, 1536])
            W = wv(l)
            P.dma("pool", [(slab[:, :, 0:256], W[:, :, O_RQ + h * 256:O_RQ + (h + 1) * 256]),
                           (slab[:, :, 256:512], W[:, :, O_RK + h * 256:O_RK + (h + 1) * 256])])
            P.dma("pool", [(slab[:, :, 512:1024], W[:, :, O_RV + h * 512:O_RV + (h + 1) * 512])])
            P.dma("pool", [(slab[:, :, 1024:1536], W[:, :, O_RG + h * 512:O_RG + (h + 1) * 512])])
            dmk = A("dmk", F32, [128]); g1 = A("g1", F32, [128]); kd = A("kd", F32, [2])
            P.dma("sp", [(dmk, dr["c_dmask"][h]), (g1, dr["c_g1"][h]), (kd, dr["c_kd"][h])])
            S = A("S", F32, [2, 512]); Sbf = A("Sbf", BF16, [2, 512])

            class Lane:
                pass

            def mklane(bk):
                L = Lane()
                L.bk = bk
                L.cs = A("cs", F32, [2, 128]); L.t12 = A("t12", F32, [4, 128])
                L.qkr = A("qkr", BF16, [4, 128]); L.qg = A("qg", BF16, [2, 128]); L.vtok = A("vtok", BF16, [512])
                L.scm = A("scm", BF16, [128]); L.kdec = A("kdec", BF16, [256])
                L.gs = A("gs", F32, [512]); L.ynb = A("ynb", BF16, [512]); L.sc = A("sc", F32, [4])
                return L

            def proj(L, col0, T):
                pf = bank(L.bk)
                for j in range(4):
                    for kc in range(KC):
                        mm(pf[:, j * 128:j * 128 + T], slab[:, kc, j * 128:(j + 1) * 128], xbf[:, kc, col0:col0 + T],
                           kc == 0, kc == KC - 1)
                pv = bank(L.bk + 1)
                for kc in range(KC):
                    mm(pv[:T, :], xbf[:, kc, col0:col0 + T], slab[:, kc, 512:1024], kc == 0, kc == KC - 1)
                pg = bank(L.bk + 2)
                for kc in range(KC):
                    mm(pg[:T, :], xbf[:, kc, col0:col0 + T], slab[:, kc, 1024:1536], kc == 0, kc == KC - 1)
                return pf, pv, pg

            def rotary(L, pf, col0, T, outb):
                cs, t1, t2 = L.cs, L.t12[:, 0:2, :], L.t12[:, 2:4, :]
                P.dma("sp", [(cs[:, 0, :T], dr["c_cos"][:, col0:col0 + T]), (cs[:, 1, :T], dr["c_sin"][:, col0:col0 + T])])
                p4 = pf[:, :].rearrange("p (a b) -> p a b", a=4)
                x1, x2 = p4[:, 0:4:2, :T], p4[:, 1:4:2, :T]
                cosb = cs[:, 0, :T].unsqueeze(1).to_broadcast([128, 2, T])
                sinb = cs[:, 1, :T].unsqueeze(1).to_broadcast([128, 2, T])
                tt(t1[:, :, :T], x1, cosb, ALU.mult)
                tt(t2[:, :, :T], x2, sinb, ALU.mult)
                yield
                tt(outb[:, 0:4:2, :T], t1[:, :, :T], t2[:, :, :T], ALU.subtract)
                tt(t1[:, :, :T], x2, cosb, ALU.mult)
                tt(t2[:, :, :T], x1, sinb, ALU.mult)
                yield
                tt(outb[:, 1:4:2, :T], t1[:, :, :T], t2[:, :, :T], ALU.add)

            def gate_finish(L, po, pg, T, col0, pt):
                sigm(L.gs[:T, :], pg[:T, :])
                yield
                tt(L.gs[:T, :], L.gs[:T, :], pg[:T, :], ALU.mult)
                yield
                yield from finish(po, T, 512, L.gs[:T, :], [(h % 2) * 4 + j for j in range(4)], col0, L.ynb, L.ynb, L.sc, pt)

            def chunk_gen(L, ci, col0, T):
                first = ci == 0
                qkr, vtok, scm, kdec, qg = L.qkr, L.vtok, L.scm, L.kdec, L.qg
                pf, pv, pg = proj(L, col0, T)
                yield
                cp(vtok[:T, :], pv[:T, :], eng="act_copy")
                yield from rotary(L, pf, col0, T, qkr)
                yield
                psc = bank(L.bk)
                for j in range(2):
                    mm(psc[:T, 0:T], qkr[:, 2 + j, :T], qkr[:, j, :T], j == 0, j == 1)
                ptk = bank(L.bk, BF16)[:, 512:1024]
                for j in range(2):
                    tr(ptk[:T, j * 128:(j + 1) * 128], qkr[:, 2 + j, :T], identb)
                if not first:
                    tt(qg[:, :, :T], qkr[:, 0:2, :T], g1[:, :T].unsqueeze(1).to_broadcast([128, 2, T]), ALU.mult)
                yield
                tt(scm[:T, :T], psc[:T, 0:T], dmk[:T, :T], ALU.mult)
                act(kdec[:T, :], ptk[:T, 0:256], AF.Identity, scale=kd[:T, (0 if T == 128 else 1):(1 if T == 128 else 2)])
                yield
                po = bank(L.bk + 3)
                mm(po[:T, :], scm[:T, :T], vtok[:T, :], True, first)
                if not first:
                    for j in range(2):
                        mm(po[:T, :], qg[:, j, :T], Sbf[:, j, :], False, j == 1)
                pS = [bank(L.bk + 1), bank(L.bk)]
                for j in range(2):
                    mm(pS[j][:, :], kdec[:T, j * 128:(j + 1) * 128], vtok[:T, :])
                yield
                for j in range(2):
                    if first:
                        cp(S[:, j, :], pS[j][:, :])
                    else:
                        stt(S[:, j, :], S[:, j, :], float(gam ** T), pS[j][:, :], ALU.mult, ALU.add)
                    cp(Sbf[:, j, :], S[:, j, :], eng="act_copy")
                yield
                yield from gate_finish(L, po[:T, :], pg, T, col0, bank(L.bk + 1, BF16))
                if ci == len(pchunks) - 1:
                    P.dma("sp", [(dr["nret_p"][l, h].rearrange("(j p) v -> p j v", p=128), S[:, :, :])])

            P.mark()
            lanes = [mklane(0), mklane(4)]

            def lane_gen(li):
                if li == 1:
                    for _ in range(6):
                        yield
                for ci in range(li, len(pchunks), 2):
                    col0, T = pchunks[ci]
                    yield from chunk_gen(lanes[li], ci, col0, T)
            run_lanes([lane_gen(0), lane_gen(1)])
            P.release()

            T = NS
            col0 = NTP
            L = mklane(0)
            pf, pv, pg = proj(L, col0, T)
            qkf = A("qkf", F32, [4, 16])
            for _ in rotary(L, pf, col0, T, qkf):
                pass
            ts(qkf[:, 2:4, :], qkf[:, 2:4, :], 1.0 / 16.0, None, ALU.mult)
            qmask = A("qmask", F32, [2, 16, 16]); vtk = L.gs; vtm = L.ynb
            sts = [S, A("st1", F32, [2, 512])]
            for j in range(2):
                tt(qmask[:, j, :, :], qkf[:, j, :].unsqueeze(1).to_broadcast([128, 16, 16]), eye16[:, :, :], ALU.mult)
            cp(vtk[:T, :], pv[:T, :], eng="act_copy")
            po = bank(3)
            vtms = [vtm, A("vtm1", BF16, [512])]

            def samp_gen(par):
                st, vtm_ = sts[par], vtms[par]
                for b in range(par, NS, 2):
                    sview = lambda nm: dr[nm][l, b, h].rearrange("(j p) v -> p j v", p=128)
                    P.dma("sp", [(st[:, :, :], sview("state_ret"))])
                    ts(vtm_[:T, :], vtk[:T, :], ident[:T, b:b + 1], None, ALU.mult)
                    yield
                    pb_ = bank(6 + par)
                    mm(pb_[:, :], onesb[:T, :], vtm_[:T, :])
                    for j in range(2):
                        act(st[:, j, :], st[:, j, :], AF.Copy, scale=float(gam))
                    yield
                    for j in range(2):
                        stt(st[:, j, :], pb_[:, :], qkf[:, 2 + j, b:b + 1], st[:, j, :], ALU.mult, ALU.add)
                        yield
                    for j in range(2):
                        mm(po[:T, :], qmask[:, j, b, :], st[:, j, :], b == 0 and j == 0, b == NS - 1 and j == 1)
                    P.dma("sp", [(sview("nret_s"), st[:, :, :])])
                    yield
            run_lanes([samp_gen(0), samp_gen(1)])
            for _ in gate_finish(L, po[:T, :], pg, T, col0, bank(1, BF16)):
                pass
            P.release()

        def branch_pass(l, wbr, gcol, first):
            P.phase(f"bp{l}")
            W = wv(l)
            wo_v = dr["w_out"][l].rearrange("(k p) c -> p k c", p=128)
            P.mark()
            gT = A("gT", BF16, [KC, NT])
            wgs = [A(f"bg{i}", BF16, [KC, 128]) for i in range(2)]
            wbs = [A(f"bb{i}", BF16, [KC, 128]) for i in range(2)]
            wos = [A(f"bo{i}", BF16, [KC, 128]) for i in range(2)]
            sgs = [A(f"bs{i}", F32, [512]) for i in range(2)]
            it = 0
            for dc in range(KC):
                wg, wb = wgs[dc % 2], wbs[dc % 2]
                P.dma("pool", [(wg, W[:, :, gcol + dc * 128:gcol + (dc + 1) * 128]),
                               (wb, wbr[:, :, dc * 128:(dc + 1) * 128])])
                if dc == KC - 2:
                    for do in range(2):
                        P.dma("pool", [(wos[do], wo_v[:, :, do * 128:(do + 1) * 128])])
                for (c0, n) in tblocks:
                    pg, pb_, sg = bank(it % 2), bank(2 + it % 2), sgs[it % 2]
                    it += 1
                    for kc in range(KC):
                        mm(pg[:, :n], wg[:, kc, :], xbf[:, kc, c0:c0 + n], kc == 0, kc == KC - 1)
                    for kc in range(KC):
                        mm(pb_[:, :n], wb[:, kc, :], yT[:, kc, c0:c0 + n], kc == 0, kc == KC - 1)
                    act(sg[:, :n], pg[:, :n], AF.Sigmoid)
                    tt(gT[:, dc, c0:c0 + n], sg[:, :n], pb_[:, :n], ALU.mult)
            for do in range(KC):
                wo = wos[do % 2]
                if do >= 2:
                    P.dma("pool", [(wo, wo_v[:, :, do * 128:(do + 1) * 128])])
                for (c0, n) in tblocks:
                    po = bank(4 + it % 4)
                    it += 1
                    for kc in range(KC):
                        mm(po[:, :n], wo[:, kc, :], gT[:, kc, c0:c0 + n], kc == 0, kc == KC - 1)
                    if first:
                        stt(xT[:, do, c0:c0 + n], xT[:, do, c0:c0 + n], ALPHA, po[:, :n], ALU.mult, ALU.add)
                    else:
                        tt(xT[:, do, c0:c0 + n], xT[:, do, c0:c0 + n], po[:, :n], ALU.add)
            P.release()

        for l in range(layers):
            for g in range(4):
                ssd_unit(l, g)
            branch_pass(l, dr["w_br_m"][l].rearrange("(k p) c -> p k c", p=128), O_GATE, True)
            for h in range(0, 8, 2):
                hgrn_pair(l, [h, h + 1])
            branch_pass(l, dr["w_br_h"][l].rearrange("(k p) c -> p k c", p=128), O_GATE + 1024, False)
            for hh in range(2):
                for h in range(2 * hh, 2 * hh + 2):
                    ret_unit(l, h)
                branch_pass(l, dr["w_br_r"][l, hh * 1024:(hh + 1) * 1024].rearrange("(k p) c -> p k c", p=128),
                            O_GATE + 2048, False)
            layer_norm_fm(2 + 4 * l + 0, 2 + 4 * l + 1)
            ffn(l)
            layer_norm_fm(2 + 4 * l + 2, 2 + 4 * l + 3, last=(l == layers - 1))

        P.phase("out")
        P.mark()
        ob = [A(f"ob{i}", F32, [D]) for i in range(2)]
        for i, (col0, T, src) in enumerate(ttiles):
            if i == 0:
                continue
            o = ob[i % 2]
            pb = 2 * (i % 2)
            for kc in range(KC):
                tr(bank(pb + kc // 4)[:T, (kc % 4) * 128:(kc % 4 + 1) * 128], xT[:, kc, col0:col0 + T], ident)
            cp(o[:T, 0:512], bank(pb)[:T, :], eng="act_copy")
            cp(o[:T, 512:1024], bank(pb + 1)[:T, :])
            dst = dr["y_sample"][:, :] if i == len(ttiles) - 1 else dr["y_prompt"][128 * (i - 1):128 * i, :]
            P.dma("sp", [(dst, o[:T])])
        P.release()
        P.final_wait("sp")

        import contextlib
        with contextlib.ExitStack() as es:
            for e in ("pe", "act", "dve", "pool"):
                P.sems[e] = es.enter_context(nc.semaphore(f"s_{e}"))
            for s in range(P.n_dma_sems):
                P.sems[("dma", s)] = es.enter_context(nc.semaphore(f"s_dma{s}"))
            P.sems["sp"] = es.enter_context(nc.semaphore("s_sp"))
            block = es.enter_context(nc.Block())
            P.lower(block)
    return nc, P


def consts(NCH):
    SEQ = 128 * NCH
    NT = NMETA + SEQ + NS
    c = {}
    r = np.arange(128)
    c["c_ident"] = np.eye(128, dtype=np.float32)
    c["c_ones"] = np.ones((128, 128), np.float32)
    c["c_triu"] = (r[:, None] <= r[None, :]).astype(np.float32)
    c["c_strl"] = (r[:, None] > r[None, :]).astype(np.float32)
    c["c_bd"] = ((r[:, None] <= r[None, :]) & (r[:, None] // 32 == r[None, :] // 32)).astype(np.float32)
    c["c_rowm"] = (r[:, None] // 32 == np.arange(4)[None, :]).astype(np.float32)
    c["c_eye16"] = np.tile(np.eye(16, dtype=np.float32).reshape(1, 256), (128, 1))
    dm = np.zeros((4, 128, 128), np.float32); g1 = np.zeros((4, 128, 128), np.float32); kd = np.zeros((4, 128, 2), np.float32)
    for h in range(4):
        lg = np.log(np.float32(GAMMA[h])).astype(np.float32)
        diff = (r[None, :] - r[:, None]).astype(np.float32)
        dm[h] = np.where(r[:, None] <= r[None, :], np.exp(diff * lg), 0.0) / 16.0
        g1[h] = np.exp((r[None, :] + 1.0) * lg) * np.ones((128, 1), np.float32)
        kd[h, :, 0] = np.exp((127.0 - r) * lg) / 16.0
        kd[h, :, 1] = np.exp((15.0 - r) * lg) / 16.0
    c["c_dmask"], c["c_g1"], c["c_kd"] = dm, g1, kd.astype(np.float32)
    pos = np.concatenate([np.arange(NMETA + SEQ, dtype=np.float32), np.full(NS, 16384.0, np.float32)])
    inv = (1.0 / (np.float32(10000.0) ** np.linspace(0.0, 1.0, 128, dtype=np.float32))).astype(np.float32)
    ang = (inv[:, None] * pos[None, :]).astype(np.float32)
    c["c_cos"] = np.cos(ang.astype(np.float64)).astype(np.float32)
    c["c_sin"] = np.sin(ang.astype(np.float64)).astype(np.float32)
    return c


_CACHE = {}


def kernel(**inputs):
    NCH = inputs["x_prompt"].shape[1] // 128
    NB = inputs["x_prompt"].shape[0]
    if NCH not in _CACHE:
        _CACHE[NCH] = build(NCH)[0]
    nc = _CACHE[NCH]
    cs = consts(NCH)
    f = lambda a: np.ascontiguousarray(np.asarray(a, dtype=np.float32))
    shared = {k: f(inputs[k]) for k in ["meta_tokens", "ln_in_g", "ln_in_b", "w_in", "conv_w", "conv_b", "dt_bias", "a_log",
                                        "d_skip", "m_norm_w", "hgrn_lb_logits", "h_norm_w", "w_br_m", "w_br_h", "w_br_r",
                                        "w_out", "w_ffn_in", "w_ffn_out", "ln1_g", "ln1_b", "ln2_g", "ln2_b"]}
    shared.update(cs)
    in_maps = []
    for c in range(NB):
        m = dict(shared)
        m["x_prompt"] = f(inputs["x_prompt"][c])
        m["x_sample"] = f(inputs["x_sample"][c * NS:(c + 1) * NS, 0])
        for nm in ["state_ssm", "state_conv", "state_hgrn", "state_ret"]:
            m[nm] = f(inputs[nm][:, c * NS:(c + 1) * NS])
        in_maps.append(m)
    res = run_bass_kernel_spmd(nc, in_maps, core_ids=list(range(NB))).results
    y_prompt = np.stack([r["y_prompt"] for r in res], 0)
    y_sample = np.concatenate([r["y_sample"] for r in res], 0)[:, None, :]
    outs = [y_prompt, y_sample]
    for nm in ["nssm_p", "nconv_p", "nhgrn_p", "nret_p"]:
        outs.append(np.stack([r[nm] for r in res], 1))
    for nm in ["nssm_s", "nconv_s", "nhgrn_s", "nret_s"]:
        outs.append(np.concatenate([r[nm] for r in res], 1))
    return tuple(np.ascontiguousarray(o, dtype=np.float32) for o in outs)
```

```python
import numpy as np
import concourse.bass as bass
import concourse.mybir as mybir
from concourse.bass_utils import run_bass_kernel_spmd

F32 = mybir.dt.float32
BF16 = mybir.dt.bfloat16
AF = mybir.ActivationFunctionType
ALU = mybir.AluOpType
AX = mybir.AxisListType

D = 1024
KC = 8
DEPTH = 2
NS = 16
NMETA = 16
IN_DIM = 16400
DFF = 2816
ALPHA = float((2 * DEPTH) ** 0.25)
EPS_LN = 1e-5
DSIZE = {F32: 4, BF16: 2}

ENGS = ["pe", "act", "dve", "pool", "sp"]


class Buf:
    def __init__(self, name, off, nbytes, space):
        self.name, self.off, self.nbytes, self.space = name, off, nbytes, space
        self.recs = []


class Prog:
    def __init__(self, nc, sb_bytes, n_dma_sems=24):
        self.nc = nc
        self.sb_bytes = sb_bytes
        self.q = {e: [] for e in ENGS}
        self.cnt = {e: 0 for e in ENGS}
        self.waited = {e: {} for e in ENGS}
        self.sems = {}
        self.dma_tot = [0] * n_dma_sems
        self.n_dma_sems = n_dma_sems
        self.dma_rr = 0
        self.bufs = []
        self.freed = []
        self.top = 0
        self.stack = []
        self.out_events = []
        self.phases = []

    def phase(self, name):
        self.phases.append((name, dict(self.cnt)))

    def alloc(self, name, nbytes):
        nbytes = (nbytes + 63) // 64 * 64
        b = Buf(name, self.top, nbytes, "sb")
        self.top += nbytes
        assert self.top <= self.sb_bytes, (name, self.top, self.sb_bytes)
        ev = {}
        keep = []
        for (o, n, evs) in self.freed:
            if o < b.off + nbytes and b.off < o + n:
                for k, v in evs.items():
                    ev[k] = max(ev.get(k, 0), v)
                if not (b.off <= o and o + n <= b.off + nbytes):
                    keep.append((o, n, evs))
            else:
                keep.append((o, n, evs))
        self.freed = keep
        for k, v in ev.items():
            b.recs.append([0, 128, b.off, b.off + nbytes, "W", "seed", k, v])
        self.bufs.append(b)
        return b

    def mark(self):
        self.stack.append((self.top, len(self.bufs)))

    def release(self):
        top, nb = self.stack.pop()
        for b in self.bufs[nb:]:
            evs = {}
            for r in b.recs:
                evs[r[6]] = max(evs.get(r[6], 0), r[7])
            if evs:
                self.freed.append((b.off, b.nbytes, evs))
        self.bufs = self.bufs[:nb]
        self.top = top

    def view(self, buf, dtype, shape, boff=0):
        es = DSIZE[dtype]
        n = int(np.prod(shape))
        assert boff + n * es <= buf.nbytes, (buf.name, boff, n * es, buf.nbytes)
        if buf.space == "sb":
            base = self.sb
            start = (buf.off + boff) // 4
            words = (n * es + 3) // 4
            ap = base[:, start:start + words]
        else:
            base = self.ps
            start = (buf.off + boff) // 4
            words = (n * es + 3) // 4
            ap = base[:, start:start + words]
        if dtype != F32:
            ap = ap.bitcast(dtype)
            ap = ap[:, 0:n]
        if len(shape) == 2:
            ap = ap.rearrange("p (a b) -> p a b", a=shape[0])
        elif len(shape) == 3:
            ap = ap.rearrange("p (a b c) -> p a b c", a=shape[0], b=shape[1])
        return ap

    def _range(self, ap):
        t = ap.tensor
        es = DSIZE[ap.dtype]
        aps = ap.ap
        pstep = aps[0][0]
        off = int(ap.offset)
        if pstep == 0:
            pstep = self.pitch_elems[(t.name, es)]
        plo = off // pstep
        phi = plo + aps[0][1]
        flo = off % pstep
        fhi = flo + 1
        for (s, c) in aps[1:]:
            fhi += (c - 1) * abs(s)
        return plo, phi, flo * es, fhi * es

    def _find(self, ap):
        name = ap.tensor.name
        plo, phi, blo, bhi = self._range(ap)
        if name == "PS":
            b0 = blo // 2048
            b1 = (bhi - 1) // 2048
            return [(self.psbanks[b], 0, 128, b * 2048, (b + 1) * 2048) for b in range(b0, b1 + 1)]
        if name != "SB":
            return []
        res = []
        for b in self.bufs:
            if b.off < bhi and blo < b.off + b.nbytes:
                res.append((b, plo, phi, max(blo, b.off), min(bhi, b.off + b.nbytes)))
        assert res, ("no buf for ap", name, blo, bhi)
        return res

    def _deps(self, eng, accesses, is_dma):
        need = {}
        touched = []
        for ap, kind in accesses:
            for (b, plo, phi, blo, bhi) in self._find(ap):
                ps = b.space == "ps"
                for r in b.recs:
                    if r[0] < phi and plo < r[1] and r[2] < bhi and blo < r[3]:
                        if not (ps or r[4] == "W" or kind == "W"):
                            continue
                        if r[5] == eng and not is_dma:
                            if eng == "pe":
                                continue
                            if r[4] == "R" and not ps:
                                continue
                            if r[4] == "R" and ps and kind == "R" and eng != "dve":
                                continue
                        need[r[6]] = max(need.get(r[6], 0), r[7])
                touched.append((b, plo, phi, blo, bhi, kind))
        return need, touched

    def _record(self, touched, eng, key, val):
        for (b, plo, phi, blo, bhi, kind) in touched:
            if kind == "W":
                b.recs = [r for r in b.recs
                          if not (plo <= r[0] and r[1] <= phi and blo <= r[2] and r[3] <= bhi)]
                b.recs.append([plo, phi, blo, bhi, "W", eng, key, val])
            else:
                for r in b.recs:
                    if r[4] == "R" and r[5] == eng and r[6] == key and r[0] == plo and r[1] == phi \
                            and r[2] == blo and r[3] == bhi:
                        r[7] = val
                        break
                else:
                    b.recs.append([plo, phi, blo, bhi, "R", eng, key, val])

    def _emit_waits(self, eng, need):
        w = self.waited[eng]
        for k, v in need.items():
            if w.get(k, 0) >= v:
                continue
            w[k] = v
            self.q[eng].append(("wait", k, v))

    def I(self, eng, meth, reads=(), writes=(), **kw):
        acc = []
        for k, v in kw.items():
            if isinstance(v, bass.AP):
                if v.tensor.name not in ("SB", "PS"):
                    continue
                acc.append((v, "W" if k in ("out", "accum_out", "ap") else "R"))
        for a in reads:
            acc.append((a, "R"))
        for a in writes:
            acc.append((a, "W"))
        need, touched = self._deps(eng, acc, False)
        self._emit_waits(eng, need)
        self.cnt[eng] += 1
        self._record(touched, eng, eng, self.cnt[eng])
        self.q[eng].append(("ins", meth, kw))

    def dma(self, eng, pairs, slow=False):
        acc = []
        for (o, i) in pairs:
            if o.tensor.name == "SB":
                acc.append((o, "W"))
            if i.tensor.name == "SB":
                acc.append((i, "R"))
        need, touched = self._deps(eng, acc, True)
        s = self.dma_rr
        self.dma_rr = (self.dma_rr + 1) % self.n_dma_sems
        key = ("dma", s)
        need[key] = max(need.get(key, 0), self.dma_tot[s])
        self._emit_waits(eng, need)
        self.dma_tot[s] += 16 * len(pairs)
        self._record(touched, "dma", key, self.dma_tot[s])
        self.q[eng].append(("dma", s, pairs, slow))
        return (key, self.dma_tot[s])

    def final_wait(self, eng="sp"):
        need = {("dma", s): self.dma_tot[s] for s in range(self.n_dma_sems) if self.dma_tot[s]}
        for e in ("pe", "act", "dve", "pool"):
            if self.cnt[e]:
                need[e] = self.cnt[e]
        self._emit_waits(eng, need)

    def lower(self, block):
        nc = self.nc
        engmap = {"pe": "tensor", "act": "scalar", "dve": "vector", "pool": "gpsimd", "sp": "sync"}

        def semof(k):
            return self.sems[k]

        def run(ename, engine):
            for it in self.q[ename]:
                if it[0] == "wait":
                    engine.wait_ge(semof(it[1]), it[2])
                elif it[0] == "ins":
                    getattr(engine, it[1])(**it[2]).then_inc(semof(ename), 1)
                else:
                    s = semof(("dma", it[1]))
                    for (o, i) in it[2]:
                        if it[3]:
                            engine.dma_start(out=o, in_=i, allow_slow_non_contiguous=True).then_inc(s, 16)
                        else:
                            engine.dma_start(out=o, in_=i).then_inc(s, 16)

        for ename in ENGS:
            if not self.q[ename]:
                continue
            dec = getattr(block, engmap[ename])

            def mk(en):
                def f(engine):
                    run(en, engine)
                return f
            dec(mk(ename))


O_MZ, O_MX, O_MB, O_MC, O_MDT = 0, 1024, 2048, 2560, 3072
O_HQ, O_HF, O_HI, O_HG = 3088, 4112, 5136, 6160
O_RQ, O_RK, O_RV, O_RG = 7184, 8208, 9232, 11280
O_GATE = 13328
GAMMA = [1.0 - 2.0 ** (-5.0 - h) for h in range(4)]


def build(NCH, layers=DEPTH):
    SEQ = 128 * NCH
    NTP = NMETA + SEQ
    NT = NTP + NS
    nc = bass.Bass("TRN2", target_bir_lowering=False, dynamic_dma_scratch_size=8192)
    dr = {}

    def din(name, shape, dt=F32):
        dr[name] = nc.dram_tensor(name, list(shape), dt, kind="ExternalInput").ap()

    def dout(name, shape, dt=F32):
        dr[name] = nc.dram_tensor(name, list(shape), dt, kind="ExternalOutput").ap()

    din("x_prompt", [SEQ, D]); din("x_sample", [NS, D]); din("meta_tokens", [NMETA, D])
    din("state_ssm", [DEPTH, NS, 16, 64, 128]); din("state_conv", [DEPTH, NS, 3, 2048])
    din("state_hgrn", [DEPTH, NS, 8, 128, 128]); din("state_ret", [DEPTH, NS, 4, 256, 512])
    din("ln_in_g", [D]); din("ln_in_b", [D])
    din("w_in", [DEPTH, D, IN_DIM]); din("conv_w", [DEPTH, 4, 2048]); din("conv_b", [DEPTH, 2048])
    din("dt_bias", [DEPTH, 16]); din("a_log", [DEPTH, 16]); din("d_skip", [DEPTH, 16])
    din("m_norm_w", [DEPTH, D]); din("hgrn_lb_logits", [DEPTH, D]); din("h_norm_w", [DEPTH, D])
    din("w_br_m", [DEPTH, D, D]); din("w_br_h", [DEPTH, D, D]); din("w_br_r", [DEPTH, 2 * D, D])
    din("w_out", [DEPTH, D, D])
    din("w_ffn_in", [DEPTH, D, 2 * DFF]); din("w_ffn_out", [DEPTH, DFF, D])
    din("ln1_g", [DEPTH, D]); din("ln1_b", [DEPTH, D]); din("ln2_g", [DEPTH, D]); din("ln2_b", [DEPTH, D])
    for nm in ["c_ident", "c_ones", "c_triu", "c_strl", "c_bd"]:
        din(nm, [128, 128])
    din("c_rowm", [128, 4]); din("c_eye16", [128, 256])
    din("c_dmask", [4, 128, 128]); din("c_g1", [4, 128, 128]); din("c_kd", [4, 128, 2])
    din("c_cos", [128, NT]); din("c_sin", [128, NT])
    dout("y_prompt", [SEQ, D]); dout("y_sample", [NS, D])
    dout("nssm_p", [DEPTH, 16, 64, 128]); dout("nconv_p", [DEPTH, 3, 2048])
    dout("nhgrn_p", [DEPTH, 8, 128, 128]); dout("nret_p", [DEPTH, 4, 256, 512])
    dout("nssm_s", [DEPTH, NS, 16, 64, 128]); dout("nconv_s", [DEPTH, NS, 3, 2048])
    dout("nhgrn_s", [DEPTH, NS, 8, 128, 128]); dout("nret_s", [DEPTH, NS, 4, 256, 512])

    SB_BYTES = 184 * 1024
    P = Prog(nc, SB_BYTES)
    P.pitch_elems = {}

    with (
        nc.sbuf_tensor("SB", [128, SB_BYTES // 4], F32) as SBt,
        nc.psum_tensor("PS", [128, 8 * 512], F32) as PSt,
    ):
        P.sb = SBt[:]
        P.ps = PSt[:]
        P.pitch_elems[("SB", 4)] = SBt[:].ap[0][0]
        P.pitch_elems[("SB", 2)] = SBt[:].ap[0][0] * 2
        P.pitch_elems[("PS", 4)] = PSt[:].ap[0][0]
        P.pitch_elems[("PS", 2)] = PSt[:].ap[0][0] * 2
        P.psbanks = [Buf(f"bank{b}", b * 2048, 2048, "ps") for b in range(8)]

        def bank(b, dtype=F32, shape=None):
            if shape is None:
                shape = [512] if dtype == F32 else [1024]
            return P.view(P.psbanks[b], dtype, shape)

        def A(name, dtype, shape):
            b = P.alloc(name, int(np.prod(shape)) * DSIZE[dtype])
            return P.view(b, dtype, list(shape))

        def mm(out, lhsT, rhs, start=True, stop=True):
            P.I("pe", "matmul", out=out, lhsT=lhsT, rhs=rhs, start=start, stop=stop)

        def tr(out, in_, idn):
            P.I("pe", "transpose", out=out, in_=in_, identity=idn)

        def act(out, in_, func, **kw):
            P.I("act", "activation", out=out, in_=in_, func=func, **kw)

        def tt(out, in0, in1, op, eng="dve"):
            P.I(eng, "tensor_tensor", out=out, in0=in0, in1=in1, op=op)

        def ts(out, in0, s1, s2, op0, op1=None, eng="dve"):
            if op1 is None:
                P.I(eng, "tensor_scalar", out=out, in0=in0, scalar1=s1, scalar2=None, op0=op0)
            else:
                P.I(eng, "tensor_scalar", out=out, in0=in0, scalar1=s1, scalar2=s2, op0=op0, op1=op1)

        def stt(out, in0, scalar, in1, op0, op1, **kw):
            P.I("dve", "scalar_tensor_tensor", out=out, in0=in0, scalar=scalar, in1=in1, op0=op0, op1=op1, **kw)

        def sigm(out, in_):
            act(out, in_, AF.Exp, scale=-1.0)
            act(out, out, AF.Ln, bias=1.0, scale=1.0)
            act(out, out, AF.Exp, scale=-1.0)

        def cp(out, in_, eng="dve"):
            if eng == "act_copy":
                P.I("act", "activation", out=out, in_=in_, func=AF.Copy)
            else:
                P.I(eng, "tensor_copy", out=out, in_=in_)

        xT = A("xT", F32, [KC, NT])
        xbf = A("xbf", BF16, [KC, NT])
        yT = A("yT", BF16, [KC, NT])
        ident = A("ident", F32, [128]); ones = A("ones", F32, [128]); triu = A("triu", F32, [128])
        strl = A("strl", F32, [128]); bdm = A("bdm", F32, [128]); rowm = A("rowm", F32, [4])
        eye16 = A("eye16", BF16, [16, 16]); identb = A("identb", BF16, [128]); onesb = A("onesb", BF16, [128])
        lnp = A("lnp", F32, [2 + 4 * DEPTH, KC])
        P.dma("sp", [(ident, dr["c_ident"]), (ones, dr["c_ones"]), (triu, dr["c_triu"]), (strl, dr["c_strl"]),
                     (bdm, dr["c_bd"]), (rowm, dr["c_rowm"])])
        P.dma("pool", [(eye16, dr["c_eye16"].rearrange("p (a b) -> p a b", a=16))])
        prs = [(lnp[:, 0, :], dr["ln_in_g"].rearrange("(k p) -> p k", p=128)),
               (lnp[:, 1, :], dr["ln_in_b"].rearrange("(k p) -> p k", p=128))]
        for l in range(DEPTH):
            for j, nm in enumerate(["ln1_g", "ln1_b", "ln2_g", "ln2_b"]):
                prs.append((lnp[:, 2 + 4 * l + j, :], dr[nm][l].rearrange("(k p) -> p k", p=128)))
        P.dma("sp", prs, slow=True)
        cp(identb, ident)
        cp(onesb, ones)

        ttiles = [(0, NMETA, dr["meta_tokens"][:, :])]
        for c in range(NCH):
            ttiles.append((NMETA + 128 * c, 128, dr["x_prompt"][128 * c:128 * (c + 1), :]))
        ttiles.append((NTP, NS, dr["x_sample"][:, :]))
        pchunks = [(c0_, T_) for (c0_, T_, _) in ttiles[:-1]]

        def mkblocks(lo, hi):
            res = []
            c0 = lo
            while c0 < hi:
                n = min(512, hi - c0)
                res.append((c0, n))
                c0 += n
            return res
        tblocks = mkblocks(0, NT)

        P.mark()
        tin = [A(f"tin{i}", F32, [D]) for i in range(2)]
        tnr = [A(f"tnr{i}", F32, [D]) for i in range(2)]
        lst = [A(f"lnst{i}", F32, [16]) for i in range(2)]
        for i, (col0, T, src) in enumerate(ttiles):
            a, nr = tin[i % 2], tnr[i % 2]
            stats = lst[i % 2][:, 0:12].rearrange("p (a b) -> p a b", a=2)
            mv = lst[i % 2][:, 12:16]
            P.dma("sp", [(a[:T], src)])
            P.I("dve", "bn_stats", out=stats[:T, 0, :], in_=a[:T, 0:512])
            P.I("dve", "bn_stats", out=stats[:T, 1, :], in_=a[:T, 512:1024])
            P.I("dve", "bn_aggr", out=mv[:T, 0:2], in_=lst[i % 2][:T, 0:12])
            act(mv[:T, 2:3], mv[:T, 1:2], AF.Ln, bias=EPS_LN, scale=1.0)
            act(mv[:T, 3:4], mv[:T, 2:3], AF.Exp, scale=-0.5)
            ts(nr[:T], a[:T], mv[:T, 0:1], mv[:T, 3:4], ALU.subtract, ALU.mult)
            pb = 2 * (i % 2)
            for kc in range(KC):
                tr(bank(pb + kc // 4)[:, (kc % 4) * 128:(kc % 4) * 128 + T], nr[:T, kc * 128:(kc + 1) * 128], ident[:T, :T])
            for kc in range(KC):
                act(xT[:, kc, col0:col0 + T], bank(pb + kc // 4)[:, (kc % 4) * 128:(kc % 4) * 128 + T],
                    AF.Identity, scale=lnp[:, 0, kc:kc + 1], bias=lnp[:, 1, kc:kc + 1])
            cp(xbf[:, :, col0:col0 + T], xT[:, :, col0:col0 + T])
        P.release()

        def layer_norm_fm(gi, bi, last=False):
            P.phase(f"ln{gi}")
            P.mark()
            sq = A("lnsq", F32, [2, 512]); mmb = A("lnm", F32, [4, 512]); ttb = A("lnt", F32, [2, 512])
            for (c0, n) in tblocks:
                s1, s2 = bank(4), bank(5)
                for kc in range(KC):
                    mm(s1[:, :n], ones, xT[:, kc, c0:c0 + n], kc == 0, kc == KC - 1)
                for kc in range(KC):
                    act(sq[:, kc % 2, :n], xT[:, kc, c0:c0 + n], AF.Square)
                    mm(s2[:, :n], ones, sq[:, kc % 2, :n], kc == 0, kc == KC - 1)
                mean, msq, var, rstd = mmb[:, 0, :n], mmb[:, 1, :n], mmb[:, 2, :n], mmb[:, 3, :n]
                act(mean, s1[:, :n], AF.Copy, scale=1.0 / D)
                act(msq, s1[:, :n], AF.Square, scale=1.0 / D)
                stt(var, s2[:, :n], 1.0 / D, msq, ALU.mult, ALU.subtract)
                act(var, var, AF.Ln, bias=EPS_LN, scale=1.0)
                act(rstd, var, AF.Exp, scale=-0.5)
                for kc in range(KC):
                    t = ttb[:, kc % 2, :n]
                    tt(t, xT[:, kc, c0:c0 + n], mean, ALU.subtract)
                    tt(t, t, rstd, ALU.mult)
                    act(xT[:, kc, c0:c0 + n], t, AF.Identity, scale=lnp[:, gi, kc:kc + 1], bias=lnp[:, bi, kc:kc + 1])
                    if not last:
                        cp(xbf[:, kc, c0:c0 + n], xT[:, kc, c0:c0 + n], eng="pool")
            P.release()

        def ffn(l):
            P.phase(f"ffn{l}")
            P.mark()
            passes = [(0, 8), (8, 15), (15, 22)]
            hT = yT
            wgs = [A(f"wg{i}", BF16, [KC, 128]) for i in range(2)]
            wus = [A(f"wu{i}", BF16, [KC, 128]) for i in range(2)]
            wos = [A(f"wo{i}", BF16, [8, 128]) for i in range(2)]
            sgs = [A(f"sg{i}", F32, [512]) for i in range(2)]
            win = dr["w_ffn_in"][l].rearrange("(k p) c -> p k c", p=128)
            wout = dr["w_ffn_out"][l].rearrange("(j p) c -> p j c", p=128)
            it = 0
            for ps_, (j0, j1) in enumerate(passes):
                TPP = j1 - j0
                for j in range(TPP):
                    jt = j0 + j
                    wg, wu = wgs[jt % 2], wus[jt % 2]
                    P.dma("pool", [(wg, win[:, :, jt * 128:(jt + 1) * 128]),
                                   (wu, win[:, :, DFF + jt * 128:DFF + (jt + 1) * 128])])
                    for (c0, n) in tblocks:
                        pg, pu, sg = bank(it % 2), bank(2 + it % 2), sgs[it % 2]
                        it += 1
                        for kc in range(KC):
                            mm(pg[:, :n], wg[:, kc, :], xbf[:, kc, c0:c0 + n], kc == 0, kc == KC - 1)
                        for kc in range(KC):
                            mm(pu[:, :n], wu[:, kc, :], xbf[:, kc, c0:c0 + n], kc == 0, kc == KC - 1)
                        act(sg[:, :n], pg[:, :n], AF.Silu)
                        tt(hT[:, j, c0:c0 + n], sg[:, :n], pu[:, :n], ALU.mult)
                for do in range(KC):
                    wo = wos[do % 2]
                    P.dma("pool", [(wo[:, 0:TPP, :], wout[:, j0:j1, do * 128:(do + 1) * 128])])
                    for (c0, n) in tblocks:
                        po = bank(6 + it % 2)
                        it += 1
                        for j in range(TPP):
                            mm(po[:, :n], wo[:, j, :], hT[:, j, c0:c0 + n], j == 0, j == TPP - 1)
                        if ps_ == 0:
                            stt(xT[:, do, c0:c0 + n], xT[:, do, c0:c0 + n], ALPHA, po[:, :n], ALU.mult, ALU.add)
                        else:
                            tt(xT[:, do, c0:c0 + n], xT[:, do, c0:c0 + n], po[:, :n], ALU.add)
            P.release()

        def run_lanes(gens):
            gens = list(gens)
            while gens:
                for g_ in list(gens):
                    try:
                        next(g_)
                    except StopIteration:
                        gens.remove(g_)

        def finish(ysrc, T, F, gate, ychunks, col0, junk, tmp_bf, sc, pt):
            act(junk[:T, :F], ysrc, AF.Square, accum_out=sc[:T, 0:1])
            act(sc[:T, 1:2], sc[:T, 0:1], AF.Ln, scale=1.0 / F, bias=1e-6)
            act(sc[:T, 2:3], sc[:T, 1:2], AF.Exp, scale=-0.5)
            yield
            stt(tmp_bf[:T, :F], ysrc, sc[:T, 2:3], gate, ALU.mult, ALU.mult)
            yield
            for j in range(F // 128):
                tr(pt[:, j * 128:j * 128 + T], tmp_bf[:T, j * 128:(j + 1) * 128], identb[:T, :T])
            yield
            for j, yc in enumerate(ychunks):
                cp(yT[:, yc, col0:col0 + T], pt[:, j * 128:j * 128 + T], eng="dve" if j % 2 else "act_copy")
            yield

        wv = lambda l: dr["w_in"][l].rearrange("(k p) c -> p k c", p=128)

        def ssd_unit(l, g):
            P.phase(f"ssd{l}.{g}")
            P.mark()
            slab = A("slab", BF16, [KC, 772])
            W = wv(l)
            P.dma("pool", [(slab[:, :, 0:256], W[:, :, O_MX + g * 256:O_MX + (g + 1) * 256]),
                           (slab[:, :, 256:384], W[:, :, O_MB + g * 128:O_MB + (g + 1) * 128]),
                           (slab[:, :, 384:512], W[:, :, O_MC + g * 128:O_MC + (g + 1) * 128]),
                           (slab[:, :, 512:768], W[:, :, O_MZ + g * 256:O_MZ + (g + 1) * 256]),
                           (slab[:, :, 768:772], W[:, :, O_MDT + 4 * g:O_MDT + 4 * g + 4])])
            cw = A("cw", F32, [4, 4]); cb = A("cb", F32, [4])
            cbase = [g * 256, g * 256 + 128, 1024 + g * 128, 1536 + g * 128]
            prs = []
            for j in range(4):
                prs.append((cw[:, j, :], dr["conv_w"][l].rearrange("k c -> c k")[cbase[j]:cbase[j] + 128, :]))
                prs.append((cb[:, j:j + 1], dr["conv_b"][l].rearrange("(c o) -> c o", o=1)[cbase[j]:cbase[j] + 128, :]))
            P.dma("sp", prs, slow=True)
            hp = A("hp", F32, [3, 4])
            mw = A("mw", F32, [256])
            P.dma("sp", [(hp[:, 0, :], dr["dt_bias"][l, 4 * g:4 * g + 4].partition_broadcast(128)),
                         (hp[:, 1, :], dr["a_log"][l, 4 * g:4 * g + 4].partition_broadcast(128)),
                         (hp[:, 2, :], dr["d_skip"][l, 4 * g:4 * g + 4].partition_broadcast(128)),
                         (mw, dr["m_norm_w"][l, g * 256:(g + 1) * 256].partition_broadcast(128))])
            act(hp[:, 1, :], hp[:, 1, :], AF.Exp)
            ts(hp[:, 1, :], hp[:, 1, :], -1.0, None, ALU.mult)
            dsk_bc = hp[:, 2, :].unsqueeze(2).to_broadcast([128, 4, 64])
            S = A("S", F32, [256]); Sbf = A("Sbf", BF16, [256])
            xt3 = lambda a, T: a[:T, :].rearrange("p (h q) -> p h q", h=4)

            class Lane:
                pass

            def mklane(bk):
                L = Lane()
                L.bk = bk
                L.raw = A("raw", F32, [4, 131]); L.cv = A("cv", F32, [4, 128]); L.xcb = A("xcb", BF16, [4, 128])
                L.xtok = A("xtok", BF16, [256]); L.xdt = A("xdt", BF16, [256]); L.xw = A("xw", BF16, [256]); L.btok = A("btok", BF16, [128])
                L.dtt = A("dtt", F32, [16]); L.cc = A("cc", F32, [16])
                L.cbm = A("cbm", F32, [128]); L.Lh = A("Lh", F32, [4, 128]); L.dec = A("dec", F32, [4, 128]); L.wT = A("wT", BF16, [4, 128])
                L.yi = A("yi", F32, [256]); L.tmpf = A("tmpf", F32, [256]); L.zs = A("zs", F32, [256]); L.ynb = A("ynb", BF16, [256])
                L.sc = A("sc", F32, [4])
                return L

            def proj(L, col0, T):
                pf = bank(L.bk)
                for j in range(4):
                    for kc in range(KC):
                        mm(pf[:, j * 128:j * 128 + T], slab[:, kc, j * 128:(j + 1) * 128], xbf[:, kc, col0:col0 + T],
                           kc == 0, kc == KC - 1)
                pz = bank(L.bk + 1)
                for kc in range(KC):
                    mm(pz[:T, 0:260], xbf[:, kc, col0:col0 + T], slab[:, kc, 512:772], kc == 0, kc == KC - 1)
                return pf, pz

            def dt_calc(L, pz, T):
                dtt = L.dtt
                tt(dtt[:T, 8:12], pz[:T, 256:260], hp[:T, 0, :], ALU.add)
                act(dtt[:T, 8:12], dtt[:T, 8:12], AF.Exp)
                act(dtt[:T, 0:4], dtt[:T, 8:12], AF.Ln, bias=1.0, scale=1.0)
                tt(dtt[:T, 4:8], dtt[:T, 0:4], hp[:T, 1, :], ALU.mult)

            def conv_silu(L, T, taps):
                cv = L.cv
                for j in range(4):
                    act(cv[:, j, :T], taps(j, 0), AF.Identity, scale=cw[:, j, 0:1], bias=cb[:, j:j + 1])
                for k in range(1, 4):
                    for j in range(4):
                        stt(cv[:, j, :T], taps(j, k), cw[:, j, k:k + 1], cv[:, j, :T], ALU.mult, ALU.add)
                sigm(L.dec[:, :, :T], cv[:, :, :T])
                tt(L.xcb[:, :, :T], cv[:, :, :T], L.dec[:, :, :T], ALU.mult)

            def gate_finish(L, pz, T, col0):
                ysb = L.yi
                tt(xt3(L.tmpf, T), xt3(L.xtok, T), dsk_bc[:T], ALU.mult)
                tt(ysb[:T, :], ysb[:T, :], L.tmpf[:T, :], ALU.add)
                sigm(L.zs[:T, :], pz[:T, 0:256])
                tt(ysb[:T, :], ysb[:T, :], pz[:T, 0:256], ALU.mult)
                yield
                tt(ysb[:T, :], ysb[:T, :], L.zs[:T, :], ALU.mult)
                yield
                yield from finish(ysb[:T, :], T, 256, mw[:T, :], [2 * g, 2 * g + 1], col0, L.tmpf, L.ynb, L.sc,
                                  bank(L.bk + 3, BF16)[:, 512:1024])

            def chunk_gen(L, Lprev, ci, col0, T):
                first = ci == 0
                raw, xcb, dtt, cc = L.raw, L.xcb, L.dtt, L.cc
                if first:
                    P.I("dve", "memset", ap=raw[:, :, 0:3], constant=0.0)
                pf, pz = proj(L, col0, T)
                yield
                act(raw[:, :, 3:3 + T], pf[:, :].rearrange("p (a b) -> p a b", a=4)[:, :, :T], AF.Copy)
                dt_calc(L, pz, T)
                yield
                if ci == len(pchunks) - 1:
                    prs = [(dr["nconv_p"][l].rearrange("k c -> c k")[cbase[j]:cbase[j] + 128, :], raw[:, j, T:T + 3])
                           for j in range(4)]
                    P.dma("sp", prs, slow=True)
                if not first:
                    pass
                conv_silu(L, T, lambda j, k: raw[:, j, k:k + T])
                yield
                pc = bank(L.bk + 1)[:, 300:312]
                mm(pc[:T, 0:4], triu[:T, :T], dtt[:T, 4:8])
                mm(pc[:, 4:8], ones[:T, :], dtt[:T, 4:8])
                ptb = bank(L.bk, BF16)
                for j in range(3):
                    tr(ptb[:T, j * 128:(j + 1) * 128], xcb[:, j, :T], identb)
                yield
                if T == 128:
                    cp(cc[:, 0:8], pc[:, 0:8], eng="act_copy")
                else:
                    cp(cc[:T, 0:4], pc[:T, 0:4], eng="act_copy")
                    cp(cc[:, 4:8], pc[:, 4:8], eng="act_copy")
                cp(L.xtok[:T, :], ptb[:T, 0:256], eng="act_copy")
                cp(L.btok[:T, :], ptb[:T, 256:384], eng="act_copy")
                yield
                act(cc[:T, 8:12], cc[:T, 0:4], AF.Exp)
                act(cc[:, 12:16], cc[:, 4:8], AF.Exp)
                tt(dtt[:T, 8:12], cc[:T, 4:8], cc[:T, 0:4], ALU.subtract)
                tt(xt3(L.xdt, T), xt3(L.xtok, T), dtt[:T, 0:4].unsqueeze(2).to_broadcast([T, 4, 64]), ALU.mult)
                yield
                act(dtt[:T, 8:12], dtt[:T, 8:12], AF.Exp)
                pcb = bank(L.bk)
                mm(pcb[:T, :T], xcb[:, 2, :T], xcb[:, 3, :T])
                for h in range(4):
                    act(L.Lh[:T, h, :T], strl[:T, :T], AF.Identity, scale=dtt[:T, 4 + h:5 + h])
                yield
                tt(dtt[:T, 12:16], dtt[:T, 8:12], dtt[:T, 0:4], ALU.mult)
                tt(L.cbm[:T, :T], pcb[:T, :T], triu[:T, :T], ALU.mult)
                yield
                tt(xt3(L.xw, T), xt3(L.xtok, T), dtt[:T, 12:16].unsqueeze(2).to_broadcast([T, 4, 64]), ALU.mult)
                pd = bank(L.bk)
                for h in range(4):
                    mm(pd[:T, h * 128:h * 128 + T], L.Lh[:T, h, :T], triu[:T, :T])
                yield
                pyy = bank(L.bk + 2)
                if not first:
                    mm(pyy[:T, 256:512], xcb[:, 3, :T], Sbf[:, :])
                pS = bank(L.bk + 3)
                mm(pS[:, 0:256], L.btok[:T, :], L.xw[:T, :])
                act(L.dec[:T, :, :T], pd[:T, :].rearrange("p (a b) -> p a b", a=4)[:, :, :T], AF.Exp)
                yield
                if not first:
                    tt(xt3(L.tmpf, T), pyy[:T, 256:512].rearrange("p (h q) -> p h q", h=4),
                       cc[:T, 8:12].unsqueeze(2).to_broadcast([T, 4, 64]), ALU.mult)
                if first:
                    cp(S[:, :], pS[:, 0:256])
                else:
                    tt(S[:, :].rearrange("p (h q) -> p h q", h=4), S[:, :].rearrange("p (h q) -> p h q", h=4),
                       cc[:, 12:16].unsqueeze(2).to_broadcast([128, 4, 64]), ALU.mult)
                    tt(S[:, :], S[:, :], pS[:, 0:256], ALU.add)
                cp(Sbf[:, :], S[:, :])
                yield
                tt(L.wT[:T, :, :T], L.dec[:T, :, :T], L.cbm[:T, :T].unsqueeze(1).to_broadcast([T, 4, T]), ALU.mult)
                yield
                for h in range(4):
                    mm(pyy[:T, h * 64:(h + 1) * 64], L.wT[:T, h, :T], L.xdt[:T, h * 64:(h + 1) * 64])
                yield
                if first:
                    cp(L.yi[:T, :], pyy[:T, 0:256], eng="act_copy")
                else:
                    tt(L.yi[:T, :], pyy[:T, 0:256], L.tmpf[:T, :], ALU.add)
                yield
                yield from gate_finish(L, pz, T, col0)
                if ci == len(pchunks) - 1:
                    pt = bank(L.bk + 2)
                    for j in range(2):
                        tr(pt[:, j * 128:(j + 1) * 128], S[:, j * 128:(j + 1) * 128], ident)
                    cp(L.tmpf[:, :], pt[:, 0:256])
                    P.dma("sp", [(dr["nssm_p"][l, 4 * g:4 * g + 4].rearrange("h p n -> (h p) n").rearrange("(t q) n -> q t n", q=128),
                                  L.tmpf[:, :].rearrange("p (t n) -> p t n", t=2))])

            P.mark()
            lanes = [mklane(0), mklane(4)]

            def lane_gen(li):
                L = lanes[li]
                for ci in range(li, len(pchunks), 2):
                    col0, T = pchunks[ci]
                    if ci > 0:
                        Lp = lanes[1 - li]
                        Tp = pchunks[ci - 1][1]
                        cp(L.raw[:, :, 0:3], Lp.raw[:, :, Tp:Tp + 3], eng="pool")
                    yield from chunk_gen(L, None, ci, col0, T)
            def skewed(li):
                if li == 1:
                    for _ in range(10):
                        yield
                yield from lane_gen(li)
            run_lanes([skewed(0), skewed(1)])
            P.release()

            P.phase("sample")
            T = NS
            col0 = NTP
            L = mklane(0)
            raw, xcb, dtt = L.raw, L.xcb, L.dtt
            pf, pz = proj(L, col0, T)
            act(raw[:, :, 3:3 + T], pf[:, :].rearrange("p (a b) -> p a b", a=4)[:, :, :T], AF.Copy)
            pxb = bank(4)
            for kc in range(KC):
                mm(pxb[:T, :], xbf[:, kc, col0:col0 + T], slab[:, kc, 0:512], kc == 0, kc == KC - 1)
            cst = A("cst", F32, [3, 512]); rtk = A("rtk", F32, [512]); halo = A("halo", F32, [4, 3, 16])
            cp(rtk[:T, :], pxb[:T, :], eng="act_copy")
            P.dma("sp", [(cst[:T, :, j * 128:(j + 1) * 128], dr["state_conv"][l, :, :, cbase[j]:cbase[j] + 128]) for j in range(4)])
            P.dma("sp", [(dr["nconv_s"][l, :, 0:2, cbase[j]:cbase[j] + 128], cst[:T, 1:3, j * 128:(j + 1) * 128]) for j in range(4)] +
                        [(dr["nconv_s"][l, :, 2, cbase[j]:cbase[j] + 128], rtk[:T, j * 128:(j + 1) * 128]) for j in range(4)])
            ph = bank(5)
            for j in range(4):
                for k in range(3):
                    tr(ph[:, (j * 3 + k) * 16:(j * 3 + k + 1) * 16], cst[:T, k, j * 128:(j + 1) * 128], ident[:T, :T])
            cp(halo[:, :, :, :], ph[:, 0:192].rearrange("p (a b c) -> p a b c", a=4, b=3))
            conv_silu(L, T, lambda j, k: halo[:, j, k, :] if k < 3 else raw[:, j, 3:3 + T])
            dt_calc(L, pz, T)
            act(dtt[:T, 8:12], dtt[:T, 4:8], AF.Exp)
            ptb = bank(2, BF16)
            for j in range(4):
                tr(ptb[:T, j * 128:(j + 1) * 128], xcb[:, j, :T], identb)
            cp(L.xtok[:T, :], ptb[:T, 0:256], eng="act_copy")
            bct = A("bct", F32, [256]); bcm = A("bcm", BF16, [256])
            cp(bct[:T, :], ptb[:T, 256:512], eng="act_copy")
            xe = A("xe", F32, [512])
            tt(xe[:T, 0:256].rearrange("p (h q) -> p h q", h=4), xt3(L.xtok, T),
               dtt[:T, 0:4].unsqueeze(2).to_broadcast([T, 4, 64]), ALU.mult)
            cp(xe[:T, 256:512].rearrange("p (h q) -> p h q", h=4), dtt[:T, 8:12].unsqueeze(2).to_broadcast([T, 4, 64]))
            pq = bank(3)
            for j in range(4):
                tr(pq[:, j * 16:(j + 1) * 16], xe[:T, j * 128:(j + 1) * 128], ident[:T, :T])
            scs = A("scs", F32, [4, 16])
            cp(scs[:, :, :], pq[:, 0:64].rearrange("p (a b) -> p a b", a=4))
            sts = [A(f"st{i}", F32, [2, 128]) for i in range(2)]
            stns = [A(f"stn{i}", F32, [2, 128]) for i in range(2)]
            yfm = A("yfm", F32, [2, 16]); tm2s = [A(f"tm2{i}", F32, [128]) for i in range(2)]
            junks = [A(f"junk{i}", F32, [128]) for i in range(2)]
            bcms = [bcm, A("bcm1", BF16, [256])]

            def samp_gen(par):
                st, stn, tm2, junk, bcm_ = sts[par], stns[par], tm2s[par], junks[par], bcms[par]
                for b in range(par, NS, 2):
                    sview = lambda nm: dr[nm][l, b, 4 * g:4 * g + 4].rearrange("h p n -> (h p) n").rearrange("(t q) n -> q t n", q=128)
                    P.dma("sp", [(st[:, :, :], sview("state_ssm"))])
                    ts(bcm_[:T, :], bct[:T, :], ident[:T, b:b + 1], None, ALU.mult)
                    yield
                    pb_ = bank(6 + par)
                    mm(pb_[:, 0:256], onesb[:T, :], bcm_[:T, :])
                    yield
                    for j in range(2):
                        ts(tm2[:, :], pb_[:, 0:128], scs[:, j, b:b + 1], None, ALU.mult)
                        yield
                        stt(stn[:, j, :], st[:, j, :], scs[:, 2 + j, b:b + 1], tm2[:, :], ALU.mult, ALU.add)
                        yield
                        stt(junk[:, :], stn[:, j, :], 1.0, pb_[:, 128:256], ALU.mult, ALU.mult, accum_out=yfm[:, j, b:b + 1])
                        yield
                    P.dma("sp", [(sview("nssm_s"), stn[:, :, :])])
            run_lanes([samp_gen(0), samp_gen(1)])
            py = bank(4)
            for j in range(2):
                tr(py[:T, j * 128:(j + 1) * 128], yfm[:, j, :], ident)
            cp(L.yi[:T, :], py[:T, 0:256], eng="act_copy")
            for _ in gate_finish(L, pz, T, col0):
                pass
            P.release()

        def hgrn_pair(l, hs):
            P.phase(f"hgrn{l}.{hs[0]}")
            P.mark()
            W = wv(l)

            class Lane:
                pass

            def mklane(h, bk):
                L = Lane()
                L.h, L.bk = h, bk
                L.slab = A("slab", BF16, [KC, 512])
                P.dma("pool", [(L.slab[:, :, 0:128], W[:, :, O_HQ + h * 128:O_HQ + (h + 1) * 128]),
                               (L.slab[:, :, 128:256], W[:, :, O_HF + h * 128:O_HF + (h + 1) * 128]),
                               (L.slab[:, :, 256:384], W[:, :, O_HI + h * 128:O_HI + (h + 1) * 128]),
                               (L.slab[:, :, 384:512], W[:, :, O_HG + h * 128:O_HG + (h + 1) * 128])])
                lbp = L.lbp = A("lbp", F32, [4])
                L.hw = A("hw", F32, [128])
                P.dma("sp", [(lbp[:, 0:2], dr["hgrn_lb_logits"].rearrange("l c -> c l")[h * 128:(h + 1) * 128, :])], slow=True)
                P.dma("sp", [(L.hw, dr["h_norm_w"][l, h * 128:(h + 1) * 128].partition_broadcast(128))])
                if l == 0:
                    P.I("dve", "memset", ap=lbp[:, 2:3], constant=0.0)
                    P.I("dve", "memset", ap=lbp[:, 3:4], constant=1.0)
                else:
                    tt(lbp[:, 2:3], lbp[:, 0:1], lbp[:, 1:2], ALU.subtract)
                    act(lbp[:, 2:3], lbp[:, 2:3], AF.Exp)
                    ts(lbp[:, 2:3], lbp[:, 2:3], 1.0, None, ALU.add)
                    P.I("dve", "reciprocal", out=lbp[:, 2:3], in_=lbp[:, 2:3])
                    ts(lbp[:, 3:4], lbp[:, 2:3], -1.0, 1.0, ALU.mult, ALU.add)
                L.fv = A("fv", F32, [128]); L.kk = A("kk", F32, [128]); L.cum = A("cum", F32, [128]); L.e1 = A("e1", F32, [128])
                L.ksf = A("ksf", F32, [128]); L.qt = A("qt", BF16, [128]); L.qtm = A("qtm", BF16, [4, 128]); L.kt = A("kt", BF16, [128])
                L.khT = A("khT", BF16, [128]); L.khm = A("khm", BF16, [4, 128]); L.vtok = A("vtok", BF16, [128]); L.scm = A("scm", BF16, [128])
                L.gs = A("gs", F32, [128]); L.tmpf = A("tmpf", F32, [128]); L.ynb = A("ynb", BF16, [128]); L.sc = A("sc", F32, [4])
                L.S = A("S", F32, [128]); L.Sbf = A("Sbf", BF16, [128])
                L.qs = A("qs", F32, [16]); L.qmask = A("qmask", F32, [16, 16]); L.itk = A("itk", F32, [128]); L.itm = A("itm", BF16, [128])
                L.st = [A(f"st{i}", F32, [128]) for i in range(2)]; L.stn = [A(f"stn{i}", F32, [128]) for i in range(2)]
                L.tm2 = A("tm2", F32, [128])
                P.I("dve", "memset", ap=L.qtm[:, :, :], constant=0.0)
                return L

            def proj(L, col0, T):
                pf = bank(L.bk)
                for j in range(2):
                    for kc in range(KC):
                        mm(pf[:, j * 128:j * 128 + T], L.slab[:, kc, j * 128:(j + 1) * 128], xbf[:, kc, col0:col0 + T],
                           kc == 0, kc == KC - 1)
                pz = bank(L.bk + 1)
                for kc in range(KC):
                    mm(pz[:T, 0:256], xbf[:, kc, col0:col0 + T], L.slab[:, kc, 256:512], kc == 0, kc == KC - 1)
                return pf, pz

            def fk(L, pf, T):
                sigm(L.fv[:, :T], pf[:, 128:128 + T])
                ts(L.fv[:, :T], L.fv[:, :T], L.lbp[:, 3:4], L.lbp[:, 2:3], ALU.mult, ALU.add)
                ts(L.kk[:, :T], L.fv[:, :T], -1.0, 1.0, ALU.mult, ALU.add)

            def gate_finish(L, po, pz, T, col0):
                sigm(L.gs[:T, :], pz[:T, 128:256])
                tt(L.gs[:T, :], L.gs[:T, :], L.hw[:T, :], ALU.mult)
                yield
                yield from finish(po, T, 128, L.gs[:T, :], [L.h], col0, L.tmpf, L.ynb, L.sc, bank(L.bk + 3, BF16)[:, 512:1024])

            def head_gen(L):
                h = L.h
                fv, kk, cum, e1, ksf, qt, qtm, kt, khT, khm, vtok, scm, S, Sbf = (
                    L.fv, L.kk, L.cum, L.e1, L.ksf, L.qt, L.qtm, L.kt, L.khT, L.khm, L.vtok, L.scm, L.S, L.Sbf)
                for ci, (col0, T) in enumerate(pchunks):
                    first = ci == 0
                    subs = [(0, 16)] if T == 16 else [(32 * j, 32) for j in range(4)]
                    pf, pz = proj(L, col0, T)
                    yield
                    fk(L, pf, T)
                    cp(vtok[:T, :], pz[:T, 0:128], eng="act_copy")
                    yield
                    act(cum[:, :T], fv[:, :T], AF.Ln)
                    yield
                    for (s0, tsz) in subs:
                        P.I("dve", "tensor_tensor_scan", out=cum[:, s0:s0 + tsz], data0=ones[:, :tsz], data1=cum[:, s0:s0 + tsz],
                            initial=0.0, op0=ALU.mult, op1=ALU.add)
                    yield
                    act(e1[:, :T], cum[:, :T], AF.Exp)
                    act(cum[:, :T], cum[:, :T], AF.Exp, scale=-1.0)
                    yield
                    stt(qt[:, :T], pf[:, 0:T], float(128 ** -0.5), e1[:, :T], ALU.mult, ALU.mult)
                    tt(ksf[:, :T], kk[:, :T], cum[:, :T], ALU.mult)
                    yield
                    if T == 128:
                        qd = bass.AP(qtm.tensor, qtm.offset, [list(qtm.ap[0]), [160, 4], [1, 32]])
                        cp(qd, qt[:, 0:128].rearrange("p (a b) -> p a b", a=4))
                    else:
                        cp(qtm[:, 0, 0:T], qt[:, 0:T])
                    cp(kt[:, :T], ksf[:, :T])
                    for (s0, tsz) in subs:
                        ts(khT[:, s0:s0 + tsz], ksf[:, s0:s0 + tsz], e1[:, s0 + tsz - 1:s0 + tsz], None, ALU.mult)
                    yield
                    psc = bank(L.bk)
                    mm(psc[:T, 0:T], kt[:, :T], qt[:, :T])
                    ptk = bank(L.bk, BF16)[:, 512:1024]
                    tr(ptk[:T, 0:128], khT[:, :T], identb)
                    yield
                    tt(scm[:T, :T], psc[:T, 0:T], bdm[:T, :T], ALU.mult)
                    for j, (s0, tsz) in enumerate(subs):
                        if T == 128:
                            act(khm[:T, j, :], ptk[:T, 0:128], AF.Identity, scale=rowm[:T, j:j + 1])
                        else:
                            cp(khm[:T, j, :], ptk[:T, 0:128], eng="act_copy")
                    yield
                    po = bank(L.bk + 2)
                    mm(po[:T, 0:128], scm[:T, :T], vtok[:T, :], True, first)
                    for j, (s0, tsz) in enumerate(subs):
                        if not first:
                            mm(po[:T, 0:128], qtm[:, j, :T], Sbf[:, :], False, j == len(subs) - 1)
                        pS = bank(L.bk + 3)
                        mm(pS[:, 0:128], khm[:T, j, :], vtok[:T, :])
                        yield
                        if first:
                            cp(S[:, :], pS[:, 0:128])
                        else:
                            stt(S[:, :], S[:, :], e1[:, s0 + tsz - 1:s0 + tsz], pS[:, 0:128], ALU.mult, ALU.add)
                        yield
                        cp(Sbf[:, :], S[:, :])
                        yield
                    yield from gate_finish(L, po[:T, 0:128], pz, T, col0)
                P.dma("sp", [(dr["nhgrn_p"][l, h], S[:, :])])
                T = NS
                col0 = NTP
                pf, pz = proj(L, col0, T)
                yield
                fk(L, pf, T)
                act(L.qs[:, :], pf[:, 0:T], AF.Copy, scale=float(128 ** -0.5))
                cp(L.itk[:T, :], pz[:T, 0:128], eng="act_copy")
                yield
                tt(L.qmask[:, :, :], L.qs[:, :].unsqueeze(1).to_broadcast([128, 16, 16]), eye16[:, :, :], ALU.mult)
                po = bank(L.bk + 2)
                for b in range(NS):
                    st, stn = L.st[b % 2], L.stn[b % 2]
                    P.dma("sp", [(st[:, :], dr["state_hgrn"][l, b, h])])
                    ts(L.itm[:T, :], L.itk[:T, :], ident[:T, b:b + 1], None, ALU.mult)
                    yield
                    pb_ = bank(L.bk + (0 if b % 2 else 3))
                    mm(pb_[:, 0:128], onesb[:T, :], L.itm[:T, :])
                    yield
                    ts(L.tm2[:, :], pb_[:, 0:128], kk[:, b:b + 1], None, ALU.mult)
                    stt(stn[:, :], st[:, :], fv[:, b:b + 1], L.tm2[:, :], ALU.mult, ALU.add)
                    yield
                    mm(po[:T, 0:128], L.qmask[:, b, :], stn[:, :], b == 0, b == NS - 1)
                    P.dma("sp", [(dr["nhgrn_s"][l, b, h], stn[:, :])])
                    yield
                yield from gate_finish(L, po[:T, 0:128], pz, T, col0)

            lanes = [mklane(h, 4 * i) for i, h in enumerate(hs)]

            def hskew(i, L):
                for _ in range(0 * i):
                    yield
                yield from head_gen(L)
            run_lanes([hskew(i, L) for i, L in enumerate(lanes)])
            P.release()

        def ret_unit(l, h):
            P.phase(f"ret{l}.{h}")
            P.mark()
            gam = GAMMA[h]
            slab = A("slab", BF16, [KC, 1536])
            W = wv(l)
            P.dma("pool", [(slab[:, :, 0:256], W[:, :, O_RQ + h * 256:O_RQ + (h + 1) * 256]),
                           (slab[:, :, 256:512], W[:, :, O_RK + h * 256:O_RK + (h + 1) * 256])])
            P.dma("pool", [(slab[:, :, 512:1024], W[:, :, O_RV + h * 512:O_RV + (h + 1) * 512])])
            P.dma("pool", [(slab[:, :, 1024:1536], W[:, :, O_RG + h * 512:O_RG + (h + 1) * 512])])
            dmk = A("dmk", F32, [128]); g1 = A("g1", F32, [128]); kd = A("kd", F32, [2])
            P.dma("sp", [(dmk, dr["c_dmask"][h]), (g1, dr["c_g1"][h]), (kd, dr["c_kd"][h])])
            S = A("S", F32, [2, 512]); Sbf = A("Sbf", BF16, [2, 512])

            class Lane:
                pass

            def mklane(bk):
                L = Lane()
                L.bk = bk
                L.cs = A("cs", F32, [2, 128]); L.t12 = A("t12", F32, [4, 128])
                L.qkr = A("qkr", BF16, [4, 128]); L.qg = A("qg", BF16, [2, 128]); L.vtok = A("vtok", BF16, [512])
                L.scm = A("scm", BF16, [128]); L.kdec = A("kdec", BF16, [256])
                L.gs = A("gs", F32, [512]); L.ynb = A("ynb", BF16, [512]); L.sc = A("sc", F32, [4])
                return L

            def proj(L, col0, T):
                pf = bank(L.bk)
                for j in range(4):
                    for kc in range(KC):
                        mm(pf[:, j * 128:j * 128 + T], slab[:, kc, j * 128:(j + 1) * 128], xbf[:, kc, col0:col0 + T],
                           kc == 0, kc == KC - 1)
                pv = bank(L.bk + 1)
                for kc in range(KC):
                    mm(pv[:T, :], xbf[:, kc, col0:col0 + T], slab[:, kc, 512:1024], kc == 0, kc == KC - 1)
                pg = bank(L.bk + 2)
                for kc in range(KC):
                    mm(pg[:T, :], xbf[:, kc, col0:col0 + T], slab[:, kc, 1024:1536], kc == 0, kc == KC - 1)
                return pf, pv, pg

            def rotary(L, pf, col0, T, outb):
                cs, t1, t2 = L.cs, L.t12[:, 0:2, :], L.t12[:, 2:4, :]
                P.dma("sp", [(cs[:, 0, :T], dr["c_cos"][:, col0:col0 + T]), (cs[:, 1, :T], dr["c_sin"][:, col0:col0 + T])])
                p4 = pf[:, :].rearrange("p (a b) -> p a b", a=4)
                x1, x2 = p4[:, 0:4:2, :T], p4[:, 1:4:2, :T]
                cosb = cs[:, 0, :T].unsqueeze(1).to_broadcast([128, 2, T])
                sinb = cs[:, 1, :T].unsqueeze(1).to_broadcast([128, 2, T])
                tt(t1[:, :, :T], x1, cosb, ALU.mult)
                tt(t2[:, :, :T], x2, sinb, ALU.mult)
                yield
                tt(outb[:, 0:4:2, :T], t1[:, :, :T], t2[:, :, :T], ALU.subtract)
                tt(t1[:, :, :T], x2, cosb, ALU.mult)
                tt(t2[:, :, :T], x1, sinb, ALU.mult)
                yield
                tt(outb[:, 1:4:2, :T], t1[:, :, :T], t2[:, :, :T], ALU.add)

            def gate_finish(L, po, pg, T, col0, pt):
                sigm(L.gs[:T, :], pg[:T, :])
                yield
                tt(L.gs[:T, :], L.gs[:T, :], pg[:T, :], ALU.mult)
                yield
                yield from finish(po, T, 512, L.gs[:T, :], [(h % 2) * 4 + j for j in range(4)], col0, L.ynb, L.ynb, L.sc, pt)

            def chunk_gen(L, ci, col0, T):
                first = ci == 0
                qkr, vtok, scm, kdec, qg = L.qkr, L.vtok, L.scm, L.kdec, L.qg
                pf, pv, pg = proj(L, col0, T)
                yield
                cp(vtok[:T, :], pv[:T, :], eng="act_copy")
                yield from rotary(L, pf, col0, T, qkr)
                yield
                psc = bank(L.bk)
                for j in range(2):
                    mm(psc[:T, 0:T], qkr[:, 2 + j, :T], qkr[:, j, :T], j == 0, j == 1)
                ptk = bank(L.bk, BF16)[:, 512:1024]
                for j in range(2):
                    tr(ptk[:T, j * 128:(j + 1) * 128], qkr[:, 2 + j, :T], identb)
                if not first:
                    tt(qg[:, :, :T], qkr[:, 0:2, :T], g1[:, :T].unsqueeze(1).to_broadcast([128, 2, T]), ALU.mult)
                yield
                tt(scm[:T, :T], psc[:T, 0:T], dmk[:T, :T], ALU.mult)
                act(kdec[:T, :], ptk[:T, 0:256], AF.Identity, scale=kd[:T, (0 if T == 128 else 1):(1 if T == 128 else 2)])
                yield
                po = bank(L.bk + 3)
                mm(po[:T, :], scm[:T, :T], vtok[:T, :], True, first)
                if not first:
                    for j in range(2):
                        mm(po[:T, :], qg[:, j, :T], Sbf[:, j, :], False, j == 1)
                pS = [bank(L.bk + 1), bank(L.bk)]
                for j in range(2):
                    mm(pS[j][:, :], kdec[:T, j * 128:(j + 1) * 128], vtok[:T, :])
                yield
                for j in range(2):
                    if first:
                        cp(S[:, j, :], pS[j][:, :])
                    else:
                        stt(S[:, j, :], S[:, j, :], float(gam ** T), pS[j][:, :], ALU.mult, ALU.add)
                    cp(Sbf[:, j, :], S[:, j, :], eng="act_copy")
                yield
                yield from gate_finish(L, po[:T, :], pg, T, col0, bank(L.bk + 1, BF16))
                if ci == len(pchunks) - 1:
                    P.dma("sp", [(dr["nret_p"][l, h].rearrange("(j p) v -> p j v", p=128), S[:, :, :])])

            P.mark()
            lanes = [mklane(0), mklane(4)]

            def lane_gen(li):
                if li == 1:
                    for _ in range(6):
                        yield
                for ci in range(li, len(pchunks), 2):
                    col0, T = pchunks[ci]
                    yield from chunk_gen(lanes[li], ci, col0, T)
            run_lanes([lane_gen(0), lane_gen(1)])
            P.release()

            T = NS
            col0 = NTP
            L = mklane(0)
            pf, pv, pg = proj(L, col0, T)
            qkf = A("qkf", F32, [4, 16])
            for _ in rotary(L, pf, col0, T, qkf):
                pass
            ts(qkf[:, 2:4, :], qkf[:, 2:4, :], 1.0 / 16.0, None, ALU.mult)
            qmask = A("qmask", F32, [2, 16, 16]); vtk = L.gs; vtm = L.ynb
            sts = [S, A("st1", F32, [2, 512])]
            for j in range(2):
                tt(qmask[:, j, :, :], qkf[:, j, :].unsqueeze(1).to_broadcast([128, 16, 16]), eye16[:, :, :], ALU.mult)
            cp(vtk[:T, :], pv[:T, :], eng="act_copy")
            po = bank(3)
            vtms = [vtm, A("vtm1", BF16, [512])]

            def samp_gen(par):
                st, vtm_ = sts[par], vtms[par]
                for b in range(par, NS, 2):
                    sview = lambda nm: dr[nm][l, b, h].rearrange("(j p) v -> p j v", p=128)
                    P.dma("sp", [(st[:, :, :], sview("state_ret"))])
                    ts(vtm_[:T, :], vtk[:T, :], ident[:T, b:b + 1], None, ALU.mult)
                    yield
                    pb_ = bank(6 + par)
                    mm(pb_[:, :], onesb[:T, :], vtm_[:T, :])
                    for j in range(2):
                        act(st[:, j, :], st[:, j, :], AF.Copy, scale=float(gam))
                    yield
                    for j in range(2):
                        stt(st[:, j, :], pb_[:, :], qkf[:, 2 + j, b:b + 1], st[:, j, :], ALU.mult, ALU.add)
                        yield
                    for j in range(2):
                        mm(po[:T, :], qmask[:, j, b, :], st[:, j, :], b == 0 and j == 0, b == NS - 1 and j == 1)
                    P.dma("sp", [(sview("nret_s"), st[:, :, :])])
                    yield
            run_lanes([samp_gen(0), samp_gen(1)])
            for _ in gate_finish(L, po[:T, :], pg, T, col0, bank(1, BF16)):
                pass
            P.release()

        def branch_pass(l, wbr, gcol, first):
            P.phase(f"bp{l}")
            W = wv(l)
            wo_v = dr["w_out"][l].rearrange("(k p) c -> p k c", p=128)
            P.mark()
            gT = A("gT", BF16, [KC, NT])
            wgs = [A(f"bg{i}", BF16, [KC, 128]) for i in range(2)]
            wbs = [A(f"bb{i}", BF16, [KC, 128]) for i in range(2)]
            wos = [A(f"bo{i}", BF16, [KC, 128]) for i in range(2)]
            sgs = [A(f"bs{i}", F32, [512]) for i in range(2)]
            it = 0
            for dc in range(KC):
                wg, wb = wgs[dc % 2], wbs[dc % 2]
                P.dma("pool", [(wg, W[:, :, gcol + dc * 128:gcol + (dc + 1) * 128]),
                               (wb, wbr[:, :, dc * 128:(dc + 1) * 128])])
                if dc == KC - 2:
                    for do in range(2):
                        P.dma("pool", [(wos[do], wo_v[:, :, do * 128:(do + 1) * 128])])
                for (c0, n) in tblocks:
                    pg, pb_, sg = bank(it % 2), bank(2 + it % 2), sgs[it % 2]
                    it += 1
                    for kc in range(KC):
                        mm(pg[:, :n], wg[:, kc, :], xbf[:, kc, c0:c0 + n], kc == 0, kc == KC - 1)
                    for kc in range(KC):
                        mm(pb_[:, :n], wb[:, kc, :], yT[:, kc, c0:c0 + n], kc == 0, kc == KC - 1)
                    act(sg[:, :n], pg[:, :n], AF.Sigmoid)
                    tt(gT[:, dc, c0:c0 + n], sg[:, :n], pb_[:, :n], ALU.mult)
            for do in range(KC):
                wo = wos[do % 2]
                if do >= 2:
                    P.dma("pool", [(wo, wo_v[:, :, do * 128:(do + 1) * 128])])
                for (c0, n) in tblocks:
                    po = bank(4 + it % 4)
                    it += 1
                    for kc in range(KC):
                        mm(po[:, :n], wo[:, kc, :], gT[:, kc, c0:c0 + n], kc == 0, kc == KC - 1)
                    if first:
                        stt(xT[:, do, c0:c0 + n], xT[:, do, c0:c0 + n], ALPHA, po[:, :n], ALU.mult, ALU.add)
                    else:
                        tt(xT[:, do, c0:c0 + n], xT[:, do, c0:c0 + n], po[:, :n], ALU.add)
            P.release()

        for l in range(layers):
            for g in range(4):
                ssd_unit(l, g)
            branch_pass(l, dr["w_br_m"][l].rearrange("(k p) c -> p k c", p=128), O_GATE, True)
            for h in range(0, 8, 2):
                hgrn_pair(l, [h, h + 1])
            branch_pass(l, dr["w_br_h"][l].rearrange("(k p) c -> p k c", p=128), O_GATE + 1024, False)
            for hh in range(2):
                for h in range(2 * hh, 2 * hh + 2):
                    ret_unit(l, h)
                branch_pass(l, dr["w_br_r"][l, hh * 1024:(hh + 1) * 1024].rearrange("(k p) c -> p k c", p=128),
                            O_GATE + 2048, False)
            layer_norm_fm(2 + 4 * l + 0, 2 + 4 * l + 1)
            ffn(l)
            layer_norm_fm(2 + 4 * l + 2, 2 + 4 * l + 3, last=(l == layers - 1))

        P.phase("out")
        P.mark()
        ob = [A(f"ob{i}", F32, [D]) for i in range(2)]
        for i, (col0, T, src) in enumerate(ttiles):
            if i == 0:
                continue
            o = ob[i % 2]
            pb = 2 * (i % 2)
            for kc in range(KC):
                tr(bank(pb + kc // 4)[:T, (kc % 4) * 128:(kc % 4 + 1) * 128], xT[:, kc, col0:col0 + T], ident)
            cp(o[:T, 0:512], bank(pb)[:T, :], eng="act_copy")
            cp(o[:T, 512:1024], bank(pb + 1)[:T, :])
            dst = dr["y_sample"][:, :] if i == len(ttiles) - 1 else dr["y_prompt"][128 * (i - 1):128 * i, :]
            P.dma("sp", [(dst, o[:T])])
        P.release()
        P.final_wait("sp")

        import contextlib
        with contextlib.ExitStack() as es:
            for e in ("pe", "act", "dve", "pool"):
                P.sems[e] = es.enter_context(nc.semaphore(f"s_{e}"))
            for s in range(P.n_dma_sems):
                P.sems[("dma", s)] = es.enter_context(nc.semaphore(f"s_dma{s}"))
            P.sems["sp"] = es.enter_context(nc.semaphore("s_sp"))
            block = es.enter_context(nc.Block())
            P.lower(block)
    return nc, P


def consts(NCH):
    SEQ = 128 * NCH
    NT = NMETA + SEQ + NS
    c = {}
    r = np.arange(128)
    c["c_ident"] = np.eye(128, dtype=np.float32)
    c["c_ones"] = np.ones((128, 128), np.float32)
    c["c_triu"] = (r[:, None] <= r[None, :]).astype(np.float32)
    c["c_strl"] = (r[:, None] > r[None, :]).astype(np.float32)
    c["c_bd"] = ((r[:, None] <= r[None, :]) & (r[:, None] // 32 == r[None, :] // 32)).astype(np.float32)
    c["c_rowm"] = (r[:, None] // 32 == np.arange(4)[None, :]).astype(np.float32)
    c["c_eye16"] = np.tile(np.eye(16, dtype=np.float32).reshape(1, 256), (128, 1))
    dm = np.zeros((4, 128, 128), np.float32); g1 = np.zeros((4, 128, 128), np.float32); kd = np.zeros((4, 128, 2), np.float32)
    for h in range(4):
        lg = np.log(np.float32(GAMMA[h])).astype(np.float32)
        diff = (r[None, :] - r[:, None]).astype(np.float32)
        dm[h] = np.where(r[:, None] <= r[None, :], np.exp(diff * lg), 0.0) / 16.0
        g1[h] = np.exp((r[None, :] + 1.0) * lg) * np.ones((128, 1), np.float32)
        kd[h, :, 0] = np.exp((127.0 - r) * lg) / 16.0
        kd[h, :, 1] = np.exp((15.0 - r) * lg) / 16.0
    c["c_dmask"], c["c_g1"], c["c_kd"] = dm, g1, kd.astype(np.float32)
    pos = np.concatenate([np.arange(NMETA + SEQ, dtype=np.float32), np.full(NS, 16384.0, np.float32)])
    inv = (1.0 / (np.float32(10000.0) ** np.linspace(0.0, 1.0, 128, dtype=np.float32))).astype(np.float32)
    ang = (inv[:, None] * pos[None, :]).astype(np.float32)
    c["c_cos"] = np.cos(ang.astype(np.float64)).astype(np.float32)
    c["c_sin"] = np.sin(ang.astype(np.float64)).astype(np.float32)
    return c


_CACHE = {}


def kernel(**inputs):
    NCH = inputs["x_prompt"].shape[1] // 128
    NB = inputs["x_prompt"].shape[0]
    if NCH not in _CACHE:
        _CACHE[NCH] = build(NCH)[0]
    nc = _CACHE[NCH]
    cs = consts(NCH)
    f = lambda a: np.ascontiguousarray(np.asarray(a, dtype=np.float32))
    shared = {k: f(inputs[k]) for k in ["meta_tokens", "ln_in_g", "ln_in_b", "w_in", "conv_w", "conv_b", "dt_bias", "a_log",
                                        "d_skip", "m_norm_w", "hgrn_lb_logits", "h_norm_w", "w_br_m", "w_br_h", "w_br_r",
                                        "w_out", "w_ffn_in", "w_ffn_out", "ln1_g", "ln1_b", "ln2_g", "ln2_b"]}
    shared.update(cs)
    in_maps = []
    for c in range(NB):
        m = dict(shared)
        m["x_prompt"] = f(inputs["x_prompt"][c])
        m["x_sample"] = f(inputs["x_sample"][c * NS:(c + 1) * NS, 0])
        for nm in ["state_ssm", "state_conv", "state_hgrn", "state_ret"]:
            m[nm] = f(inputs[nm][:, c * NS:(c + 1) * NS])
        in_maps.append(m)
    res = run_bass_kernel_spmd(nc, in_maps, core_ids=list(range(NB))).results
    y_prompt = np.stack([r["y_prompt"] for r in res], 0)
    y_sample = np.concatenate([r["y_sample"] for r in res], 0)[:, None, :]
    outs = [y_prompt, y_sample]
    for nm in ["nssm_p", "nconv_p", "nhgrn_p", "nret_p"]:
        outs.append(np.stack([r[nm] for r in res], 1))
    for nm in ["nssm_s", "nconv_s", "nhgrn_s", "nret_s"]:
        outs.append(np.concatenate([r[nm] for r in res], 1))
    return tuple(np.ascontiguousarray(o, dtype=np.float32) for o in outs)
```
